# Optimizing a Trainium2 kernel written in Bass

```python
import jax, jax.numpy as jnp
from jax import lax
import numpy as np

D_MODEL = 1024
BATCH = 8
SEQ = 2048
DEPTH = 1

GRID_W = 64
MEM_LEN = 256
NA_HEADS = 8
NA_HEAD_DIM = 64
NA_WIDTH = NA_HEADS * NA_HEAD_DIM
NA_KH_MAX = 8
NA_KW = 16
NA_QB = 16
NA_KB = 32
HG_HEADS = 4
HG_KDIM = 128
HG_VDIM = 128
HG_QK = HG_HEADS * HG_KDIM
HG_V = HG_HEADS * HG_VDIM
HG_CHUNK = 64
MEM_HEADS = 4
MEM_HEAD_DIM = 128
MEM_WIDTH = MEM_HEADS * MEM_HEAD_DIM
N_BRANCH = 3
D_FF = 4 * D_MODEL
ALPHA = (2.0 * DEPTH) ** 0.25
BETA = (8.0 * DEPTH) ** -0.25
LN_EPS = 1e-5
RMS_EPS = 1e-6

_IN_SIZES = (NA_WIDTH, NA_WIDTH, NA_WIDTH, HG_QK, HG_V, HG_V, HG_QK, HG_QK, MEM_WIDTH, N_BRANCH * D_MODEL)
_IN_COL_SCALE = (1.0, 1.0, BETA, 1.0, BETA, 1.0, 1.0, 1.0, 1.0, 1.0)
IN_COLS = sum(_IN_SIZES)

kernel_name = "hybrid_na_hgrn2_memattn_encoder"


def _split_points():
    return np.cumsum(_IN_SIZES)[:-1].tolist()


def layer_norm(x, g, b):
    xf = x.astype(jnp.float32)
    mu = jnp.mean(xf, axis=-1, keepdims=True)
    var = jnp.mean(jnp.square(xf - mu), axis=-1, keepdims=True)
    return ((xf - mu) * lax.rsqrt(var + LN_EPS) * g + b).astype(x.dtype)


def neighbourhood_attention(q, k, v, rpb):
    B, S, H, dh = q.shape
    rows = S // GRID_W
    kh = min(NA_KH_MAX, rows)
    ncb = GRID_W // NA_QB
    to_grid = lambda t: t.reshape(B, rows, GRID_W, H, dh).transpose(0, 3, 1, 2, 4)
    qg, kg, vg = to_grid(q), to_grid(k), to_grid(v)

    row_start = jnp.clip(jnp.arange(rows) - kh // 2, 0, rows - kh)
    col_blk = jnp.clip(jnp.arange(ncb) * NA_QB - NA_KW // 2, 0, GRID_W - NA_KB)
    kcol = col_blk[:, None] + jnp.arange(NA_KB)
    qcol = jnp.arange(GRID_W).reshape(ncb, NA_QB)
    c0 = jnp.clip(qcol - NA_KW // 2, 0, GRID_W - NA_KW)
    kc = kcol[:, None, :]
    valid = (kc >= c0[..., None]) & (kc < c0[..., None] + NA_KW)
    dc_idx = jnp.clip(kc - qcol[..., None] + NA_KW - 1, 0, 2 * NA_KW - 2)
    rpb_c = rpb.astype(jnp.float32)[:, :, dc_idx]
    scale = dh ** -0.5

    def one_row(r):
        rs = row_start[r]
        k_rows = lax.dynamic_slice_in_dim(kg, rs, kh, axis=2)
        v_rows = lax.dynamic_slice_in_dim(vg, rs, kh, axis=2)
        k_blk = k_rows[:, :, :, kcol]
        v_blk = v_rows[:, :, :, kcol]
        q_row = lax.dynamic_index_in_dim(qg, r, axis=2, keepdims=False).reshape(B, H, ncb, NA_QB, dh)
        s = jnp.einsum('bhnqd,bhanjd->bhnqaj', q_row, k_blk).astype(jnp.float32) * scale
        dr_idx = rs + jnp.arange(kh) - r + NA_KH_MAX - 1
        bias = jnp.take(rpb_c, dr_idx, axis=1).transpose(0, 2, 3, 1, 4)
        s = jnp.where(valid[:, :, None, :], s + bias, -jnp.inf)
        p = jax.nn.softmax(s.reshape(B, H, ncb, NA_QB, kh * NA_KB), axis=-1).reshape(s.shape)
        o = jnp.einsum('bhnqaj,bhanjd->bhnqd', p.astype(v.dtype), v_blk)
        return o.reshape(B, H, GRID_W, dh)

    out = lax.map(one_row, jnp.arange(rows))
    return out.transpose(1, 0, 3, 2, 4).reshape(B, S, H * dh)


def hgrn2_scan(q, k, v, g):
    B, S, H, K = q.shape
    V = v.shape[-1]
    nc = S // HG_CHUNK
    f32 = jnp.float32

    def chunks(t):
        return t.astype(f32).reshape(B, nc, HG_CHUNK, H, t.shape[-1]).transpose(1, 0, 3, 2, 4)

    xs = (chunks(q), chunks(k), chunks(v), chunks(g))
    incl = jnp.tril(jnp.ones((HG_CHUNK, HG_CHUNK), bool))[:, :, None]

    def step(state, inp):
        qt, kt, vt, gt = inp
        b = jnp.cumsum(gt, axis=2)
        diff = b[:, :, :, None, :] - b[:, :, None, :, :]
        decay = jnp.exp(jnp.where(incl, diff, -jnp.inf))
        a = jnp.einsum('bhtk,bhtsk,bhsk->bhts', qt, decay, kt)
        o = jnp.einsum('bhts,bhsv->bhtv', a, vt) + jnp.einsum('bhtk,bhkv->bhtv', qt * jnp.exp(b), state)
        b_last = b[:, :, -1:, :]
        state = jnp.exp(b_last[:, :, 0, :])[..., None] * state + jnp.einsum('bhsk,bhsv->bhkv', kt * jnp.exp(b_last - b), vt)
        return state, o

    s0 = jnp.zeros((B, H, K, V), f32)
    _, o = lax.scan(step, s0, xs)
    return o.transpose(1, 0, 3, 2, 4).reshape(B, S, H, V)


def hgrn2_bidirectional(hq, hi, hog, hf_fwd, hf_bwd, lb_fwd, lb_bwd, norm_g):
    B, S, _ = hq.shape
    f32 = jnp.float32
    q = jax.nn.silu(hq.astype(f32)).reshape(B, S, HG_HEADS, HG_KDIM)
    v = hi.reshape(B, S, HG_HEADS, HG_VDIM)

    def gates(fpre, lb):
        fpre = fpre.astype(f32)
        lb = lb.astype(f32)
        f = lb + (1.0 - lb) * jax.nn.sigmoid(fpre)
        k = (1.0 - lb) * jax.nn.sigmoid(-fpre)
        return k.reshape(B, S, HG_HEADS, HG_KDIM), jnp.log(f).reshape(B, S, HG_HEADS, HG_KDIM)

    k_f, g_f = gates(hf_fwd, lb_fwd)
    k_b, g_b = gates(hf_bwd, lb_bwd)
    flip = lambda t: jnp.flip(t, axis=1)
    o = hgrn2_scan(q, k_f, v, g_f) + flip(hgrn2_scan(flip(q), flip(k_b), flip(v), flip(g_b)))
    o = o * lax.rsqrt(jnp.mean(jnp.square(o), axis=-1, keepdims=True) + RMS_EPS)
    o = o.reshape(B, S, HG_V) * norm_g * jax.nn.silu(hog.astype(f32))
    return o.astype(hi.dtype)


def memory_attention(q, mem, w_mem_kv):
    B, S, _ = q.shape
    M = mem.shape[1]
    k, v = jnp.split(mem @ w_mem_kv, 2, axis=-1)
    q = q.reshape(B, S, MEM_HEADS, MEM_HEAD_DIM)
    k = k.reshape(B, M, MEM_HEADS, MEM_HEAD_DIM)
    v = v.reshape(B, M, MEM_HEADS, MEM_HEAD_DIM)
    s = jnp.einsum('bshd,bmhd->bhsm', q, k).astype(jnp.float32) * (MEM_HEAD_DIM ** -0.5)
    p = jax.nn.softmax(s, axis=-1).astype(v.dtype)
    return jnp.einsum('bhsm,bmhd->bshd', p, v).reshape(B, S, MEM_WIDTH)


def setup_inputs(seed: int = 0) -> dict:
    key = jax.random.key(seed)
    ks = jax.random.split(key, 20)
    f32 = jnp.float32
    nrm = lambda k, shape, s: jax.random.normal(k, shape, f32) * s
    col_scale = jnp.asarray(np.concatenate([np.full(n, s, np.float32) for n, s in zip(_IN_SIZES, _IN_COL_SCALE)]))
    mem_kv_scale = jnp.concatenate([jnp.ones((MEM_WIDTH,), f32), jnp.full((MEM_WIDTH,), BETA, f32)])
    return {
        "x": nrm(ks[0], (BATCH, SEQ, D_MODEL), 1.0),
        "mem": nrm(ks[1], (BATCH, MEM_LEN, D_MODEL), 1.0),
        "ln_emb_g": 1.0 + nrm(ks[2], (D_MODEL,), 0.02),
        "ln_emb_b": nrm(ks[3], (D_MODEL,), 0.02),
        "w_in": nrm(ks[4], (DEPTH, D_MODEL, IN_COLS), D_MODEL ** -0.5) * col_scale,
        "na_rpb": nrm(ks[5], (DEPTH, NA_HEADS, 2 * NA_KH_MAX - 1, 2 * NA_KW - 1), 0.02),
        "hg_lb_logits": nrm(ks[6], (2, DEPTH + 1, HG_QK), 0.5),
        "hg_norm_g": 1.0 + nrm(ks[7], (DEPTH, HG_V), 0.02),
        "w_mem_kv": nrm(ks[8], (DEPTH, D_MODEL, 2 * MEM_WIDTH), D_MODEL ** -0.5) * mem_kv_scale,
        "w_branch_na": nrm(ks[9], (DEPTH, NA_WIDTH, D_MODEL), BETA * NA_WIDTH ** -0.5),
        "w_branch_hg": nrm(ks[10], (DEPTH, HG_V, D_MODEL), BETA * HG_V ** -0.5),
        "w_branch_mem": nrm(ks[11], (DEPTH, MEM_WIDTH, D_MODEL), BETA * MEM_WIDTH ** -0.5),
        "w_out": nrm(ks[12], (DEPTH, D_MODEL, D_MODEL), BETA * D_MODEL ** -0.5),
        "ln1_g": 1.0 + nrm(ks[13], (DEPTH, D_MODEL), 0.02),
        "ln1_b": nrm(ks[14], (DEPTH, D_MODEL), 0.02),
        "w_ff1": nrm(ks[15], (DEPTH, D_MODEL, D_FF), BETA * D_MODEL ** -0.5),
        "w_ff2": nrm(ks[16], (DEPTH, D_FF, D_MODEL), BETA * D_FF ** -0.5),
        "ln2_g": 1.0 + nrm(ks[17], (DEPTH, D_MODEL), 0.02),
        "ln2_b": nrm(ks[18], (DEPTH, D_MODEL), 0.02),
    }


def reference(x, mem, ln_emb_g, ln_emb_b, w_in, na_rpb, hg_lb_logits, hg_norm_g, w_mem_kv,
              w_branch_na, w_branch_hg, w_branch_mem, w_out, ln1_g, ln1_b, w_ff1, w_ff2, ln2_g, ln2_b):
    B, S, _ = x.shape
    lb_all = jnp.cumsum(jax.nn.softmax(hg_lb_logits.astype(jnp.float32), axis=1), axis=1)
    x = layer_norm(x, ln_emb_g, ln_emb_b)
    for l in range(DEPTH):
        proj = x @ w_in[l]
        na_q, na_k, na_v, hg_q, hg_i, hg_og, hg_ff, hg_fb, mem_q, gates = jnp.split(proj, _split_points(), axis=-1)
        shp = (B, S, NA_HEADS, NA_HEAD_DIM)
        y_na = neighbourhood_attention(na_q.reshape(shp), na_k.reshape(shp), na_v.reshape(shp), na_rpb[l])
        y_hg = hgrn2_bidirectional(hg_q, hg_i, hg_og, hg_ff, hg_fb, lb_all[0, l], lb_all[1, l], hg_norm_g[l])
        y_mem = memory_attention(mem_q, mem, w_mem_kv[l])
        g_na, g_hg, g_mem = jnp.split(jax.nn.sigmoid(gates.astype(jnp.float32)).astype(x.dtype), N_BRANCH, axis=-1)
        merged = g_na * (y_na @ w_branch_na[l]) + g_hg * (y_hg @ w_branch_hg[l]) + g_mem * (y_mem @ w_branch_mem[l])
        x = layer_norm(ALPHA * x + merged @ w_out[l], ln1_g[l], ln1_b[l])
        h = jnp.square(jax.nn.relu(x @ w_ff1[l]))
        x = layer_norm(ALPHA * x + h @ w_ff2[l], ln2_g[l], ln2_b[l])
    return x
```

```python
import numpy as np
from contextlib import ExitStack
import concourse.bass as bass
import concourse.mybir as mybir
from concourse.bass_utils import run_bass_kernel_spmd

F32 = mybir.dt.float32
BF16 = mybir.dt.bfloat16
AF = mybir.ActivationFunctionType
ALU = mybir.AluOpType

T = 2048
D = 1024
NT = 16
ALPHA = 2.0 ** 0.25
LN_EPS = 1e-5
RMS_EPS = 1e-6
NEG = -30000.0


class Buf:
    __slots__ = ("w", "r", "name")

    def __init__(self, name=""):
        self.w = {}
        self.r = {}
        self.name = name


class _Eng:
    def __init__(self, e, sem, name):
        self.e = e
        self.sem = sem
        self.cnt = 0
        self.waited = {}
        self.name = name


class _DSem:
    def __init__(self, sem):
        self.sem = sem
        self.cnt = 0


class Sched:
    def __init__(self, nc, stack):
        self.nc = nc
        self.stack = stack
        self.E = {}
        for name, e in (("pe", nc.tensor), ("act", nc.scalar), ("dve", nc.vector),
                        ("pool", nc.gpsimd), ("sp", nc.sync)):
            sem = stack.enter_context(nc.semaphore("s_" + name))
            self.E[name] = _Eng(e, sem, name)
        self.dsems = []
        self.sems = {}
        for E in self.E.values():
            self.sems[id(E.sem)] = E.sem

    def dsem(self, name):
        sem = self.stack.enter_context(self.nc.semaphore("d_" + name))
        d = _DSem(sem)
        self.dsems.append(d)
        self.sems[id(sem)] = sem
        return d

    def _need(self, E, deps):
        need = {}
        for sid, val in deps:
            if E.name == "pe" and sid == id(E.sem):
                continue
            if E.waited.get(sid, 0) >= val:
                continue
            if need.get(sid, 0) < val:
                need[sid] = val
        return list(need.items())

    @staticmethod
    def _deps(reads, writes):
        deps = []
        for b in reads:
            deps += list(b.w.items())
        for b in writes:
            deps += list(b.w.items())
            deps += list(b.r.items())
        return deps

    @staticmethod
    def _record(tok, reads, writes):
        sid, val = tok
        for b in reads:
            if b.r.get(sid, 0) < val:
                b.r[sid] = val
        for b in writes:
            if b.w.get(sid, 0) < val:
                b.w[sid] = val

    def op(self, eng, fn, reads=(), writes=()):
        E = self.E[eng]
        need = self._need(E, self._deps(reads, writes))
        for sid, val in need[:-1]:
            E.e.wait_ge(self.sems[sid], val)
        res = fn(E.e)
        insts = list(res) if isinstance(res, (list, tuple)) else [res]
        if need:
            sid, val = need[-1]
            insts[0]._wait_ge(self.sems[sid], val)
        for sid, val in need:
            E.waited[sid] = val
        E.cnt += 1
        insts[-1].then_inc(E.sem, 1)
        tok = (id(E.sem), E.cnt)
        self._record(tok, reads, writes)
        return tok

    def dma(self, q, out, in_, dsem, reads=(), writes=(), **kw):
        E = self.E[q]
        need = self._need(E, self._deps(reads, writes))
        for sid, val in need:
            E.e.wait_ge(self.sems[sid], val)
            E.waited[sid] = val
        ins = E.e.dma_start(out=out, in_=in_, **kw)
        dsem.cnt += 16
        ins.then_inc(dsem.sem, 16)
        tok = (id(dsem.sem), dsem.cnt)
        self._record(tok, reads, writes)
        return tok

    def barrier(self, engines=None):
        toks = [(id(E.sem), E.cnt) for E in self.E.values() if E.cnt > 0]
        toks += [(id(d.sem), d.cnt) for d in self.dsems if d.cnt > 0]
        for name, E in self.E.items():
            if engines is not None and name not in engines:
                continue
            saved = E.name
            E.name = "_"
            need = self._need(E, toks)
            E.name = saved
            for sid, val in need:
                E.e.wait_ge(self.sems[sid], val)
                E.waited[sid] = val


class _StopBuild(Exception):
    pass


class Arena:
    def __init__(self, big, cap_bytes):
        self.big = big
        self.cap = cap_bytes

    def at(self, off, shape, dt):
        n = 1
        for s in shape[1:]:
            n *= s
        esz = 4 if dt == F32 else 2
        assert off % 4 == 0
        assert off + n * esz <= self.cap, (off, shape, self.cap)
        ap = self.big[:, off // 2: off // 2 + n * esz // 2]
        if dt == F32:
            ap = ap.bitcast(F32)
        if len(shape) == 3:
            ap = ap.rearrange("p (a b) -> p a b", a=shape[1])
        elif len(shape) == 4:
            ap = ap.rearrange("p (a b c) -> p a b c", a=shape[1], b=shape[2])
        if shape[0] != 128:
            ap = ap[0:shape[0]]
        return ap


class Bump:
    def __init__(self, arena, lo, hi):
        self.a = arena
        self.lo = lo
        self.hi = hi
        self.top = lo

    def alloc(self, shape, dt):
        n = 1
        for s in shape[1:]:
            n *= s
        nbytes = n * (4 if dt == F32 else 2)
        nbytes = (nbytes + 63) // 64 * 64
        off = self.top
        assert off + nbytes <= self.hi, ("arena overflow", off, nbytes, self.hi)
        self.top += nbytes
        return self.a.at(off, shape, dt)


def _na_plan():
    rs = lambda r: int(np.clip(r - 4, 0, 24))
    plan = {}
    for j in range(16):
        lo = rs(2 * j) // 2
        hi = (rs(2 * j + 1) + 7) // 2
        plan[j] = list(range(lo, hi + 1))
    return plan


def _na_mask_tile(j, c):
    rs = lambda r: int(np.clip(r - 4, 0, 24))
    m = np.full((128, 128), NEG, np.float32)
    kc = np.arange(64)[:, None]
    qc = np.arange(64)[None, :]
    c0 = np.clip(qc - 8, 0, 48)
    colv = (kc >= c0) & (kc < c0 + 16)
    for krl in range(2):
        for qrl in range(2):
            kr = 2 * c + krl
            qr = 2 * j + qrl
            if rs(qr) <= kr <= rs(qr) + 7:
                blk = np.where(colv, 0.0, NEG).astype(np.float32)
                m[krl * 64:(krl + 1) * 64, qrl * 64:(qrl + 1) * 64] = blk
    return m


def _na_tables(rpb):
    plan = _na_plan()
    kc = np.arange(64)[:, None]
    qc = np.arange(64)[None, :]
    dc = np.clip(kc - qc + 15, 0, 30)
    bias = np.zeros((8, 128, 7 * 128), np.float32)
    for di in range(7):
        delta = di - 3
        for krl in range(2):
            for qrl in range(2):
                dr = int(np.clip(2 * delta + krl - qrl + 7, 0, 14))
                bias[:, krl * 64:(krl + 1) * 64, di * 128 + qrl * 64: di * 128 + (qrl + 1) * 64] = rpb[:, dr][:, dc]
    tiles = []
    for delta in range(-2, 3):
        t = _na_mask_tile(6, 6 + delta)
        tiles.append(t)
    for j in (0, 1, 14, 15):
        assert len(plan[j]) == 4
        for c in plan[j]:
            tiles.append(_na_mask_tile(j, c))
    mask = np.concatenate(tiles, axis=1)
    return bias, mask


def _na_offsets(j):
    plan = _na_plan()
    chunks = plan[j]
    delta0 = chunks[0] - j
    boff = (delta0 + 3) * 128
    if 2 <= j <= 13:
        assert len(chunks) == 5 and delta0 == -2
        moff = 0
    else:
        moff = 640 + {0: 0, 1: 512, 14: 1024, 15: 1536}[j]
    return chunks, boff, moff


def _hg_masks():
    s = np.arange(128)[:, None]
    t = np.arange(128)[None, :]
    same = (s // 64) == (t // 64)
    mf = (same & (s <= t)).astype(np.float32)
    mb = (same & (s >= t)).astype(np.float32)
    return mf, mb


CAP = 211968


def build_program(dbg=None):
    nc = bass.Bass("TRN2", target_bir_lowering=False)

    def din(name, shape):
        return nc.dram_tensor(name, list(shape), F32, kind="ExternalInput").ap()

    x_d = din("x", [T, D])
    mem_d = din("mem", [256, D])
    lnp_d = din("lnp", [6, D])
    w_na_d = din("w_na", [D, 1536])
    w_hg_d = din("w_hg", [4, D, 640])
    w_mq_d = din("w_mq", [D, 512])
    w_mkv_d = din("w_mkv", [D, 1024])
    w_mg_d = din("w_mg", [8, D, 384])
    w_mb_d = din("w_mb", [8, 512, 384])
    w_out_d = din("w_out", [D, D])
    w_ff1_d = din("w_ff1", [D, 4096])
    w_ff2_d = din("w_ff2", [4096, D])
    ident_d = din("ident", [128, 128])
    hgm_d = din("hgm", [2, 128, 128])
    scanm_d = din("scanm", [1, T])
    nab_d = din("nab", [8, 128, 896])
    nam_d = din("nam", [128, 2688])
    lbl_d = din("lbl", [128, 16])
    hgn_d = din("hgn", [1, 512])
    lnpT_d = din("lnpT", [128, 48])
    y_d = nc.dram_tensor("y", [T, D], F32, kind="ExternalOutput").ap()
    dbg_d = None
    if dbg is not None:
        dbg_d = nc.dram_tensor("dbg", [128, 8192], F32, kind="ExternalOutput").ap()

    with ExitStack() as stack:
      try:
        big = stack.enter_context(nc.sbuf_tensor("big", [128, CAP // 2], BF16))
        pp = [stack.enter_context(nc.psum_tensor(f"pp{i}", [128, 1024], F32)) for i in range(4)]
        S = Sched(nc, stack)
        A = Arena(big, CAP)
        KB = 1024

        def bank(k):
            return pp[k // 2][:, (k % 2) * 512:(k % 2) * 512 + 512]

        bankB = [Buf(f"bank{k}") for k in range(8)]
        rr = {"n": 0}

        def next_bank(lo=0, hi=8):
            k = lo + rr["n"] % (hi - lo)
            rr["n"] += 1
            return k

        cb = Bump(A, 0, 12 * KB)
        ident = cb.alloc([128, 128], F32)
        hgm = cb.alloc([128, 2, 128], F32)
        lbl = cb.alloc([128, 16], F32)
        lbv = cb.alloc([128, 8], F32)
        omv = cb.alloc([128, 8], F32)
        hgn = cb.alloc([128, 512], F32)
        epsln = cb.alloc([128, 1], F32)
        epsrms = cb.alloc([128, 1], F32)
        lnsm = [cb.alloc([128, 32], F32) for _ in range(6)]
        rec_s = [cb.alloc([128, 8], F32) for _ in range(4)]
        scanm = cb.alloc([128, T], BF16)
        lnpT = cb.alloc([128, 48], F32)
        identb = cb.alloc([128, 128], BF16)
        ones_bf = cb.alloc([128, 128], BF16)
        lnst = cb.alloc([128, NT, 2], F32)
        lnstB = Buf("lnst")
        constB = Buf("const")
        cD = S.dsem("const")
        gbuf = A.at(12 * KB, [128, D], F32)
        bbuf = A.at(16 * KB, [128, D], F32)
        gbB = Buf("gb")
        gbD = S.dsem("gb")
        RA = 20 * KB
        RB = 52 * KB
        RC = 100 * KB
        RD = 132 * KB
        TOP = CAP

        S.dma("sp", out=ident, in_=ident_d, dsem=cD, writes=[constB])
        S.dma("sp", out=hgm, in_=hgm_d.rearrange("a p n -> p a n"), dsem=cD, writes=[constB])
        S.dma("sp", out=lbl, in_=lbl_d, dsem=cD, writes=[constB])
        S.dma("sp", out=lnpT, in_=lnpT_d, dsem=cD, writes=[constB])
        S.dma("sp", out=hgn, in_=hgn_d.partition_broadcast(128) if False else hgn_d[0:1, :].to_broadcast([128, 512]),
              dsem=cD, writes=[constB])
        S.op("pool", lambda e: e.memset(epsln, LN_EPS), writes=[constB])
        S.op("act", lambda e: e.activation(out=identb, in_=ident, func=AF.Copy), reads=[constB], writes=[constB])
        S.op("pool", lambda e: e.memset(ones_bf, 1.0), writes=[constB])
        S.op("pool", lambda e: e.memset(epsrms, RMS_EPS), writes=[constB])
        lbl3 = lbl.rearrange("p (d s h) -> p d s h", d=2, s=2)
        lbv3 = lbv.rearrange("p (d h) -> p d h", d=2)
        omv3 = omv.rearrange("p (d h) -> p d h", d=2)
        S.op("dve", lambda e: e.tensor_tensor(out=lbv3, in0=lbl3[:, :, 0, :], in1=lbl3[:, :, 1, :], op=ALU.subtract),
             reads=[constB], writes=[constB])
        S.op("act", lambda e: e.activation(out=lbv, in_=lbv, func=AF.Sigmoid), reads=[constB], writes=[constB])
        S.op("dve", lambda e: e.tensor_scalar(out=omv, in0=lbv, scalar1=-1.0, scalar2=1.0, op0=ALU.mult, op1=ALU.add),
             reads=[constB], writes=[constB])

        def load_ln(k):
            S.dma("sp", out=gbuf, in_=lnp_d[2 * k:2 * k + 1, :].to_broadcast([128, D]), dsem=gbD, writes=[gbB])
            S.dma("sp", out=bbuf, in_=lnp_d[2 * k + 1:2 * k + 2, :].to_broadcast([128, D]), dsem=gbD, writes=[gbB])

        lnB = [Buf(f"ln{i}") for i in range(6)]
        lncnt = {"n": 0}

        def ln_gen(src, srcB, dst, dstB, gA, bA, gB_, rn=None, rnB=None, affine=True):
            slot = lncnt["n"] % 6
            lncnt["n"] += 1
            sm = lnsm[slot]
            B_ = lnB[slot]
            rstd_ap = sm[:, 15:16] if rn is None else rn[:, 0:1]
            nmr_ap = sm[:, 16:17] if rn is None else rn[:, 1:2]
            rw = [B_] if rnB is None else [B_, rnB]
            S.op("dve", lambda e: [e.bn_stats(out=sm[:, 0:6], in_=src[:, 0:512]),
                                   e.bn_stats(out=sm[:, 6:12], in_=src[:, 512:1024])], reads=[srcB], writes=[B_])
            yield
            S.op("dve", lambda e: e.bn_aggr(out=sm[:, 12:14], in_=sm[:, 0:12]), reads=[B_], writes=[B_])
            yield
            S.op("act", lambda e: e.activation(out=sm[:, 14:15], in_=sm[:, 13:14], func=AF.Sqrt, bias=epsln[:, 0:1],
                                               scale=1.0), reads=[B_, constB], writes=[B_])
            yield
            S.op("dve", lambda e: e.reciprocal(out=rstd_ap, in_=sm[:, 14:15]), reads=[B_], writes=rw)
            yield
            S.op("dve", lambda e: e.tensor_scalar(out=nmr_ap, in0=sm[:, 12:13], scalar1=rstd_ap,
                                                  scalar2=-1.0, op0=ALU.mult, op1=ALU.mult), reads=rw, writes=rw)
            yield
            S.op("act", lambda e: e.activation(out=src, in_=src, func=AF.Identity, bias=nmr_ap,
                                               scale=rstd_ap), reads=[srcB] + rw, writes=[srcB])
            yield
            if not affine:
                return
            S.op("pool", lambda e: e.tensor_tensor(out=src, in0=src, in1=gA, op=ALU.mult),
                 reads=[srcB, gB_], writes=[srcB])
            yield
            S.op("dve", lambda e: e.tensor_tensor(out=dst, in0=src, in1=bA, op=ALU.add),
                 reads=[srcB, gB_], writes=[dstB])
            yield

        def ln_apply_gen(src, srcB, dst, dstB, gA, bA, gB_, rn, rnB):
            S.op("act", lambda e: e.activation(out=src, in_=src, func=AF.Identity, bias=rn[:, 1:2],
                                               scale=rn[:, 0:1]), reads=[srcB, rnB], writes=[srcB])
            yield
            S.op("pool", lambda e: e.tensor_tensor(out=src, in0=src, in1=gA, op=ALU.mult),
                 reads=[srcB, gB_], writes=[srcB])
            yield
            S.op("dve", lambda e: e.tensor_tensor(out=dst, in0=src, in1=bA, op=ALU.add),
                 reads=[srcB, gB_], writes=[dstB])
            yield

        def ln_ip(*a_):
            for _ in ln_gen(*a_):
                pass

        def run_streams(gens, max_active, skew=1):
            active = []
            nxt = 0
            tick = 0
            while nxt < len(gens) or active:
                if nxt < len(gens) and len(active) < max_active and tick % skew == 0:
                    active.append(gens[nxt])
                    nxt += 1
                for g in list(active):
                    try:
                        next(g)
                    except StopIteration:
                        active.remove(g)
                tick += 1

        class BG:
            def __init__(self):
                self.active = []

            def add(self, g):
                self.active.append(g)

            def step(self, n=1):
                for _ in range(n):
                    for g in list(self.active):
                        try:
                            next(g)
                        except StopIteration:
                            self.active.remove(g)

            def drain(self):
                while self.active:
                    self.step(1)

        def transpose_tile_to(srcs, srcB, dst_fn, dstB, evac="act", banks=(4, 8)):
            bk = next_bank(*banks)
            n = len(srcs)
            S.op("pe", lambda e: [e.transpose(out=bank(bk)[:, q * 128:(q + 1) * 128], in_=srcs[q], identity=ident)
                                  for q in range(n)], reads=[srcB, constB], writes=[bankB[bk]])
            out_ap, in_ap = dst_fn(bank(bk)[:, 0:n * 128])
            if evac == "act":
                S.op("act", lambda e: e.activation(out=out_ap, in_=in_ap, func=AF.Copy),
                     reads=[bankB[bk]], writes=[dstB])
            else:
                S.op("dve", lambda e: e.tensor_copy(out=out_ap, in_=in_ap), reads=[bankB[bk]], writes=[dstB])

        def transpose_gen(srcs, srcB, dst_fn, dstB, bk=None):
            bk = next_bank(4, 8) if bk is None else bk
            n = len(srcs)
            S.op("pe", lambda e: [e.transpose(out=bank(bk)[:, q * 128:(q + 1) * 128], in_=srcs[q], identity=ident)
                                  for q in range(n)], reads=[srcB, constB], writes=[bankB[bk]])
            yield
            out_ap, in_ap = dst_fn(bank(bk)[:, 0:n * 128])
            S.op("act", lambda e: e.activation(out=out_ap, in_=in_ap, func=AF.Copy),
                 reads=[bankB[bk]], writes=[dstB])
            yield

        def load_w(dst, src, buf, dsem):
            S.dma("pool", out=dst, in_=src.rearrange("(c p) n -> p c n", p=128), dsem=dsem, writes=[buf])

        def dump(ap_list, stage_off=None):
            S.barrier()
            off = 0
            dD = S.dsem("dbg")
            stage = A.at(RD if stage_off is None else stage_off, [128, 8192], F32)
            sB = Buf("dbgstage")
            for ap in ap_list:
                n = ap.shape[1]
                S.op("dve", lambda e: e.tensor_copy(out=stage[:, off:off + n], in_=ap), writes=[sB])
                off += n
            S.dma("sp", out=dbg_d[:, 0:off], in_=stage[:, 0:off], dsem=dD, reads=[sB])
            S.barrier()
            raise _StopBuild()

        x0T = A.at(RA, [128, 8, T], BF16)
        x0TB = Buf("x0T")
        x0TBk = [Buf(f"x0Tk{i}") for i in range(4)]
        load_ln(0)
        p0 = Bump(A, RC, RC + 16 * KB)
        NS0 = 4
        xs = [p0.alloc([128, D], F32) for _ in range(NS0)]
        xsB = [Buf(f"xs{i}") for i in range(NS0)]
        xsD = [S.dsem(f"xs{i}") for i in range(NS0)]

        def p0_stream(i):
            sl = i % NS0
            S.dma("sp", out=xs[sl], in_=x_d[i * 128:(i + 1) * 128, :], dsem=xsD[sl], writes=[xsB[sl]])
            yield
            yield from ln_gen(xs[sl], xsB[sl], xs[sl], xsB[sl], gbuf, bbuf, gbB, rn=lnst[:, i, :], rnB=lnstB,
                              affine=False)
            for half in range(2):
                bk = 4 + sl
                S.op("pe", lambda e: [e.transpose(out=bank(bk)[:, q * 128:(q + 1) * 128],
                                                  in_=xs[sl][:, (half * 4 + q) * 128:(half * 4 + q + 1) * 128],
                                                  identity=ident) for q in range(4)],
                     reads=[xsB[sl], constB], writes=[bankB[bk]])
                yield
                for q in range(4):
                    kc = half * 4 + q
                    if q % 2 == 0:
                        S.op("act", lambda e: e.activation(out=x0T[:, kc, i * 128:(i + 1) * 128],
                                                           in_=bank(bk)[:, q * 128:(q + 1) * 128], func=AF.Identity,
                                                           scale=lnpT[:, kc:kc + 1], bias=lnpT[:, 8 + kc:9 + kc]),
                             reads=[bankB[bk], constB], writes=[x0TB, x0TBk[i // 4]])
                    else:
                        S.op("dve", lambda e: e.tensor_scalar(out=x0T[:, kc, i * 128:(i + 1) * 128],
                                                              in0=bank(bk)[:, q * 128:(q + 1) * 128],
                                                              scalar1=lnpT[:, kc:kc + 1], scalar2=lnpT[:, 8 + kc:9 + kc],
                                                              op0=ALU.mult, op1=ALU.add),
                             reads=[bankB[bk], constB], writes=[x0TB, x0TBk[i // 4]])
                    yield

        ph = Bump(A, RC + 16 * KB, TOP)
        ovl = Bump(A, RC, RC + 16 * KB)
        wbuf = [ph.alloc([128, 8, 640], BF16) for _ in range(2)]
        wB = [Buf("w0"), Buf("w1")]
        wD = [S.dsem("w0"), S.dsem("w1")]
        wi = {"n": 0}
        ph_mark = ph.top

        def proj_fm(W, wBuf, c0, dst_fn, dstB, func=AF.Copy, scale=1.0, kch=8, rhsT=None, rhsB=None, ntb=4,
                    banks=(0, 8), bgc=None, bgn=8):
            rT = x0T if rhsT is None else rhsT
            rB = x0TB if rhsB is None else rhsB
            for tb in range(ntb):
                bk = next_bank(*banks)
                S.op("pe", lambda e: [e.matmul(out=bank(bk), lhsT=W[:, kc, c0:c0 + 128],
                                               rhs=rT[:, kc, tb * 512:(tb + 1) * 512],
                                               start=(kc == 0), stop=(kc == kch - 1)) for kc in range(kch)],
                     reads=[wBuf, rB], writes=[bankB[bk]])
                o = dst_fn(tb)
                S.op("act", lambda e: e.activation(out=o, in_=bank(bk), func=func, scale=scale),
                     reads=[bankB[bk]], writes=[dstB[tb] if isinstance(dstB, list) else dstB])
                if bgc is not None:
                    bgc.step(bgn)

        y_naT = A.at(RB, [128, 4, T], BF16)
        y_hgT = A.at(RB + 16 * KB, [128, 4, T], BF16)
        y_memT = A.at(RB + 32 * KB, [128, 4, T], BF16)
        yTB = Buf("yT")

        qT = ph.alloc([128, 4, T], BF16)
        kT = ph.alloc([128, 4, T], BF16)
        vaug = ph.alloc([128, 16, 8, 66], BF16)
        qkvB = Buf("qkv")
        namask = ph.alloc([128, 2688], F32)
        nabias = [ph.alloc([128, 896], F32) for _ in range(2)]
        nabB = [Buf("nab0"), Buf("nab1")]
        nabD = [S.dsem("nab0"), S.dsem("nab1")]
        sc = [ovl.alloc([128, 640], F32) for _ in range(3)]
        scB = [Buf("sc0"), Buf("sc1"), Buf("sc2")]
        pT = [ph.alloc([128, 640], BF16) for _ in range(3)]
        pTB = [Buf("pT0"), Buf("pT1"), Buf("pT2")]
        yst = ovl.alloc([128, 16, 128], F32)
        ystB = Buf("yst")
        S.dma("sp", out=namask, in_=nam_d, dsem=cD, writes=[constB])
        S.op("pool", lambda e: e.memset(vaug[:, :, :, 64:66], 1.0), writes=[qkvB])

        nx = Bump(A, RB + 16 * KB, RB + 48 * KB)
        Bh = [nx.alloc([128, 2688], BF16) for _ in range(3)]
        BhB = [Buf(f"Bh{i}") for i in range(4)]
        ystraw = nx.alloc([128, 16, 2, 66], F32)
        ystrawB = Buf("ystraw")
        w3 = nx.alloc([128, 8, 512], BF16)
        w3B = Buf("w3")
        w3D = S.dsem("w3")
        Bh.append(w3.rearrange("p a b -> p (a b)")[:, 0:2688])
        Wna = [wbuf[0][:, :, 0:512], wbuf[1][:, :, 0:512], w3]
        WnaB = [wB[0], wB[1], w3B]
        load_w(Wna[0], w_na_d[:, 0:512], wB[0], wD[0])
        load_w(Wna[1], w_na_d[:, 512:1024], wB[1], wD[1])
        load_w(Wna[2], w_na_d[:, 1024:1536], w3B, w3D)
        wi["n"] = 3
        evn = {"n": 0, "b": 0}

        def pbank():
            evn["b"] += 1
            return evn["b"] % 4

        def na_proj_stream(tb):
            tsl = slice(tb * 512, (tb + 1) * 512)
            for blk in range(2):
                dstT = qT if blk == 0 else kT
                scl = 0.125 if blk == 0 else 1.0
                for cb_ in range(4):
                    bk = pbank()
                    S.op("pe", lambda e: [e.matmul(out=bank(bk), lhsT=Wna[blk][:, kc, cb_ * 128:(cb_ + 1) * 128],
                                                   rhs=x0T[:, kc, tsl], start=(kc == 0), stop=(kc == 7))
                                          for kc in range(8)], reads=[WnaB[blk], x0TBk[tb]], writes=[bankB[bk]])
                    yield
                    evn["n"] += 1
                    if evn["n"] % 2 == 0:
                        S.op("act", lambda e: e.activation(out=dstT[:, cb_, tsl], in_=bank(bk), func=AF.Copy,
                                                           scale=scl), reads=[bankB[bk]], writes=[qkvB])
                    else:
                        S.op("dve", lambda e: e.tensor_scalar(out=dstT[:, cb_, tsl], in0=bank(bk), scalar1=scl,
                                                              scalar2=None, op0=ALU.mult),
                             reads=[bankB[bk]], writes=[qkvB])
                    yield
            for q in range(4):
                i = tb * 4 + q
                bk = pbank()
                S.op("pe", lambda e: [e.matmul(out=bank(bk), lhsT=x0T[:, kc, i * 128:(i + 1) * 128],
                                               rhs=Wna[2][:, kc, :], start=(kc == 0), stop=(kc == 7))
                                      for kc in range(8)], reads=[WnaB[2], x0TBk[tb]], writes=[bankB[bk]])
                yield
                S.op("act", lambda e: e.activation(out=vaug[:, i, :, 0:64],
                                                   in_=bank(bk).rearrange("p (h d) -> p h d", h=8),
                                                   func=AF.Copy), reads=[bankB[bk]], writes=[qkvB])
                yield

        for tb in range(5):
            gens = []
            if tb < 4:
                gens += [p0_stream(i) for i in range(tb * 4, tb * 4 + 4)]
            if tb >= 1:
                gens.insert(min(1, len(gens)), na_proj_stream(tb - 1))
            run_streams(gens, 5, skew=2)
        if dbg == "x0T":
            dump([x0T[:, kc, 0:1024] for kc in range(8)])

        load_w(wbuf[1], w_hg_d[0], wB[1], wD[1])
        NSA = 3
        ucnt = {"n": 0}
        recn = lnsm[5][:, 0:32].rearrange("p (j h o) -> p j h o", j=16, h=2)

        def bh_build(h):
            hs = h % 2
            k = h % 4
            S.dma("sp", out=nabias[hs], in_=nab_d[h], dsem=nabD[hs], writes=[nabB[hs]])
            for (m0, m1, b0) in ((0, 640, 128), (640, 1152, 384), (1152, 1664, 256), (1664, 2176, 128),
                                 (2176, 2688, 0)):
                S.op("pool", lambda e: e.tensor_tensor(out=Bh[k][:, m0:m1], in0=namask[:, m0:m1],
                                                       in1=nabias[hs][:, b0:b0 + (m1 - m0)], op=ALU.add),
                     reads=[nabB[hs], constB] + ([w3B, qkvB] if k == 3 else []), writes=[BhB[k]])

        def na_unit(h, j):
            hb = h // 2
            po = (h % 2) * 64
            hs = h % 2
            k = h % 4
            u = ucnt["n"]
            ucnt["n"] += 1
            si = u % NSA
            ob = 6 + u % 2
            if j == 2 + (h % 2) and h + 2 < 8:
                bh_build(h + 2)
            chunks, boff, moff = _na_offsets(j)
            n = len(chunks)
            ps = pp[si]
            n0 = min(n, 4) * 128
            S.op("pe", lambda e: [e.matmul(out=ps[:, 0:n0], lhsT=identb, rhs=Bh[k][:, moff:moff + n0],
                                           start=True, stop=False)] +
                                 ([e.matmul(out=ps[:, 512:640], lhsT=identb, rhs=Bh[k][:, moff + 512:moff + 640],
                                            start=True, stop=False)] if n == 5 else []) +
                                 [e.matmul(out=ps[:, ci * 128:(ci + 1) * 128],
                                           lhsT=kT[po:po + 64, hb, c * 128:(c + 1) * 128],
                                           rhs=qT[po:po + 64, hb, j * 128:(j + 1) * 128],
                                           start=False, stop=True) for ci, c in enumerate(chunks)],
                 reads=[qkvB, BhB[k], constB], writes=[bankB[2 * si], bankB[2 * si + 1]])
            yield
            S.op("act", lambda e: e.activation(out=pT[si][:, 0:n * 128], in_=ps[:, 0:n * 128], func=AF.Exp),
                 reads=[bankB[2 * si], bankB[2 * si + 1]], writes=[pTB[si]])
            yield
            S.op("pe", lambda e: [e.matmul(out=bank(ob)[:, 0:66], lhsT=pT[si][:, ci * 128:(ci + 1) * 128],
                                           rhs=vaug[:, c, h, :], start=(ci == 0), stop=(ci == n - 1))
                                  for ci, c in enumerate(chunks)],
                 reads=[pTB[si], qkvB], writes=[bankB[ob]])
            yield
            S.op("dve", lambda e: e.tensor_copy(out=ystraw[:, j, hs, :], in_=bank(ob)[:, 0:66]),
                 reads=[bankB[ob]], writes=[ystrawB])
            yield

        bh_build(0)
        bh_build(1)
        for hb_ in range(4):
            run_streams([na_unit(h, j) for j in range(NT) for h in (2 * hb_, 2 * hb_ + 1)], NSA, skew=1)
            S.op("dve", lambda e: e.reciprocal(out=recn, in_=ystraw[:, :, :, 64:65]),
                 reads=[ystrawB], writes=[ystB])
            S.op("dve", lambda e: e.tensor_tensor(out=yst.rearrange("p j (h d) -> p j h d", h=2),
                                                  in0=ystraw[:, :, :, 0:64],
                                                  in1=recn.to_broadcast([128, 16, 2, 64]), op=ALU.mult),
                 reads=[ystrawB, ystB], writes=[ystB])
            for j4 in range(4):
                srcs = [yst[:, j4 * 4 + q, :] for q in range(4)]
                transpose_tile_to(srcs, ystB,
                                  lambda bap: (y_naT[:, hb_, j4 * 512:(j4 + 1) * 512], bap), yTB, banks=(6, 8))
        if dbg == "y_naT":
            dump([y_naT[:, kc, 0:T] for kc in range(4)])

        S.barrier()
        ph.top = ph_mark
        hx = Bump(A, RB + 32 * KB, RB + 48 * KB)
        ov2 = Bump(A, RC, RC + 16 * KB)
        q32 = ov2.alloc([128, T], F32)
        Fb = ov2.alloc([128, T], F32)
        Gext = ph.alloc([128, T + 16], F32)
        Gb = Gext[:, 1:T + 1]
        Tq = lnsm[4][:, 0:32].rearrange("p (a b) -> p a b", a=4)
        TqB = Buf("Tq")
        Ib = ph.alloc([128, T], F32)
        Ab = ph.alloc([128, T], F32)
        qt = [ph.alloc([128, T], BF16) for _ in range(2)]
        kt = [ph.alloc([128, T], BF16) for _ in range(2)]
        kh = [ph.alloc([128, T], F32) for _ in range(2)]
        expT = [ph.alloc([128, 32], F32) for _ in range(2)]
        vh = ph.alloc([128, 16, 128], BF16)
        oacc = ph.alloc([128, 16, 128], F32)
        sog = hx.alloc([128, 16, 128], F32)
        kA = [hx.alloc([128, 512], BF16) for _ in range(3)]
        At2 = [hx.alloc([128, 256], BF16) for _ in range(3)]
        S32p = [hx.alloc([128, 2, 128], F32) for _ in range(2)]
        S32B = [Buf("S32_0"), Buf("S32_1")]
        Sbf2 = [hx.alloc([128, 2, 128], BF16) for _ in range(2)]
        ssq = hx.alloc([128, 16], F32)
        rstd_h = hx.alloc([128, 16], F32)
        ysth = [ph.alloc([128, 128], F32) for _ in range(4)]
        junk = ph.alloc([128, 128], F32)
        hgQ = [Buf(f"hgQ{i}") for i in range(4)]
        q32Q = [Buf(f"q32Q{i}") for i in range(4)]
        dirQ = [[Buf(f"dirQ{d}{i}") for i in range(4)] for d in range(2)]
        hgB = Buf("hg")
        khtokB = [Buf(f"khtok{i}") for i in range(4)]
        AtB = [Buf(f"At{i}") for i in range(4)]
        SB_ = [Buf("S_f"), Buf("S_b")]
        SbB = [Buf("Sb0"), Buf("Sb1")]
        fbQ = [[Buf(f"fb{k}_{q}") for q in range(4)] for k in range(4)]
        oaccB = Buf("oacc")
        ysthB = [Buf(f"ysth{i}") for i in range(4)]
        voB = Buf("vo")
        S.op("pool", lambda e: e.memset(Gext[:, 0:1], 0.0), writes=[constB])
        S.op("pool", lambda e: e.memset(scanm, 1.0), writes=[constB])
        S.op("pool", lambda e: e.memset(scanm.rearrange("p (c s) -> p c s", c=32)[:, :, 0:1], 0.0), writes=[constB])
        rot = {"k": 0, "f": 0}
        HB = (4, 8)

        def Q(ap, qr):
            return ap[:, qr * 512:(qr + 1) * 512]

        def Q3(ap, qr):
            return ap[:, qr * 512:(qr + 1) * 512].rearrange("p (c s) -> p c s", c=8)

        def hg_load(h_):
            ws_ = (3 + h_) % 2
            load_w(wbuf[ws_], w_hg_d[h_], wB[ws_], wD[ws_])

        def epi_gen(h):
            S.op("act", lambda e: e.activation(out=sog.rearrange("p a b -> p (a b)"),
                                               in_=sog.rearrange("p a b -> p (a b)"), func=AF.Silu),
                 reads=[voB], writes=[voB])
            yield
            for i in range(NT):
                S.op("act", lambda e: e.activation(out=junk, in_=oacc[:, i, :], func=AF.Square,
                                                   accum_out=ssq[:, i:i + 1]), reads=[oaccB], writes=[hgB])
                yield
            S.op("act", lambda e: e.activation(out=rstd_h, in_=ssq, func=AF.Sqrt, bias=epsrms[:, 0:1],
                                               scale=1.0 / 128.0), reads=[hgB, constB], writes=[hgB])
            yield
            S.op("dve", lambda e: e.reciprocal(out=rstd_h, in_=rstd_h), reads=[hgB], writes=[hgB])
            yield
            for i4 in range(4):
                for q in range(4):
                    i = i4 * 4 + q
                    S.op("dve", lambda e: e.scalar_tensor_tensor(out=ysth[q], in0=oacc[:, i, :],
                                                                 scalar=rstd_h[:, i:i + 1],
                                                                 in1=hgn[:, h * 128:(h + 1) * 128],
                                                                 op0=ALU.mult, op1=ALU.mult),
                         reads=[oaccB, hgB, constB], writes=[ysthB[q]])
                    yield
                    S.op("pool", lambda e: e.tensor_tensor(out=ysth[q], in0=ysth[q], in1=sog[:, i, :], op=ALU.mult),
                         reads=[ysthB[q], voB], writes=[ysthB[q]])
                    yield
                bk = next_bank(*HB)
                S.op("pe", lambda e: [e.transpose(out=bank(bk)[:, q * 128:(q + 1) * 128], in_=ysth[q], identity=ident)
                                      for q in range(4)], reads=ysthB + [constB], writes=[bankB[bk]])
                S.op("act", lambda e: e.activation(out=y_hgT[:, h, i4 * 512:(i4 + 1) * 512], in_=bank(bk), func=AF.Copy),
                     reads=[bankB[bk]], writes=[yTB])
                yield

        pend_epi = None
        for h in range(4):
            ws = (3 + h) % 2
            wi["n"] += 1
            W = wbuf[ws]
            bge = BG()
            if pend_epi is not None:
                bge.add(pend_epi)
            proj_fm(W, wB[ws], 0, lambda tb: q32[:, tb * 512:(tb + 1) * 512], q32Q, func=AF.Silu, banks=HB,
                    bgc=bge, bgn=8)
            def vog_emit(bgc):
                for i in range(NT):
                    bk = next_bank(*HB)
                    S.op("pe", lambda e: [e.matmul(out=bank(bk)[:, 0:256], lhsT=x0T[:, kc, i * 128:(i + 1) * 128],
                                                   rhs=W[:, kc, 384:640], start=(kc == 0), stop=(kc == 7))
                                          for kc in range(8)], reads=[wB[ws], x0TB], writes=[bankB[bk]])
                    S.op("dve", lambda e: e.tensor_copy(out=vh[:, i, :], in_=bank(bk)[:, 0:128]),
                         reads=[bankB[bk]], writes=[voB])
                    S.op("act", lambda e: e.activation(out=sog[:, i, :], in_=bank(bk)[:, 128:256], func=AF.Copy),
                         reads=[bankB[bk]], writes=[voB])
                    bgc.step(4)

            for d in range(2):
                col = d * 4 + h
                lb_ap = lbv[:, col:col + 1]
                om_ap = omv[:, col:col + 1]
                proj_fm(W, wB[ws], 128 + d * 128, lambda tb: Fb[:, tb * 512:(tb + 1) * 512], hgQ,
                        func=AF.Sigmoid, banks=HB, bgc=(bge if d == 0 else None), bgn=8)
                if d == 0:
                    bge.drain()
                if d == 1 and h + 1 < 4:
                    hg_load(h + 1)
                if d == 1 and h == 3:
                    load_w(wbuf[1][:, :, 0:512], w_mkv_d[:, 0:512], wB[1], wD[1])
                    load_w(wbuf[0][:, :, 0:512], w_mkv_d[:, 512:1024], wB[0], wD[0])
                chain = []
                chain.append(lambda qr: S.op("act", lambda e: e.activation(
                    out=Q(Fb, qr), in_=Q(Fb, qr), func=AF.Identity, bias=lb_ap, scale=om_ap),
                    reads=[hgQ[qr], constB], writes=[hgQ[qr]]))
                chain.append(lambda qr: S.op("act", lambda e: e.activation(out=Q(Gb, qr), in_=Q(Fb, qr), func=AF.Ln),
                                             reads=[hgQ[qr]], writes=[hgQ[qr]]))
                chain.append(lambda qr: S.op("act", lambda e: e.activation(
                    out=Q(Fb, qr), in_=Q(Fb, qr), func=AF.Identity, bias=1.0, scale=-1.0),
                    reads=[hgQ[qr]], writes=[hgQ[qr]]))
                if d == 0:
                    chain.append(lambda qr: S.op("dve", lambda e: e.tensor_tensor_scan(
                        out=Q(Ib, qr), data0=scanm[:, 0:512], data1=Q(Gb, qr), initial=0.0, op0=ALU.mult, op1=ALU.add),
                        reads=[hgQ[qr], constB], writes=[hgQ[qr]]))
                    chain.append(lambda qr: S.op("dve", lambda e: e.tensor_tensor(
                        out=Q3(Ab, qr), in0=Q3(Ib, qr)[:, :, 63:64].to_broadcast([128, 8, 64]), in1=Q3(Ib, qr),
                        op=ALU.subtract), reads=[hgQ[qr]], writes=[hgQ[qr]]))
                    chain.append(lambda qr: S.op("act", lambda e: e.activation(
                        out=expT[d][:, qr * 8:(qr + 1) * 8], in_=Q3(Ib, qr)[:, :, 63:64].rearrange("p c o -> p (c o)"),
                        func=AF.Exp), reads=[hgQ[qr]], writes=[dirQ[d][qr]]))
                    X_, U_ = Ab, Ib
                else:
                    chain.append(lambda qr: S.op("dve", lambda e: e.tensor_tensor_scan(
                        out=Q(Ib, qr), data0=Gext[:, qr * 512:qr * 512 + 512], data1=scanm[:, 0:512], initial=0.0,
                        op0=ALU.add, op1=ALU.mult), reads=[hgQ[qr], constB], writes=[hgQ[qr]]))
                    chain.append(lambda qr: S.op("dve", lambda e: e.tensor_tensor(
                        out=Tq[:, qr, :], in0=Q3(Ib, qr)[:, :, 63:64].rearrange("p c o -> p (c o)"),
                        in1=Q3(Gb, qr)[:, :, 63:64].rearrange("p c o -> p (c o)"), op=ALU.add),
                        reads=[hgQ[qr]], writes=[TqB]))
                    chain.append(lambda qr: S.op("act", lambda e: e.activation(
                        out=expT[d][:, qr * 8:(qr + 1) * 8], in_=Tq[:, qr, :], func=AF.Exp),
                        reads=[TqB], writes=[dirQ[d][qr]]))
                    chain.append(lambda qr: S.op("dve", lambda e: e.tensor_tensor(
                        out=Q3(Ab, qr), in0=Tq[:, qr, :].rearrange("p (c o) -> p c o", o=1).to_broadcast([128, 8, 64]),
                        in1=Q3(Ib, qr), op=ALU.subtract), reads=[hgQ[qr], TqB], writes=[hgQ[qr]]))
                    X_, U_ = Ib, Ab
                chain.append(lambda qr: S.op("act", lambda e: e.activation(out=Q(X_, qr), in_=Q(X_, qr), func=AF.Exp),
                                             reads=[hgQ[qr]], writes=[hgQ[qr]]))
                chain.append(lambda qr: S.op("pool", lambda e: e.tensor_tensor(
                    out=Q(kh[d], qr), in0=Q(Fb, qr), in1=Q(X_, qr), op=ALU.mult),
                    reads=[hgQ[qr]], writes=[dirQ[d][qr]]))
                chain.append(lambda qr: S.op("act", lambda e: e.activation(out=Q(X_, qr), in_=Q(U_, qr), func=AF.Exp),
                                             reads=[hgQ[qr], dirQ[d][qr]], writes=[hgQ[qr]]))
                chain.append(lambda qr: S.op("dve", lambda e: e.tensor_tensor(
                    out=Q(qt[d], qr), in0=Q(q32, qr), in1=Q(X_, qr), op=ALU.mult),
                    reads=[hgQ[qr], q32Q[qr]], writes=[dirQ[d][qr]]))
                chain.append(lambda qr: S.op("act", lambda e: e.activation(out=Q(X_, qr), in_=Q(U_, qr), func=AF.Exp,
                                                                           scale=-1.0),
                                             reads=[hgQ[qr], dirQ[d][qr]], writes=[hgQ[qr]]))
                chain.append(lambda qr: S.op("pool", lambda e: e.tensor_tensor(
                    out=Q(kt[d], qr), in0=Q(Fb, qr), in1=Q(X_, qr), op=ALU.mult),
                    reads=[hgQ[qr]], writes=[dirQ[d][qr]]))
                def chain_gen():
                    for opf in chain:
                        for qr in range(4):
                            opf(qr)
                            yield

                if d == 0:
                    bgc = BG()
                    bgc.add(chain_gen())
                    vog_emit(bgc)
                    bgc.drain()
                else:
                    for _ in chain_gen():
                        pass
            spp = [0]
            S.op("pool", lambda e: e.memset(S32p[0], 0.0), writes=[S32B[0]])
            S.op("pool", lambda e: e.memset(Sbf2[0], 0.0), writes=[SbB[0]])

            def tile_of(d, step):
                return step if d == 0 else NT - 1 - step

            fr = {}

            def F1_emit(step):
                r = rot["k"] % 3
                rot["k"] += 1
                xy = step % 2
                z = [2 + (step % 2) * 2, 3 + (step % 2) * 2]
                tiles = [tile_of(d, step) for d in range(2)]
                dqs = [dirQ[d][tiles[d] // 4] for d in range(2)]
                tsl = [slice(i * 128, (i + 1) * 128) for i in tiles]
                S.op("pe", lambda e: [e.transpose(out=bank(xy)[:, d * 128:(d + 1) * 128], in_=kh[d][:, tsl[d]],
                                                  identity=ident) for d in range(2)] +
                                     [e.matmul(out=bank(xy)[:, 256 + d * 128:256 + (d + 1) * 128], lhsT=kt[d][:, tsl[d]],
                                               rhs=qt[d][:, tsl[d]], start=True, stop=True) for d in range(2)],
                     reads=dqs + [constB], writes=[bankB[xy]])
                S.op("act", lambda e: e.activation(out=kA[r], in_=bank(xy), func=AF.Copy),
                     reads=[bankB[xy]], writes=[khtokB[r]])
                S.op("pool", lambda e: e.tensor_tensor(out=At2[r], in0=kA[r][:, 256:512],
                                                       in1=hgm.rearrange("p a n -> p (a n)"), op=ALU.mult),
                     reads=[khtokB[r], constB], writes=[AtB[r]])
                fr[step] = (r, z)

            def F2_emit(step):
                r, z = fr[step]
                tiles = [tile_of(d, step) for d in range(2)]
                S.op("pe", lambda e: [e.matmul(out=bank(z[sub])[:, d * 128:(d + 1) * 128],
                                               lhsT=kA[r][sub * 64:(sub + 1) * 64, d * 128:(d + 1) * 128],
                                               rhs=vh[sub * 64:(sub + 1) * 64, tiles[d], :], start=True, stop=True)
                                      for sub in range(2) for d in range(2)],
                     reads=[khtokB[r], voB], writes=[bankB[z[0]], bankB[z[1]]])

            def B_emit(step):
                r, z = fr[step]
                obs = {}
                for d in range(2):
                    i = tile_of(d, step)
                    ob = 6 + d
                    obs[d] = ob
                    S.op("pe", lambda e: e.matmul(out=bank(ob)[:, 0:128], lhsT=At2[r][:, d * 128:(d + 1) * 128],
                                                  rhs=vh[:, i, :], start=True, stop=False),
                         reads=[AtB[r], voB], writes=[bankB[ob]])
                for si_ in range(2):
                    if si_ == 1 and step + 1 < NT:
                        F2_emit(step + 1)
                    cur = spp[0]
                    nxt = 1 - cur
                    for d in range(2):
                        i = tile_of(d, step)
                        ob = obs[d]
                        sub = si_ if d == 0 else 1 - si_
                        c = 2 * i + sub
                        ssl = slice(i * 128 + sub * 64, i * 128 + sub * 64 + 64)
                        dq = dirQ[d][i // 4]
                        S.op("pe", lambda e: e.matmul(out=bank(ob)[sub * 64:(sub + 1) * 64, 0:128],
                                                      lhsT=qt[d][:, ssl], rhs=Sbf2[cur][:, d, :],
                                                      start=False, stop=True),
                             reads=[dq, SbB[cur]], writes=[bankB[ob]])
                        kv = bank(z[sub])[:, d * 128:(d + 1) * 128]
                        S.op("dve", lambda e: e.scalar_tensor_tensor(out=S32p[nxt][:, d, :], in0=S32p[cur][:, d, :],
                                                                     scalar=expT[d][:, c:c + 1], in1=kv,
                                                                     op0=ALU.mult, op1=ALU.add),
                             reads=[bankB[z[sub]], dq, S32B[cur]], writes=[S32B[nxt]])
                    S.op("act", lambda e: e.activation(out=Sbf2[nxt], in_=S32p[nxt], func=AF.Copy),
                         reads=[S32B[nxt]], writes=[SbB[nxt]])
                    spp[0] = nxt
                for d in range(2):
                    i = tile_of(d, step)
                    ob = obs[d]
                    if step < NT // 2:
                        S.op("act", lambda e: e.activation(out=oacc[:, i, :], in_=bank(ob)[:, 0:128], func=AF.Copy),
                             reads=[bankB[ob]], writes=[oaccB])
                    else:
                        S.op("dve", lambda e: e.tensor_tensor(out=oacc[:, i, :], in0=bank(ob)[:, 0:128],
                                                              in1=oacc[:, i, :], op=ALU.add),
                             reads=[bankB[ob], oaccB], writes=[oaccB])

            F1_emit(0)
            F2_emit(0)
            for step in range(NT):
                if step + 1 < NT:
                    F1_emit(step + 1)
                B_emit(step)
            pend_epi = epi_gen(h)
        for _ in pend_epi:
            pass
        if dbg == "y_hgT":
            dump([y_hgT[:, kc, 0:T] for kc in range(4)])

        S.barrier()
        ph.top = ph_mark
        memst = [ph.alloc([128, D], F32) for _ in range(2)]
        memT = ph.alloc([128, 8, 256], BF16)
        kmT = ph.alloc([128, 4, 256], BF16)
        vma = ph.alloc([128, 2, 4, 130], BF16)
        qmT = ph.alloc([128, 4, T], BF16)
        pm = [ph.alloc([128, 8, 512], BF16) for _ in range(2)]
        ystm = [ph.alloc([128, 512], F32) for _ in range(2)]
        ysraw = [ph.alloc([128, 4, 130], F32) for _ in range(2)]
        ysrawB = [Buf("ysraw0"), Buf("ysraw1")]
        memB = Buf("mem")
        assert ph.top <= TOP - 9 * KB
        wmg0 = A.at(TOP - 9 * KB, [128, 8, 384], BF16)
        wmb0 = A.at(TOP - 3 * KB, [128, 4, 384], BF16)
        wmB = [Buf("wm0"), Buf("wm1")]
        wmD = [S.dsem("wm0"), S.dsem("wm1")]
        load_w(wmg0, w_mg_d[0], wmB[0], wmD[0])
        load_w(wmb0, w_mb_d[0], wmB[0], wmD[0])
        mD = [S.dsem("mem0"), S.dsem("mem1")]
        pmB = [Buf("pm0"), Buf("pm1")]
        ystmB = [Buf("ystm0"), Buf("ystm1")]
        ws = wi["n"] % 2
        wi["n"] += 1
        Wk = wbuf[ws][:, :, 0:512]
        assert ws == 1
        ws2 = wi["n"] % 2
        wi["n"] += 1
        Wv = wbuf[ws2][:, :, 0:512]
        assert ws2 == 0
        for mc in range(2):
            S.dma("sp", out=memst[mc], in_=mem_d[mc * 128:(mc + 1) * 128, :], dsem=mD[mc], writes=[memB])
        S.op("pool", lambda e: e.memset(vma[:, :, :, 128:130], 1.0), writes=[memB])
        for mc in range(2):
            for half in range(2):
                srcs = [memst[mc][:, (half * 4 + q) * 128:(half * 4 + q + 1) * 128] for q in range(4)]
                transpose_tile_to(srcs, memB,
                                  lambda bap: (memT[:, half * 4:half * 4 + 4, mc * 128:(mc + 1) * 128],
                                               bap.rearrange("p (a b) -> p a b", a=4)), memB)
        for hd in range(4):
            bk = next_bank()
            S.op("pe", lambda e: [e.matmul(out=bank(bk)[:, 0:256], lhsT=Wk[:, kc, hd * 128:(hd + 1) * 128],
                                           rhs=memT[:, kc, :], start=(kc == 0), stop=(kc == 7)) for kc in range(8)],
                 reads=[wB[ws], memB], writes=[bankB[bk]])
            S.op("act", lambda e: e.activation(out=kmT[:, hd, :], in_=bank(bk)[:, 0:256], func=AF.Copy),
                 reads=[bankB[bk]], writes=[memB])
        for mc in range(2):
            bk = next_bank()
            S.op("pe", lambda e: [e.matmul(out=bank(bk), lhsT=memT[:, kc, mc * 128:(mc + 1) * 128],
                                           rhs=Wv[:, kc, :], start=(kc == 0), stop=(kc == 7)) for kc in range(8)],
                 reads=[wB[ws2], memB], writes=[bankB[bk]])
            S.op("act", lambda e: e.activation(out=vma[:, mc, :, 0:128],
                                               in_=bank(bk).rearrange("p (h d) -> p h d", h=4), func=AF.Copy),
                 reads=[bankB[bk]], writes=[memB])
        ws = wi["n"] % 2
        wi["n"] += 1
        Wq = wbuf[ws][:, :, 0:512]
        load_w(Wq, w_mq_d, wB[ws], wD[ws])
        for hd in range(4):
            proj_fm(Wq, wB[ws], hd * 128, lambda tb: qmT[:, hd, tb * 512:(tb + 1) * 512], memB)
        mscale = 128.0 ** -0.5
        for tb in range(4):
            ps_ = tb % 2
            for hd in range(4):
                for mc in range(2):
                    bk = next_bank(0, 4)
                    S.op("pe", lambda e: e.matmul(out=bank(bk), lhsT=kmT[:, hd, mc * 128:(mc + 1) * 128],
                                                  rhs=qmT[:, hd, tb * 512:(tb + 1) * 512], start=True, stop=True),
                         reads=[memB], writes=[bankB[bk]])
                    S.op("act", lambda e: e.activation(out=pm[ps_][:, hd * 2 + mc, :], in_=bank(bk), func=AF.Exp,
                                                       scale=mscale), reads=[bankB[bk]], writes=[pmB[ps_]])
            def mem_unit(i):
                q = i % 4
                ys = i % 2
                obs_ = [next_bank(4, 8), next_bank(4, 8)]
                for pr in range(2):
                    S.op("pe", lambda e: [e.matmul(out=bank(obs_[pr])[:, hh * 130:(hh + 1) * 130],
                                                   lhsT=pm[ps_][:, (pr * 2 + hh) * 2 + mc, q * 128:(q + 1) * 128],
                                                   rhs=vma[:, mc, pr * 2 + hh, :], start=(mc == 0), stop=(mc == 1))
                                          for hh in range(2) for mc in range(2)],
                         reads=[pmB[ps_], memB], writes=[bankB[obs_[pr]]])
                    yield
                    S.op("act", lambda e: e.activation(out=ysraw[ys][:, pr * 2:pr * 2 + 2, :],
                                                       in_=bank(obs_[pr])[:, 0:260].rearrange("p (h d) -> p h d", h=2),
                                                       func=AF.Copy), reads=[bankB[obs_[pr]]], writes=[ysrawB[ys]])
                    yield
                rc = rec_s[ys]
                S.op("dve", lambda e: e.reciprocal(out=rc[:, 0:4].rearrange("p (h o) -> p h o", o=1),
                                                   in_=ysraw[ys][:, :, 128:129]),
                     reads=[ysrawB[ys]], writes=[ystmB[ys]])
                yield
                S.op("dve", lambda e: e.tensor_tensor(out=ystm[ys].rearrange("p (h d) -> p h d", h=4),
                                                      in0=ysraw[ys][:, :, 0:128],
                                                      in1=rc[:, 0:4].rearrange("p (h o) -> p h o", o=1)
                                                      .to_broadcast([128, 4, 128]), op=ALU.mult),
                     reads=[ysrawB[ys], ystmB[ys]], writes=[ystmB[ys]])
                yield
                srcs = [ystm[ys][:, k * 128:(k + 1) * 128] for k in range(4)]
                yield from transpose_gen(srcs, ystmB[ys],
                                         lambda bap: (y_memT[:, 0:4, i * 128:(i + 1) * 128],
                                                      bap.rearrange("p (a b) -> p a b", a=4)), yTB)

            run_streams([mem_unit(tb * 4 + q) for q in range(4)], 2, skew=2)
        if dbg == "y_memT":
            dump([y_memT[:, kc, 0:T] for kc in range(4)])

        S.barrier()
        mergedT = A.at(RC, [128, 8, T], BF16)
        mgB = Buf("mergedT")
        p3 = Bump(A, RD, TOP)
        wmg = [wmg0, p3.alloc([128, 8, 384], BF16)]
        wmb = [wmb0, p3.alloc([128, 4, 384], BF16)]
        sg = [p3.alloc([128, 512], F32) for _ in range(2)]
        sgB = [Buf("sg0"), Buf("sg1")]
        macc = [p3.alloc([128, 512], F32) for _ in range(2)]
        maccB = [Buf("macc0"), Buf("macc1")]
        tmpm = [p3.alloc([128, 512], F32) for _ in range(2)]
        tmpmB = [Buf("tmpm0"), Buf("tmpm1")]
        yTs = [y_naT, y_hgT, y_memT]
        cnt = {"sg": 0, "ma": 0}
        def mg_load(j):
            ws_ = j % 2
            load_w(wmg[ws_], w_mg_d[j], wmB[ws_], wmD[ws_])
            load_w(wmb[ws_], w_mb_d[j], wmB[ws_], wmD[ws_])

        for j in range(8):
            ws = j % 2
            if j + 1 < 8:
                mg_load(j + 1)
            for tb in range(4):
                ma = cnt["ma"] % 2
                cnt["ma"] += 1
                tsl = slice(tb * 512, (tb + 1) * 512)
                for b in range(3):
                    gb_ = next_bank()
                    S.op("pe", lambda e: [e.matmul(out=bank(gb_), lhsT=wmg[ws][:, kc, b * 128:(b + 1) * 128],
                                                   rhs=x0T[:, kc, tsl], start=(kc == 0), stop=(kc == 7))
                                          for kc in range(8)], reads=[wmB[ws], x0TB], writes=[bankB[gb_]])
                    pb_ = next_bank()
                    S.op("pe", lambda e: [e.matmul(out=bank(pb_), lhsT=wmb[ws][:, kc, b * 128:(b + 1) * 128],
                                                   rhs=yTs[b][:, kc, tsl], start=(kc == 0), stop=(kc == 3))
                                          for kc in range(4)], reads=[wmB[ws], yTB], writes=[bankB[pb_]])
                    s_ = cnt["sg"] % 2
                    cnt["sg"] += 1
                    S.op("act", lambda e: e.activation(out=sg[s_], in_=bank(gb_), func=AF.Sigmoid),
                         reads=[bankB[gb_]], writes=[sgB[s_]])
                    if b == 0:
                        S.op("dve", lambda e: e.tensor_tensor(out=macc[ma], in0=bank(pb_), in1=sg[s_], op=ALU.mult),
                             reads=[bankB[pb_], sgB[s_]], writes=[maccB[ma]])
                    else:
                        S.op("dve", lambda e: e.tensor_tensor(out=tmpm[ma], in0=bank(pb_), in1=sg[s_], op=ALU.mult),
                             reads=[bankB[pb_], sgB[s_]], writes=[tmpmB[ma]])
                        if b == 1:
                            S.op("pool", lambda e: e.tensor_tensor(out=macc[ma], in0=macc[ma], in1=tmpm[ma], op=ALU.add),
                                 reads=[maccB[ma], tmpmB[ma]], writes=[maccB[ma]])
                        else:
                            S.op("pool", lambda e: e.tensor_tensor(out=mergedT[:, j, tsl], in0=macc[ma], in1=tmpm[ma],
                                                                   op=ALU.add),
                                 reads=[maccB[ma], tmpmB[ma]], writes=[mgB])
        if dbg == "mergedT":
            dump([mergedT[:, kc, 0:1024] for kc in range(8)])

        S.barrier()
        x1 = A.at(RD, [128, NT, D], F32)
        x1B = [Buf(f"x1_{i}") for i in range(NT)]
        x1T = A.at(RA, [128, 8, T], BF16)
        x1TB = Buf("x1T")
        pb3 = Bump(A, RB, RC)
        wout = pb3.alloc([128, 8, D], BF16)
        woB = Buf("wout")
        woD = S.dsem("wout")
        for hf in range(2):
            S.dma("pool", out=wout[:, :, hf * 512:(hf + 1) * 512],
                  in_=w_out_d[:, hf * 512:(hf + 1) * 512].rearrange("(c p) n -> p c n", p=128), dsem=woD, writes=[woB])
        g1 = pb3.alloc([128, D], F32)
        b1 = pb3.alloc([128, D], F32)
        g1B = Buf("g1")
        g1D = S.dsem("g1")
        S.dma("sp", out=g1, in_=lnp_d[2:3, :].to_broadcast([128, D]), dsem=g1D, writes=[g1B])
        S.dma("sp", out=b1, in_=lnp_d[3:4, :].to_broadcast([128, D]), dsem=g1D, writes=[g1B])
        pf = Bump(A, RB, RD)
        hT = [pf.alloc([128, 4, T], BF16) for _ in range(2)]
        hTB = [Buf("hT0"), Buf("hT1")]
        w1 = [None, None]
        w2 = [None, None]
        for k_ in range(2):
            w1[k_] = pf.alloc([128, 8, 512], BF16)
            w2[k_] = pf.alloc([128, 4, D], BF16)
        w1B = [Buf("w1_0"), Buf("w1_1")]
        w2B = [Buf("w2_0"), Buf("w2_1")]
        w1D = [S.dsem("w1_0"), S.dsem("w1_1")]
        w2D = [S.dsem("w2_0"), S.dsem("w2_1")]
        pf_mark = pf.top
        assert pb3.top <= RB + 28 * KB

        def ffn_load(g):
            s_ = g % 2
            load_w(w1[s_], w_ff1_d[:, g * 512:(g + 1) * 512], w1B[s_], w1D[s_])
            load_w(w2[s_], w_ff2_d[g * 512:(g + 1) * 512, :], w2B[s_], w2D[s_])

        hl0 = pb3.alloc([128, 2, D], BF16)
        hlB = Buf("hl")
        S.op("act", lambda e: e.activation(out=hl0[0:1, 0, :], in_=bbuf[0:1, :], func=AF.Copy, scale=ALPHA),
             reads=[gbB], writes=[hlB])
        S.op("dve", lambda e: e.scalar_tensor_tensor(out=hl0[0:1, 1, :], in0=bbuf[0:1, :], scalar=ALPHA,
                                                     in1=hl0[0:1, 0, :], op0=ALU.mult, op1=ALU.subtract),
             reads=[gbB, hlB], writes=[hlB])
        S.op("act", lambda e: e.activation(out=gbuf, in_=gbuf, func=AF.Copy, scale=ALPHA), reads=[gbB], writes=[gbB])
        S.op("act", lambda e: e.activation(out=g1, in_=g1, func=AF.Copy, scale=ALPHA), reads=[g1B], writes=[g1B])
        x1D = [S.dsem(f"x1d{i}") for i in range(NT)]
        ffn_load(0)

        def s3a_stream(i):
            sa = i % 4
            X = x1[:, i, :]
            S.dma("sp", out=X, in_=x_d[i * 128:(i + 1) * 128, :], dsem=x1D[i], writes=[x1B[i]],
                  reads=([x1B[i - 3]] if i >= 3 else [woB]))
            yield
            S.op("act", lambda e: e.activation(out=X, in_=X, func=AF.Identity, bias=lnst[:, i, 1:2],
                                               scale=lnst[:, i, 0:1]), reads=[x1B[i], lnstB], writes=[x1B[i]])
            yield
            S.op("pool", lambda e: e.tensor_tensor(out=X, in0=X, in1=gbuf, op=ALU.mult),
                 reads=[x1B[i], gbB], writes=[x1B[i]])
            yield
            for hf in range(2):
                bk = sa
                hsl = slice(hf * 512, (hf + 1) * 512)
                S.op("pe", lambda e: [e.matmul(out=bank(bk), lhsT=mergedT[:, kc, i * 128:(i + 1) * 128],
                                               rhs=wout[:, kc, hsl], start=(kc == 0), stop=False)
                                      for kc in range(8)] +
                                     [e.matmul(out=bank(bk), lhsT=ones_bf[0:1, :], rhs=hl0[0:1, r_, hsl],
                                               start=False, stop=(r_ == 1)) for r_ in range(2)],
                     reads=[mgB, woB, hlB, constB], writes=[bankB[bk]])
                yield
                S.op("dve", lambda e: e.tensor_tensor(out=X[:, hsl], in0=bank(bk), in1=X[:, hsl], op=ALU.add),
                     reads=[bankB[bk], x1B[i]], writes=[x1B[i]])
                yield

        def s3b_stream(i):
            sb = i % 4
            X = x1[:, i, :]
            yield from ln_gen(X, x1B[i], X, x1B[i], None, None, None, affine=False)
            for half in range(2):
                bk = 4 + sb
                S.op("pe", lambda e: [e.transpose(out=bank(bk)[:, q * 128:(q + 1) * 128],
                                                  in_=X[:, (half * 4 + q) * 128:(half * 4 + q + 1) * 128],
                                                  identity=ident) for q in range(4)],
                     reads=[x1B[i], constB], writes=[bankB[bk]])
                yield
                for q in range(4):
                    kc = half * 4 + q
                    if q % 2 == 0:
                        S.op("act", lambda e: e.activation(out=x1T[:, kc, i * 128:(i + 1) * 128],
                                                           in_=bank(bk)[:, q * 128:(q + 1) * 128], func=AF.Identity,
                                                           scale=lnpT[:, 16 + kc:17 + kc], bias=lnpT[:, 24 + kc:25 + kc]),
                             reads=[bankB[bk], constB], writes=[x1TB])
                    else:
                        S.op("dve", lambda e: e.tensor_scalar(out=x1T[:, kc, i * 128:(i + 1) * 128],
                                                              in0=bank(bk)[:, q * 128:(q + 1) * 128],
                                                              scalar1=lnpT[:, 16 + kc:17 + kc],
                                                              scalar2=lnpT[:, 24 + kc:25 + kc],
                                                              op0=ALU.mult, op1=ALU.add),
                             reads=[bankB[bk], constB], writes=[x1TB])
                    yield
            S.op("pool", lambda e: e.tensor_tensor(out=X, in0=X, in1=g1, op=ALU.mult),
                 reads=[x1B[i], g1B], writes=[x1B[i]])
            yield

        ga = [s3a_stream(i) for i in range(NT)]
        gb3 = [s3b_stream(i) for i in range(NT)]
        a_act, b_act = [], []
        na_, nb_, adone = 0, 0, 0
        tick = 0
        while na_ < NT or nb_ < NT or a_act or b_act:
            if na_ < NT and len(a_act) < 4 and tick % 2 == 0:
                a_act.append((na_, ga[na_]))
                na_ += 1
            if nb_ < NT and len(b_act) < 4 and nb_ < adone and tick % 2 == 1:
                b_act.append((nb_, gb3[nb_]))
                nb_ += 1
            for item in list(a_act):
                try:
                    next(item[1])
                except StopIteration:
                    a_act.remove(item)
                    adone += 1
            for item in list(b_act):
                try:
                    next(item[1])
                except StopIteration:
                    b_act.remove(item)
            tick += 1
        if dbg == "x1":
            dump([x1[:, i, :] for i in range(8)], stage_off=RB)

        S.barrier()
        pf.top = pf_mark
        rl = [pf.alloc([128, 512], F32) for _ in range(2)]
        rlB = [Buf("rl0"), Buf("rl1")]
        g2 = pf.alloc([128, D], F32)
        b2 = pf.alloc([128, D], F32)
        g2B = Buf("g2")
        g2D = S.dsem("g2")
        hl1 = pf.alloc([128, 2, D], BF16)
        hl1B = Buf("hl1")
        brow = A.at(RB + 64 * KB, [128, D], F32)
        browD = S.dsem("brow")
        S.dma("sp", out=brow[0:1, :], in_=lnp_d[3:4, :], dsem=browD, writes=[rlB[0], rlB[1]])
        S.op("act", lambda e: e.activation(out=hl1[0:1, 0, :], in_=brow[0:1, :], func=AF.Copy, scale=ALPHA),
             reads=[rlB[0], rlB[1]], writes=[hl1B])
        S.op("dve", lambda e: e.scalar_tensor_tensor(out=hl1[0:1, 1, :], in0=brow[0:1, :], scalar=ALPHA,
                                                     in1=hl1[0:1, 0, :], op0=ALU.mult, op1=ALU.subtract),
             reads=[rlB[0], rlB[1], hl1B], writes=[hl1B, rlB[0], rlB[1]])
        S.dma("sp", out=g2, in_=lnp_d[4:5, :].to_broadcast([128, D]), dsem=g2D, writes=[g2B])
        S.dma("sp", out=b2, in_=lnp_d[5:6, :].to_broadcast([128, D]), dsem=g2D, writes=[g2B])
        yo = [A.at(RD + 64 * KB, [128, D], F32), A.at(RD + 68 * KB, [128, D], F32)]
        yoB = [Buf("yo0"), Buf("yo1")]
        yoD = [S.dsem("yo0"), S.dsem("yo1")]
        NG = 8
        rcnt = {"n": 0}

        def ffn1(g):
            s = g % 2
            for fb in range(4):
                for tb in range(4):
                    bk = next_bank()
                    S.op("pe", lambda e: [e.matmul(out=bank(bk), lhsT=w1[s][:, kc, fb * 128:(fb + 1) * 128],
                                                   rhs=x1T[:, kc, tb * 512:(tb + 1) * 512],
                                                   start=(kc == 0), stop=(kc == 7)) for kc in range(8)],
                         reads=[w1B[s], x1TB], writes=[bankB[bk]])
                    r_ = rcnt["n"] % 2
                    rcnt["n"] += 1
                    S.op("act", lambda e: e.activation(out=rl[r_], in_=bank(bk), func=AF.Relu),
                         reads=[bankB[bk]], writes=[rlB[r_]])
                    S.op("dve", lambda e: e.tensor_tensor(out=hT[s][:, fb, tb * 512:(tb + 1) * 512], in0=bank(bk),
                                                          in1=rl[r_], op=ALU.mult),
                         reads=[bankB[bk], rlB[r_]], writes=[hTB[s]])

        def ffn2(g):
            s = g % 2
            for i in range(NT):
                for hf in range(2):
                    bk = next_bank()
                    extra = 2 if g == 0 else 0
                    S.op("pe", lambda e: [e.matmul(out=bank(bk), lhsT=hT[s][:, fb, i * 128:(i + 1) * 128],
                                                   rhs=w2[s][:, fb, hf * 512:(hf + 1) * 512],
                                                   start=(fb == 0), stop=(fb == 3 and extra == 0)) for fb in range(4)] +
                                         [e.matmul(out=bank(bk), lhsT=ones_bf[0:1, :],
                                                   rhs=hl1[0:1, r_, hf * 512:(hf + 1) * 512],
                                                   start=False, stop=(r_ == 1)) for r_ in range(extra)],
                         reads=[hTB[s], w2B[s], hl1B, constB], writes=[bankB[bk]])
                    xa = x1[:, i, hf * 512:(hf + 1) * 512]
                    S.op("dve", lambda e: e.tensor_tensor(out=xa, in0=bank(bk), in1=xa, op=ALU.add),
                         reads=[bankB[bk], x1B[i]], writes=[x1B[i]])
                if g == NG - 1:
                    bg.add(tail_stream(i))
                    bg.step(2)

        bg = BG()

        def tail_stream(i):
            yield from ln_gen(x1[:, i, :], x1B[i], x1[:, i, :], x1B[i], g2, b2, g2B)
            S.dma("sp", out=y_d[i * 128:(i + 1) * 128, :], in_=x1[:, i, :], dsem=yoD[i % 2], reads=[x1B[i]])
            yield

        def ffn2_last_pair():
            for i in range(NT):
                for hf in range(2):
                    bk = next_bank()
                    S.op("pe", lambda e: [e.matmul(out=bank(bk), lhsT=hT[gg % 2][:, fb, i * 128:(i + 1) * 128],
                                                   rhs=w2[gg % 2][:, fb, hf * 512:(hf + 1) * 512],
                                                   start=(gg == NG - 2 and fb == 0), stop=(gg == NG - 1 and fb == 3))
                                          for gg in (NG - 2, NG - 1) for fb in range(4)],
                         reads=[hTB[0], hTB[1], w2B[0], w2B[1]], writes=[bankB[bk]])
                    xa = x1[:, i, hf * 512:(hf + 1) * 512]
                    S.op("dve", lambda e: e.tensor_tensor(out=xa, in0=bank(bk), in1=xa, op=ALU.add),
                         reads=[bankB[bk], x1B[i]], writes=[x1B[i]])
                bg.add(tail_stream(i))
                bg.step(3)

        ffn1(0)
        for g in range(NG - 2):
            ffn_load(g + 1)
            ffn1(g + 1)
            ffn2(g)
        ffn_load(NG - 1)
        ffn1(NG - 1)
        ffn2_last_pair()
        bg.drain()
        S.barrier()
      except _StopBuild:
        pass
    return nc


_PROG = {}


def _prep_shared(inp):
    f = lambda a: np.ascontiguousarray(np.asarray(a, dtype=np.float32))
    w_in = f(inp["w_in"])[0]
    sh = {}
    sh["lnp"] = f(np.stack([inp["ln_emb_g"], inp["ln_emb_b"], inp["ln1_g"][0], inp["ln1_b"][0],
                            inp["ln2_g"][0], inp["ln2_b"][0]], axis=0))
    sh["w_na"] = f(w_in[:, 0:1536])
    hg = []
    for h in range(4):
        cs = lambda base: w_in[:, base + h * 128: base + (h + 1) * 128]
        hg.append(np.concatenate([cs(1536), cs(3072), cs(3584), cs(2048), cs(2560)], axis=1))
    sh["w_hg"] = f(np.stack(hg, axis=0))
    sh["w_mq"] = f(w_in[:, 4096:4608])
    sh["w_mkv"] = f(inp["w_mem_kv"][0])
    wbr = [inp["w_branch_na"][0], inp["w_branch_hg"][0], inp["w_branch_mem"][0]]
    mg, mb = [], []
    for j in range(8):
        mg.append(np.concatenate([w_in[:, 4608 + b * 1024 + j * 128: 4608 + b * 1024 + (j + 1) * 128]
                                  for b in range(3)], axis=1))
        mb.append(np.concatenate([np.asarray(wbr[b])[:, j * 128:(j + 1) * 128] for b in range(3)], axis=1))
    sh["w_mg"] = f(np.stack(mg, axis=0))
    sh["w_mb"] = f(np.stack(mb, axis=0))
    sh["w_out"] = f(inp["w_out"][0])
    sh["w_ff1"] = f(inp["w_ff1"][0])
    sh["w_ff2"] = f(inp["w_ff2"][0])
    sh["ident"] = np.eye(128, dtype=np.float32)
    mf, mb_ = _hg_masks()
    sh["hgm"] = f(np.stack([mf, mb_], axis=0))
    m = np.ones((1, T), np.float32)
    m[0, ::64] = 0.0
    sh["scanm"] = m
    bias, mask = _na_tables(f(inp["na_rpb"])[0])
    sh["nab"] = f(bias)
    sh["nam"] = f(mask)
    lbl = f(inp["hg_lb_logits"]).reshape(2, 2, 4, 128).transpose(3, 0, 1, 2).reshape(128, 16)
    sh["lbl"] = f(lbl)
    sh["hgn"] = f(inp["hg_norm_g"]).reshape(1, 512)
    sh["lnpT"] = f(sh["lnp"].reshape(6, 8, 128).transpose(2, 0, 1).reshape(128, 48))
    return sh


def kernel(**inputs):
    x = np.asarray(inputs["x"], dtype=np.float32)
    mem = np.asarray(inputs["mem"], dtype=np.float32)
    nb = x.shape[0]
    sh = _prep_shared(inputs)
    if "nc" not in _PROG:
        _PROG["nc"] = build_program()
    nc = _PROG["nc"]
    in_maps = []
    for b in range(nb):
        m = dict(sh)
        m["x"] = np.ascontiguousarray(x[b])
        m["mem"] = np.ascontiguousarray(mem[b])
        in_maps.append(m)
    res = run_bass_kernel_spmd(nc, in_maps, core_ids=list(range(nb)))
    out = np.stack([np.asarray(r["y"], dtype=np.float32) for r in res.results], axis=0)
    return out
```

```python
import numpy as np
from contextlib import ExitStack
import concourse.bass as bass
import concourse.mybir as mybir
from concourse.bass_utils import run_bass_kernel_spmd

F32 = mybir.dt.float32
BF16 = mybir.dt.bfloat16
AF = mybir.ActivationFunctionType
ALU = mybir.AluOpType

T = 2048
D = 1024
NT = 16
ALPHA = 2.0 ** 0.25
LN_EPS = 1e-5
RMS_EPS = 1e-6
NEG = -30000.0


class Buf:
    __slots__ = ("w", "r", "name")

    def __init__(self, name=""):
        self.w = {}
        self.r = {}
        self.name = name


class _Eng:
    def __init__(self, e, sem, name):
        self.e = e
        self.sem = sem
        self.cnt = 0
        self.waited = {}
        self.name = name


class _DSem:
    def __init__(self, sem):
        self.sem = sem
        self.cnt = 0


class Sched:
    def __init__(self, nc, stack):
        self.nc = nc
        self.stack = stack
        self.E = {}
        for name, e in (("pe", nc.tensor), ("act", nc.scalar), ("dve", nc.vector),
                        ("pool", nc.gpsimd), ("sp", nc.sync)):
            sem = stack.enter_context(nc.semaphore("s_" + name))
            self.E[name] = _Eng(e, sem, name)
        self.dsems = []
        self.sems = {}
        for E in self.E.values():
            self.sems[id(E.sem)] = E.sem

    def dsem(self, name):
        sem = self.stack.enter_context(self.nc.semaphore("d_" + name))
        d = _DSem(sem)
        self.dsems.append(d)
        self.sems[id(sem)] = sem
        return d

    def _need(self, E, deps):
        need = {}
        for sid, val in deps:
            if E.name == "pe" and sid == id(E.sem):
                continue
            if E.waited.get(sid, 0) >= val:
                continue
            if need.get(sid, 0) < val:
                need[sid] = val
        return list(need.items())

    @staticmethod
    def _deps(reads, writes):
        deps = []
        for b in reads:
            deps += list(b.w.items())
        for b in writes:
            deps += list(b.w.items())
            deps += list(b.r.items())
        return deps

    @staticmethod
    def _record(tok, reads, writes):
        sid, val = tok
        for b in reads:
            if b.r.get(sid, 0) < val:
                b.r[sid] = val
        for b in writes:
            if b.w.get(sid, 0) < val:
                b.w[sid] = val

    def op(self, eng, fn, reads=(), writes=()):
        E = self.E[eng]
        need = self._need(E, self._deps(reads, writes))
        for sid, val in need[:-1]:
            E.e.wait_ge(self.sems[sid], val)
        res = fn(E.e)
        insts = list(res) if isinstance(res, (list, tuple)) else [res]
        if need:
            sid, val = need[-1]
            insts[0]._wait_ge(self.sems[sid], val)
        for sid, val in need:
            E.waited[sid] = val
        E.cnt += 1
        insts[-1].then_inc(E.sem, 1)
        tok = (id(E.sem), E.cnt)
        self._record(tok, reads, writes)
        return tok

    def dma(self, q, out, in_, dsem, reads=(), writes=(), **kw):
        E = self.E[q]
        need = self._need(E, self._deps(reads, writes))
        for sid, val in need:
            E.e.wait_ge(self.sems[sid], val)
            E.waited[sid] = val
        ins = E.e.dma_start(out=out, in_=in_, **kw)
        dsem.cnt += 16
        ins.then_inc(dsem.sem, 16)
        tok = (id(dsem.sem), dsem.cnt)
        self._record(tok, reads, writes)
        return tok

    def barrier(self, engines=None):
        toks = [(id(E.sem), E.cnt) for E in self.E.values() if E.cnt > 0]
        toks += [(id(d.sem), d.cnt) for d in self.dsems if d.cnt > 0]
        for name, E in self.E.items():
            if engines is not None and name not in engines:
                continue
            saved = E.name
            E.name = "_"
            need = self._need(E, toks)
            E.name = saved
            for sid, val in need:
                E.e.wait_ge(self.sems[sid], val)
                E.waited[sid] = val


class _StopBuild(Exception):
    pass


class Arena:
    def __init__(self, big, cap_bytes):
        self.big = big
        self.cap = cap_bytes

    def at(self, off, shape, dt):
        n = 1
        for s in shape[1:]:
            n *= s
        esz = 4 if dt == F32 else 2
        assert off % 4 == 0
        assert off + n * esz <= self.cap, (off, shape, self.cap)
        ap = self.big[:, off // 2: off // 2 + n * esz // 2]
        if dt == F32:
            ap = ap.bitcast(F32)
        if len(shape) == 3:
            ap = ap.rearrange("p (a b) -> p a b", a=shape[1])
        elif len(shape) == 4:
            ap = ap.rearrange("p (a b c) -> p a b c", a=shape[1], b=shape[2])
        if shape[0] != 128:
            ap = ap[0:shape[0]]
        return ap


class Bump:
    def __init__(self, arena, lo, hi):
        self.a = arena
        self.lo = lo
        self.hi = hi
        self.top = lo

    def alloc(self, shape, dt):
        n = 1
        for s in shape[1:]:
            n *= s
        nbytes = n * (4 if dt == F32 else 2)
        nbytes = (nbytes + 63) // 64 * 64
        off = self.top
        assert off + nbytes <= self.hi, ("arena overflow", off, nbytes, self.hi)
        self.top += nbytes
        return self.a.at(off, shape, dt)


def _na_plan():
    rs = lambda r: int(np.clip(r - 4, 0, 24))
    plan = {}
    for j in range(16):
        lo = rs(2 * j) // 2
        hi = (rs(2 * j + 1) + 7) // 2
        plan[j] = list(range(lo, hi + 1))
    return plan


def _na_mask_tile(j, c):
    rs = lambda r: int(np.clip(r - 4, 0, 24))
    m = np.full((128, 128), NEG, np.float32)
    kc = np.arange(64)[:, None]
    qc = np.arange(64)[None, :]
    c0 = np.clip(qc - 8, 0, 48)
    colv = (kc >= c0) & (kc < c0 + 16)
    for krl in range(2):
        for qrl in range(2):
            kr = 2 * c + krl
            qr = 2 * j + qrl
            if rs(qr) <= kr <= rs(qr) + 7:
                blk = np.where(colv, 0.0, NEG).astype(np.float32)
                m[krl * 64:(krl + 1) * 64, qrl * 64:(qrl + 1) * 64] = blk
    return m


def _na_tables(rpb):
    plan = _na_plan()
    kc = np.arange(64)[:, None]
    qc = np.arange(64)[None, :]
    dc = np.clip(kc - qc + 15, 0, 30)
    bias = np.zeros((8, 128, 7 * 128), np.float32)
    for di in range(7):
        delta = di - 3
        for krl in range(2):
            for qrl in range(2):
                dr = int(np.clip(2 * delta + krl - qrl + 7, 0, 14))
                bias[:, krl * 64:(krl + 1) * 64, di * 128 + qrl * 64: di * 128 + (qrl + 1) * 64] = rpb[:, dr][:, dc]
    tiles = []
    for delta in range(-2, 3):
        t = _na_mask_tile(6, 6 + delta)
        tiles.append(t)
    for j in (0, 1, 14, 15):
        assert len(plan[j]) == 4
        for c in plan[j]:
            tiles.append(_na_mask_tile(j, c))
    mask = np.concatenate(tiles, axis=1)
    return bias, mask


def _na_offsets(j):
    plan = _na_plan()
    chunks = plan[j]
    delta0 = chunks[0] - j
    boff = (delta0 + 3) * 128
    if 2 <= j <= 13:
        assert len(chunks) == 5 and delta0 == -2
        moff = 0
    else:
        moff = 640 + {0: 0, 1: 512, 14: 1024, 15: 1536}[j]
    return chunks, boff, moff


def _hg_masks():
    s = np.arange(128)[:, None]
    t = np.arange(128)[None, :]
    same = (s // 64) == (t // 64)
    mf = (same & (s <= t)).astype(np.float32)
    mb = (same & (s >= t)).astype(np.float32)
    return mf, mb


CAP = 211968


def build_program(dbg=None):
    nc = bass.Bass("TRN2", target_bir_lowering=False)

    def din(name, shape):
        return nc.dram_tensor(name, list(shape), F32, kind="ExternalInput").ap()

    x_d = din("x", [T, D])
    mem_d = din("mem", [256, D])
    lnp_d = din("lnp", [6, D])
    w_na_d = din("w_na", [D, 1536])
    w_hg_d = din("w_hg", [4, D, 640])
    w_mq_d = din("w_mq", [D, 512])
    w_mkv_d = din("w_mkv", [D, 1024])
    w_mg_d = din("w_mg", [8, D, 384])
    w_mb_d = din("w_mb", [8, 512, 384])
    w_out_d = din("w_out", [D, D])
    w_ff1_d = din("w_ff1", [D, 4096])
    w_ff2_d = din("w_ff2", [4096, D])
    ident_d = din("ident", [128, 128])
    hgm_d = din("hgm", [2, 128, 128])
    scanm_d = din("scanm", [1, T])
    nab_d = din("nab", [8, 128, 896])
    nam_d = din("nam", [128, 2688])
    lbl_d = din("lbl", [128, 16])
    hgn_d = din("hgn", [1, 512])
    lnpT_d = din("lnpT", [128, 48])
    y_d = nc.dram_tensor("y", [T, D], F32, kind="ExternalOutput").ap()
    dbg_d = None
    if dbg is not None:
        dbg_d = nc.dram_tensor("dbg", [128, 8192], F32, kind="ExternalOutput").ap()

    with ExitStack() as stack:
      try:
        big = stack.enter_context(nc.sbuf_tensor("big", [128, CAP // 2], BF16))
        pp = [stack.enter_context(nc.psum_tensor(f"pp{i}", [128, 1024], F32)) for i in range(4)]
        S = Sched(nc, stack)
        A = Arena(big, CAP)
        KB = 1024

        def bank(k):
            return pp[k // 2][:, (k % 2) * 512:(k % 2) * 512 + 512]

        bankB = [Buf(f"bank{k}") for k in range(8)]
        rr = {"n": 0}

        def next_bank(lo=0, hi=8):
            k = lo + rr["n"] % (hi - lo)
            rr["n"] += 1
            return k

        cb = Bump(A, 0, 12 * KB)
        ident = cb.alloc([128, 128], F32)
        hgm = cb.alloc([128, 2, 128], F32)
        lbl = cb.alloc([128, 16], F32)
        lbv = cb.alloc([128, 8], F32)
        omv = cb.alloc([128, 8], F32)
        hgn = cb.alloc([128, 512], F32)
        epsln = cb.alloc([128, 1], F32)
        epsrms = cb.alloc([128, 1], F32)
        lnsm = [cb.alloc([128, 32], F32) for _ in range(6)]
        rec_s = [cb.alloc([128, 8], F32) for _ in range(4)]
        scanm = cb.alloc([128, T], BF16)
        lnpT = cb.alloc([128, 48], F32)
        identb = cb.alloc([128, 128], BF16)
        ones_bf = cb.alloc([128, 128], BF16)
        lnst = cb.alloc([128, NT, 2], F32)
        lnstB = Buf("lnst")
        constB = Buf("const")
        cD = S.dsem("const")
        gbuf = A.at(12 * KB, [128, D], F32)
        bbuf = A.at(16 * KB, [128, D], F32)
        gbB = Buf("gb")
        gbD = S.dsem("gb")
        RA = 20 * KB
        RB = 52 * KB
        RC = 100 * KB
        RD = 132 * KB
        TOP = CAP

        S.dma("sp", out=ident, in_=ident_d, dsem=cD, writes=[constB])
        S.dma("sp", out=hgm, in_=hgm_d.rearrange("a p n -> p a n"), dsem=cD, writes=[constB])
        S.dma("sp", out=lbl, in_=lbl_d, dsem=cD, writes=[constB])
        S.dma("sp", out=lnpT, in_=lnpT_d, dsem=cD, writes=[constB])
        S.dma("sp", out=hgn, in_=hgn_d.partition_broadcast(128) if False else hgn_d[0:1, :].to_broadcast([128, 512]),
              dsem=cD, writes=[constB])
        S.op("pool", lambda e: e.memset(epsln, LN_EPS), writes=[constB])
        S.op("act", lambda e: e.activation(out=identb, in_=ident, func=AF.Copy), reads=[constB], writes=[constB])
        S.op("pool", lambda e: e.memset(ones_bf, 1.0), writes=[constB])
        S.op("pool", lambda e: e.memset(epsrms, RMS_EPS), writes=[constB])
        lbl3 = lbl.rearrange("p (d s h) -> p d s h", d=2, s=2)
        lbv3 = lbv.rearrange("p (d h) -> p d h", d=2)
        omv3 = omv.rearrange("p (d h) -> p d h", d=2)
        S.op("dve", lambda e: e.tensor_tensor(out=lbv3, in0=lbl3[:, :, 0, :], in1=lbl3[:, :, 1, :], op=ALU.subtract),
             reads=[constB], writes=[constB])
        S.op("act", lambda e: e.activation(out=lbv, in_=lbv, func=AF.Sigmoid), reads=[constB], writes=[constB])
        S.op("dve", lambda e: e.tensor_scalar(out=omv, in0=lbv, scalar1=-1.0, scalar2=1.0, op0=ALU.mult, op1=ALU.add),
             reads=[constB], writes=[constB])

        def load_ln(k):
            S.dma("sp", out=gbuf, in_=lnp_d[2 * k:2 * k + 1, :].to_broadcast([128, D]), dsem=gbD, writes=[gbB])
            S.dma("sp", out=bbuf, in_=lnp_d[2 * k + 1:2 * k + 2, :].to_broadcast([128, D]), dsem=gbD, writes=[gbB])

        lnB = [Buf(f"ln{i}") for i in range(6)]
        lncnt = {"n": 0}

        def ln_gen(src, srcB, dst, dstB, gA, bA, gB_, rn=None, rnB=None, affine=True):
            slot = lncnt["n"] % 6
            lncnt["n"] += 1
            sm = lnsm[slot]
            B_ = lnB[slot]
            rstd_ap = sm[:, 15:16] if rn is None else rn[:, 0:1]
            nmr_ap = sm[:, 16:17] if rn is None else rn[:, 1:2]
            rw = [B_] if rnB is None else [B_, rnB]
            S.op("dve", lambda e: [e.bn_stats(out=sm[:, 0:6], in_=src[:, 0:512]),
                                   e.bn_stats(out=sm[:, 6:12], in_=src[:, 512:1024])], reads=[srcB], writes=[B_])
            yield
            S.op("dve", lambda e: e.bn_aggr(out=sm[:, 12:14], in_=sm[:, 0:12]), reads=[B_], writes=[B_])
            yield
            S.op("act", lambda e: e.activation(out=sm[:, 14:15], in_=sm[:, 13:14], func=AF.Sqrt, bias=epsln[:, 0:1],
                                               scale=1.0), reads=[B_, constB], writes=[B_])
            yield
            S.op("dve", lambda e: e.reciprocal(out=rstd_ap, in_=sm[:, 14:15]), reads=[B_], writes=rw)
            yield
            S.op("dve", lambda e: e.tensor_scalar(out=nmr_ap, in0=sm[:, 12:13], scalar1=rstd_ap,
                                                  scalar2=-1.0, op0=ALU.mult, op1=ALU.mult), reads=rw, writes=rw)
            yield
            S.op("act", lambda e: e.activation(out=src, in_=src, func=AF.Identity, bias=nmr_ap,
                                               scale=rstd_ap), reads=[srcB] + rw, writes=[srcB])
            yield
            if not affine:
                return
            S.op("pool", lambda e: e.tensor_tensor(out=src, in0=src, in1=gA, op=ALU.mult),
                 reads=[srcB, gB_], writes=[srcB])
            yield
            S.op("dve", lambda e: e.tensor_tensor(out=dst, in0=src, in1=bA, op=ALU.add),
                 reads=[srcB, gB_], writes=[dstB])
            yield

        def ln_apply_gen(src, srcB, dst, dstB, gA, bA, gB_, rn, rnB):
            S.op("act", lambda e: e.activation(out=src, in_=src, func=AF.Identity, bias=rn[:, 1:2],
                                               scale=rn[:, 0:1]), reads=[srcB, rnB], writes=[srcB])
            yield
            S.op("pool", lambda e: e.tensor_tensor(out=src, in0=src, in1=gA, op=ALU.mult),
                 reads=[srcB, gB_], writes=[srcB])
            yield
            S.op("dve", lambda e: e.tensor_tensor(out=dst, in0=src, in1=bA, op=ALU.add),
                 reads=[srcB, gB_], writes=[dstB])
            yield

        def ln_ip(*a_):
            for _ in ln_gen(*a_):
                pass

        def run_streams(gens, max_active, skew=1):
            active = []
            nxt = 0
            tick = 0
            while nxt < len(gens) or active:
                if nxt < len(gens) and len(active) < max_active and tick % skew == 0:
                    active.append(gens[nxt])
                    nxt += 1
                for g in list(active):
                    try:
                        next(g)
                    except StopIteration:
                        active.remove(g)
                tick += 1

        class BG:
            def __init__(self):
                self.active = []

            def add(self, g):
                self.active.append(g)

            def step(self, n=1):
                for _ in range(n):
                    for g in list(self.active):
                        try:
                            next(g)
                        except StopIteration:
                            self.active.remove(g)

            def drain(self):
                while self.active:
                    self.step(1)

        def transpose_tile_to(srcs, srcB, dst_fn, dstB, evac="act", banks=(4, 8)):
            bk = next_bank(*banks)
            n = len(srcs)
            S.op("pe", lambda e: [e.transpose(out=bank(bk)[:, q * 128:(q + 1) * 128], in_=srcs[q], identity=ident)
                                  for q in range(n)], reads=[srcB, constB], writes=[bankB[bk]])
            out_ap, in_ap = dst_fn(bank(bk)[:, 0:n * 128])
            if evac == "act":
                S.op("act", lambda e: e.activation(out=out_ap, in_=in_ap, func=AF.Copy),
                     reads=[bankB[bk]], writes=[dstB])
            else:
                S.op("dve", lambda e: e.tensor_copy(out=out_ap, in_=in_ap), reads=[bankB[bk]], writes=[dstB])

        def transpose_gen(srcs, srcB, dst_fn, dstB, bk=None):
            bk = next_bank(4, 8) if bk is None else bk
            n = len(srcs)
            S.op("pe", lambda e: [e.transpose(out=bank(bk)[:, q * 128:(q + 1) * 128], in_=srcs[q], identity=ident)
                                  for q in range(n)], reads=[srcB, constB], writes=[bankB[bk]])
            yield
            out_ap, in_ap = dst_fn(bank(bk)[:, 0:n * 128])
            S.op("act", lambda e: e.activation(out=out_ap, in_=in_ap, func=AF.Copy),
                 reads=[bankB[bk]], writes=[dstB])
            yield

        def load_w(dst, src, buf, dsem):
            S.dma("pool", out=dst, in_=src.rearrange("(c p) n -> p c n", p=128), dsem=dsem, writes=[buf])

        def dump(ap_list, stage_off=None):
            S.barrier()
            off = 0
            dD = S.dsem("dbg")
            stage = A.at(RD if stage_off is None else stage_off, [128, 8192], F32)
            sB = Buf("dbgstage")
            for ap in ap_list:
                n = ap.shape[1]
                S.op("dve", lambda e: e.tensor_copy(out=stage[:, off:off + n], in_=ap), writes=[sB])
                off += n
            S.dma("sp", out=dbg_d[:, 0:off], in_=stage[:, 0:off], dsem=dD, reads=[sB])
            S.barrier()
            raise _StopBuild()

        x0T = A.at(RA, [128, 8, T], BF16)
        x0TB = Buf("x0T")
        x0TBk = [Buf(f"x0Tk{i}") for i in range(4)]
        load_ln(0)
        p0 = Bump(A, RC, RC + 16 * KB)
        NS0 = 4
        xs = [p0.alloc([128, D], F32) for _ in range(NS0)]
        xsB = [Buf(f"xs{i}") for i in range(NS0)]
        xsD = [S.dsem(f"xs{i}") for i in range(NS0)]

        def p0_stream(i):
            sl = i % NS0
            S.dma("sp", out=xs[sl], in_=x_d[i * 128:(i + 1) * 128, :], dsem=xsD[sl], writes=[xsB[sl]])
            yield
            yield from ln_gen(xs[sl], xsB[sl], xs[sl], xsB[sl], gbuf, bbuf, gbB, rn=lnst[:, i, :], rnB=lnstB,
                              affine=False)
            for half in range(2):
                bk = 4 + sl
                S.op("pe", lambda e: [e.transpose(out=bank(bk)[:, q * 128:(q + 1) * 128],
                                                  in_=xs[sl][:, (half * 4 + q) * 128:(half * 4 + q + 1) * 128],
                                                  identity=ident) for q in range(4)],
                     reads=[xsB[sl], constB], writes=[bankB[bk]])
                yield
                for q in range(4):
                    kc = half * 4 + q
                    if q % 2 == 0:
                        S.op("act", lambda e: e.activation(out=x0T[:, kc, i * 128:(i + 1) * 128],
                                                           in_=bank(bk)[:, q * 128:(q + 1) * 128], func=AF.Identity,
                                                           scale=lnpT[:, kc:kc + 1], bias=lnpT[:, 8 + kc:9 + kc]),
                             reads=[bankB[bk], constB], writes=[x0TB, x0TBk[i // 4]])
                    else:
                        S.op("dve", lambda e: e.tensor_scalar(out=x0T[:, kc, i * 128:(i + 1) * 128],
                                                              in0=bank(bk)[:, q * 128:(q + 1) * 128],
                                                              scalar1=lnpT[:, kc:kc + 1], scalar2=lnpT[:, 8 + kc:9 + kc],
                                                              op0=ALU.mult, op1=ALU.add),
                             reads=[bankB[bk], constB], writes=[x0TB, x0TBk[i // 4]])
                    yield

        ph = Bump(A, RC + 16 * KB, TOP)
        ovl = Bump(A, RC, RC + 16 * KB)
        wbuf = [ph.alloc([128, 8, 640], BF16) for _ in range(2)]
        wB = [Buf("w0"), Buf("w1")]
        wD = [S.dsem("w0"), S.dsem("w1")]
        wi = {"n": 0}
        ph_mark = ph.top

        def proj_fm(W, wBuf, c0, dst_fn, dstB, func=AF.Copy, scale=1.0, kch=8, rhsT=None, rhsB=None, ntb=4,
                    banks=(0, 8), bgc=None, bgn=8):
            rT = x0T if rhsT is None else rhsT
            rB = x0TB if rhsB is None else rhsB
            for tb in range(ntb):
                bk = next_bank(*banks)
                S.op("pe", lambda e: [e.matmul(out=bank(bk), lhsT=W[:, kc, c0:c0 + 128],
                                               rhs=rT[:, kc, tb * 512:(tb + 1) * 512],
                                               start=(kc == 0), stop=(kc == kch - 1)) for kc in range(kch)],
                     reads=[wBuf, rB], writes=[bankB[bk]])
                o = dst_fn(tb)
                S.op("act", lambda e: e.activation(out=o, in_=bank(bk), func=func, scale=scale),
                     reads=[bankB[bk]], writes=[dstB[tb] if isinstance(dstB, list) else dstB])
                if bgc is not None:
                    bgc.step(bgn)

        y_naT = A.at(RB, [128, 4, T], BF16)
        y_hgT = A.at(RB + 16 * KB, [128, 4, T], BF16)
        y_memT = A.at(RB + 32 * KB, [128, 4, T], BF16)
        yTB = Buf("yT")

        qT = ph.alloc([128, 4, T], BF16)
        kT = ph.alloc([128, 4, T], BF16)
        vaug = ph.alloc([128, 16, 8, 66], BF16)
        qkvB = Buf("qkv")
        namask = ph.alloc([128, 2688], F32)
        nabias = [ph.alloc([128, 896], F32) for _ in range(2)]
        nabB = [Buf("nab0"), Buf("nab1")]
        nabD = [S.dsem("nab0"), S.dsem("nab1")]
        sc = [ovl.alloc([128, 640], F32) for _ in range(3)]
        scB = [Buf("sc0"), Buf("sc1"), Buf("sc2")]
        pT = [ph.alloc([128, 640], BF16) for _ in range(3)]
        pTB = [Buf("pT0"), Buf("pT1"), Buf("pT2")]
        yst = ovl.alloc([128, 16, 128], F32)
        ystB = Buf("yst")
        S.dma("sp", out=namask, in_=nam_d, dsem=cD, writes=[constB])
        S.op("pool", lambda e: e.memset(vaug[:, :, :, 64:66], 1.0), writes=[qkvB])

        nx = Bump(A, RB + 16 * KB, RB + 48 * KB)
        Bh = [nx.alloc([128, 2688], BF16) for _ in range(3)]
        BhB = [Buf(f"Bh{i}") for i in range(4)]
        ystraw = nx.alloc([128, 16, 2, 66], F32)
        ystrawB = Buf("ystraw")
        w3 = nx.alloc([128, 8, 512], BF16)
        w3B = Buf("w3")
        w3D = S.dsem("w3")
        Bh.append(w3.rearrange("p a b -> p (a b)")[:, 0:2688])
        Wna = [wbuf[0][:, :, 0:512], wbuf[1][:, :, 0:512], w3]
        WnaB = [wB[0], wB[1], w3B]
        load_w(Wna[0], w_na_d[:, 0:512], wB[0], wD[0])
        load_w(Wna[1], w_na_d[:, 512:1024], wB[1], wD[1])
        load_w(Wna[2], w_na_d[:, 1024:1536], w3B, w3D)
        wi["n"] = 3
        evn = {"n": 0, "b": 0}

        def pbank():
            evn["b"] += 1
            return evn["b"] % 4

        def na_proj_stream(tb):
            tsl = slice(tb * 512, (tb + 1) * 512)
            for blk in range(2):
                dstT = qT if blk == 0 else kT
                scl = 0.125 if blk == 0 else 1.0
                for cb_ in range(4):
                    bk = pbank()
                    S.op("pe", lambda e: [e.matmul(out=bank(bk), lhsT=Wna[blk][:, kc, cb_ * 128:(cb_ + 1) * 128],
                                                   rhs=x0T[:, kc, tsl], start=(kc == 0), stop=(kc == 7))
                                          for kc in range(8)], reads=[WnaB[blk], x0TBk[tb]], writes=[bankB[bk]])
                    yield
                    evn["n"] += 1
                    if evn["n"] % 2 == 0:
                        S.op("act", lambda e: e.activation(out=dstT[:, cb_, tsl], in_=bank(bk), func=AF.Copy,
                                                           scale=scl), reads=[bankB[bk]], writes=[qkvB])
                    else:
                        S.op("dve", lambda e: e.tensor_scalar(out=dstT[:, cb_, tsl], in0=bank(bk), scalar1=scl,
                                                              scalar2=None, op0=ALU.mult),
                             reads=[bankB[bk]], writes=[qkvB])
                    yield
            for q in range(4):
                i = tb * 4 + q
                bk = pbank()
                S.op("pe", lambda e: [e.matmul(out=bank(bk), lhsT=x0T[:, kc, i * 128:(i + 1) * 128],
                                               rhs=Wna[2][:, kc, :], start=(kc == 0), stop=(kc == 7))
                                      for kc in range(8)], reads=[WnaB[2], x0TBk[tb]], writes=[bankB[bk]])
                yield
                S.op("act", lambda e: e.activation(out=vaug[:, i, :, 0:64],
                                                   in_=bank(bk).rearrange("p (h d) -> p h d", h=8),
                                                   func=AF.Copy), reads=[bankB[bk]], writes=[qkvB])
                yield

        for tb in range(5):
            gens = []
            if tb < 4:
                gens += [p0_stream(i) for i in range(tb * 4, tb * 4 + 4)]
            if tb >= 1:
                gens.insert(min(1, len(gens)), na_proj_stream(tb - 1))
            run_streams(gens, 5, skew=2)
        if dbg == "x0T":
            dump([x0T[:, kc, 0:1024] for kc in range(8)])

        load_w(wbuf[1], w_hg_d[0], wB[1], wD[1])
        NSA = 3
        ucnt = {"n": 0}
        recn = lnsm[5][:, 0:32].rearrange("p (j h o) -> p j h o", j=16, h=2)

        def bh_build(h):
            hs = h % 2
            k = h % 4
            S.dma("sp", out=nabias[hs], in_=nab_d[h], dsem=nabD[hs], writes=[nabB[hs]])
            for (m0, m1, b0) in ((0, 640, 128), (640, 1152, 384), (1152, 1664, 256), (1664, 2176, 128),
                                 (2176, 2688, 0)):
                S.op("pool", lambda e: e.tensor_tensor(out=Bh[k][:, m0:m1], in0=namask[:, m0:m1],
                                                       in1=nabias[hs][:, b0:b0 + (m1 - m0)], op=ALU.add),
                     reads=[nabB[hs], constB] + ([w3B, qkvB] if k == 3 else []), writes=[BhB[k]])

        def na_unit(h, j):
            hb = h // 2
            po = (h % 2) * 64
            hs = h % 2
            k = h % 4
            u = ucnt["n"]
            ucnt["n"] += 1
            si = u % NSA
            ob = 6 + u % 2
            if j == 2 + (h % 2) and h + 2 < 8:
                bh_build(h + 2)
            chunks, boff, moff = _na_offsets(j)
            n = len(chunks)
            ps = pp[si]
            n0 = min(n, 4) * 128
            S.op("pe", lambda e: [e.matmul(out=ps[:, 0:n0], lhsT=identb, rhs=Bh[k][:, moff:moff + n0],
                                           start=True, stop=False)] +
                                 ([e.matmul(out=ps[:, 512:640], lhsT=identb, rhs=Bh[k][:, moff + 512:moff + 640],
                                            start=True, stop=False)] if n == 5 else []) +
                                 [e.matmul(out=ps[:, ci * 128:(ci + 1) * 128],
                                           lhsT=kT[po:po + 64, hb, c * 128:(c + 1) * 128],
                                           rhs=qT[po:po + 64, hb, j * 128:(j + 1) * 128],
                                           start=False, stop=True) for ci, c in enumerate(chunks)],
                 reads=[qkvB, BhB[k], constB], writes=[bankB[2 * si], bankB[2 * si + 1]])
            yield
            S.op("act", lambda e: e.activation(out=pT[si][:, 0:n * 128], in_=ps[:, 0:n * 128], func=AF.Exp),
                 reads=[bankB[2 * si], bankB[2 * si + 1]], writes=[pTB[si]])
            yield
            S.op("pe", lambda e: [e.matmul(out=bank(ob)[:, 0:66], lhsT=pT[si][:, ci * 128:(ci + 1) * 128],
                                           rhs=vaug[:, c, h, :], start=(ci == 0), stop=(ci == n - 1))
                                  for ci, c in enumerate(chunks)],
                 reads=[pTB[si], qkvB], writes=[bankB[ob]])
            yield
            S.op("dve", lambda e: e.tensor_copy(out=ystraw[:, j, hs, :], in_=bank(ob)[:, 0:66]),
                 reads=[bankB[ob]], writes=[ystrawB])
            yield

        bh_build(0)
        bh_build(1)
        for hb_ in range(4):
            run_streams([na_unit(h, j) for j in range(NT) for h in (2 * hb_, 2 * hb_ + 1)], NSA, skew=1)
            S.op("dve", lambda e: e.reciprocal(out=recn, in_=ystraw[:, :, :, 64:65]),
                 reads=[ystrawB], writes=[ystB])
            S.op("dve", lambda e: e.tensor_tensor(out=yst.rearrange("p j (h d) -> p j h d", h=2),
                                                  in0=ystraw[:, :, :, 0:64],
                                                  in1=recn.to_broadcast([128, 16, 2, 64]), op=ALU.mult),
                 reads=[ystrawB, ystB], writes=[ystB])
            for j4 in range(4):
                srcs = [yst[:, j4 * 4 + q, :] for q in range(4)]
                transpose_tile_to(srcs, ystB,
                                  lambda bap: (y_naT[:, hb_, j4 * 512:(j4 + 1) * 512], bap), yTB, banks=(6, 8))
        if dbg == "y_naT":
            dump([y_naT[:, kc, 0:T] for kc in range(4)])

        S.barrier()
        ph.top = ph_mark
        hx = Bump(A, RB + 32 * KB, RB + 48 * KB)
        ov2 = Bump(A, RC, RC + 16 * KB)
        q32 = ov2.alloc([128, T], F32)
        Fb = ov2.alloc([128, T], F32)
        Gext = ph.alloc([128, T + 16], F32)
        Gb = Gext[:, 1:T + 1]
        Tq = lnsm[4][:, 0:32].rearrange("p (a b) -> p a b", a=4)
        TqB = Buf("Tq")
        Ib = ph.alloc([128, T], F32)
        Ab = ph.alloc([128, T], F32)
        qt = [ph.alloc([128, T], BF16) for _ in range(2)]
        kt = [ph.alloc([128, T], BF16) for _ in range(2)]
        kh = [ph.alloc([128, T], F32) for _ in range(2)]
        expT = [ph.alloc([128, 32], F32) for _ in range(2)]
        vh = ph.alloc([128, 16, 128], BF16)
        oacc = ph.alloc([128, 16, 128], F32)
        sog = hx.alloc([128, 16, 128], F32)
        kA = [hx.alloc([128, 512], BF16) for _ in range(3)]
        At2 = [hx.alloc([128, 256], BF16) for _ in range(3)]
        S32p = [hx.alloc([128, 2, 128], F32) for _ in range(2)]
        S32B = [Buf("S32_0"), Buf("S32_1")]
        Sbf2 = [hx.alloc([128, 2, 128], BF16) for _ in range(2)]
        ssq = hx.alloc([128, 16], F32)
        rstd_h = hx.alloc([128, 16], F32)
        ysth = [ph.alloc([128, 128], F32) for _ in range(4)]
        junk = ph.alloc([128, 128], F32)
        hgQ = [Buf(f"hgQ{i}") for i in range(4)]
        q32Q = [Buf(f"q32Q{i}") for i in range(4)]
        dirQ = [[Buf(f"dirQ{d}{i}") for i in range(4)] for d in range(2)]
        hgB = Buf("hg")
        khtokB = [Buf(f"khtok{i}") for i in range(4)]
        AtB = [Buf(f"At{i}") for i in range(4)]
        SB_ = [Buf("S_f"), Buf("S_b")]
        SbB = [Buf("Sb0"), Buf("Sb1")]
        fbQ = [[Buf(f"fb{k}_{q}") for q in range(4)] for k in range(4)]
        oaccB = Buf("oacc")
        ysthB = [Buf(f"ysth{i}") for i in range(4)]
        voB = Buf("vo")
        S.op("pool", lambda e: e.memset(Gext[:, 0:1], 0.0), writes=[constB])
        S.op("pool", lambda e: e.memset(scanm, 1.0), writes=[constB])
        S.op("pool", lambda e: e.memset(scanm.rearrange("p (c s) -> p c s", c=32)[:, :, 0:1], 0.0), writes=[constB])
        rot = {"k": 0, "f": 0}
        HB = (4, 8)

        def Q(ap, qr):
            return ap[:, qr * 512:(qr + 1) * 512]

        def Q3(ap, qr):
            return ap[:, qr * 512:(qr + 1) * 512].rearrange("p (c s) -> p c s", c=8)

        def hg_load(h_):
            ws_ = (3 + h_) % 2
            load_w(wbuf[ws_], w_hg_d[h_], wB[ws_], wD[ws_])

        def epi_gen(h):
            S.op("act", lambda e: e.activation(out=sog.rearrange("p a b -> p (a b)"),
                                               in_=sog.rearrange("p a b -> p (a b)"), func=AF.Silu),
                 reads=[voB], writes=[voB])
            yield
            for i in range(NT):
                S.op("act", lambda e: e.activation(out=junk, in_=oacc[:, i, :], func=AF.Square,
                                                   accum_out=ssq[:, i:i + 1]), reads=[oaccB], writes=[hgB])
                yield
            S.op("act", lambda e: e.activation(out=rstd_h, in_=ssq, func=AF.Sqrt, bias=epsrms[:, 0:1],
                                               scale=1.0 / 128.0), reads=[hgB, constB], writes=[hgB])
            yield
            S.op("dve", lambda e: e.reciprocal(out=rstd_h, in_=rstd_h), reads=[hgB], writes=[hgB])
            yield
            for i4 in range(4):
                for q in range(4):
                    i = i4 * 4 + q
                    S.op("dve", lambda e: e.scalar_tensor_tensor(out=ysth[q], in0=oacc[:, i, :],
                                                                 scalar=rstd_h[:, i:i + 1],
                                                                 in1=hgn[:, h * 128:(h + 1) * 128],
                                                                 op0=ALU.mult, op1=ALU.mult),
                         reads=[oaccB, hgB, constB], writes=[ysthB[q]])
                    yield
                    S.op("pool", lambda e: e.tensor_tensor(out=ysth[q], in0=ysth[q], in1=sog[:, i, :], op=ALU.mult),
                         reads=[ysthB[q], voB], writes=[ysthB[q]])
                    yield
                bk = next_bank(*HB)
                S.op("pe", lambda e: [e.transpose(out=bank(bk)[:, q * 128:(q + 1) * 128], in_=ysth[q], identity=ident)
                                      for q in range(4)], reads=ysthB + [constB], writes=[bankB[bk]])
                S.op("act", lambda e: e.activation(out=y_hgT[:, h, i4 * 512:(i4 + 1) * 512], in_=bank(bk), func=AF.Copy),
                     reads=[bankB[bk]], writes=[yTB])
                yield

        pend_epi = None
        for h in range(4):
            ws = (3 + h) % 2
            wi["n"] += 1
            W = wbuf[ws]
            bge = BG()
            if pend_epi is not None:
                bge.add(pend_epi)
            proj_fm(W, wB[ws], 0, lambda tb: q32[:, tb * 512:(tb + 1) * 512], q32Q, func=AF.Silu, banks=HB,
                    bgc=bge, bgn=8)
            def vog_emit(bgc):
                for i in range(NT):
                    bk = next_bank(*HB)
                    S.op("pe", lambda e: [e.matmul(out=bank(bk)[:, 0:256], lhsT=x0T[:, kc, i * 128:(i + 1) * 128],
                                                   rhs=W[:, kc, 384:640], start=(kc == 0), stop=(kc == 7))
                                          for kc in range(8)], reads=[wB[ws], x0TB], writes=[bankB[bk]])
                    S.op("act", lambda e: e.activation(out=vh[:, i, :], in_=bank(bk)[:, 0:128], func=AF.Copy),
                         reads=[bankB[bk]], writes=[voB])
                    S.op("act", lambda e: e.activation(out=sog[:, i, :], in_=bank(bk)[:, 128:256], func=AF.Copy),
                         reads=[bankB[bk]], writes=[voB])
                    bgc.step(4)

            for d in range(2):
                col = d * 4 + h
                lb_ap = lbv[:, col:col + 1]
                om_ap = omv[:, col:col + 1]
                proj_fm(W, wB[ws], 128 + d * 128, lambda tb: Fb[:, tb * 512:(tb + 1) * 512], hgQ,
                        func=AF.Sigmoid, banks=HB, bgc=(bge if d == 0 else None), bgn=8)
                if d == 0:
                    bge.drain()
                if d == 1 and h + 1 < 4:
                    hg_load(h + 1)
                if d == 1 and h == 3:
                    load_w(wbuf[1][:, :, 0:512], w_mkv_d[:, 0:512], wB[1], wD[1])
                    load_w(wbuf[0][:, :, 0:512], w_mkv_d[:, 512:1024], wB[0], wD[0])
                chain = []
                chain.append(lambda qr: S.op("act", lambda e: e.activation(
                    out=Q(Fb, qr), in_=Q(Fb, qr), func=AF.Identity, bias=lb_ap, scale=om_ap),
                    reads=[hgQ[qr], constB], writes=[hgQ[qr]]))
                chain.append(lambda qr: S.op("act", lambda e: e.activation(out=Q(Gb, qr), in_=Q(Fb, qr), func=AF.Ln),
                                             reads=[hgQ[qr]], writes=[hgQ[qr]]))
                chain.append(lambda qr: S.op("act", lambda e: e.activation(
                    out=Q(Fb, qr), in_=Q(Fb, qr), func=AF.Identity, bias=1.0, scale=-1.0),
                    reads=[hgQ[qr]], writes=[hgQ[qr]]))
                if d == 0:
                    chain.append(lambda qr: S.op("dve", lambda e: e.tensor_tensor_scan(
                        out=Q(Ib, qr), data0=scanm[:, 0:512], data1=Q(Gb, qr), initial=0.0, op0=ALU.mult, op1=ALU.add),
                        reads=[hgQ[qr], constB], writes=[hgQ[qr]]))
                    chain.append(lambda qr: S.op("dve", lambda e: e.tensor_tensor(
                        out=Q3(Ab, qr), in0=Q3(Ib, qr)[:, :, 63:64].to_broadcast([128, 8, 64]), in1=Q3(Ib, qr),
                        op=ALU.subtract), reads=[hgQ[qr]], writes=[hgQ[qr]]))
                    chain.append(lambda qr: S.op("act", lambda e: e.activation(
                        out=expT[d][:, qr * 8:(qr + 1) * 8], in_=Q3(Ib, qr)[:, :, 63:64].rearrange("p c o -> p (c o)"),
                        func=AF.Exp), reads=[hgQ[qr]], writes=[dirQ[d][qr]]))
                    X_, U_ = Ab, Ib
                else:
                    chain.append(lambda qr: S.op("dve", lambda e: e.tensor_tensor_scan(
                        out=Q(Ib, qr), data0=Gext[:, qr * 512:qr * 512 + 512], data1=scanm[:, 0:512], initial=0.0,
                        op0=ALU.add, op1=ALU.mult), reads=[hgQ[qr], constB], writes=[hgQ[qr]]))
                    chain.append(lambda qr: S.op("dve", lambda e: e.tensor_tensor(
                        out=Tq[:, qr, :], in0=Q3(Ib, qr)[:, :, 63:64].rearrange("p c o -> p (c o)"),
                        in1=Q3(Gb, qr)[:, :, 63:64].rearrange("p c o -> p (c o)"), op=ALU.add),
                        reads=[hgQ[qr]], writes=[TqB]))
                    chain.append(lambda qr: S.op("act", lambda e: e.activation(
                        out=expT[d][:, qr * 8:(qr + 1) * 8], in_=Tq[:, qr, :], func=AF.Exp),
                        reads=[TqB], writes=[dirQ[d][qr]]))
                    chain.append(lambda qr: S.op("dve", lambda e: e.tensor_tensor(
                        out=Q3(Ab, qr), in0=Tq[:, qr, :].rearrange("p (c o) -> p c o", o=1).to_broadcast([128, 8, 64]),
                        in1=Q3(Ib, qr), op=ALU.subtract), reads=[hgQ[qr], TqB], writes=[hgQ[qr]]))
                    X_, U_ = Ib, Ab
                chain.append(lambda qr: S.op("act", lambda e: e.activation(out=Q(X_, qr), in_=Q(X_, qr), func=AF.Exp),
                                             reads=[hgQ[qr]], writes=[hgQ[qr]]))
                chain.append(lambda qr: S.op("pool", lambda e: e.tensor_tensor(
                    out=Q(kh[d], qr), in0=Q(Fb, qr), in1=Q(X_, qr), op=ALU.mult),
                    reads=[hgQ[qr]], writes=[dirQ[d][qr]]))
                chain.append(lambda qr: S.op("act", lambda e: e.activation(out=Q(X_, qr), in_=Q(U_, qr), func=AF.Exp),
                                             reads=[hgQ[qr], dirQ[d][qr]], writes=[hgQ[qr]]))
                chain.append(lambda qr: S.op("dve", lambda e: e.tensor_tensor(
                    out=Q(qt[d], qr), in0=Q(q32, qr), in1=Q(X_, qr), op=ALU.mult),
                    reads=[hgQ[qr], q32Q[qr]], writes=[dirQ[d][qr]]))
                chain.append(lambda qr: S.op("act", lambda e: e.activation(out=Q(X_, qr), in_=Q(U_, qr), func=AF.Exp,
                                                                           scale=-1.0),
                                             reads=[hgQ[qr], dirQ[d][qr]], writes=[hgQ[qr]]))
                chain.append(lambda qr: S.op("pool", lambda e: e.tensor_tensor(
                    out=Q(kt[d], qr), in0=Q(Fb, qr), in1=Q(X_, qr), op=ALU.mult),
                    reads=[hgQ[qr]], writes=[dirQ[d][qr]]))
                def chain_gen():
                    for opf in chain:
                        for qr in range(4):
                            opf(qr)
                            yield

                if d == 0:
                    bgc = BG()
                    bgc.add(chain_gen())
                    vog_emit(bgc)
                    bgc.drain()
                else:
                    for _ in chain_gen():
                        pass
            spp = [0]
            S.op("pool", lambda e: e.memset(S32p[0], 0.0), writes=[S32B[0]])
            S.op("pool", lambda e: e.memset(Sbf2[0], 0.0), writes=[SbB[0]])

            def tile_of(d, step):
                return step if d == 0 else NT - 1 - step

            fr = {}

            def F1_emit(step):
                r = rot["k"] % 3
                rot["k"] += 1
                xy = step % 2
                z = [2 + (step % 2) * 2, 3 + (step % 2) * 2]
                tiles = [tile_of(d, step) for d in range(2)]
                dqs = [dirQ[d][tiles[d] // 4] for d in range(2)]
                tsl = [slice(i * 128, (i + 1) * 128) for i in tiles]
                S.op("pe", lambda e: [e.transpose(out=bank(xy)[:, d * 128:(d + 1) * 128], in_=kh[d][:, tsl[d]],
                                                  identity=ident) for d in range(2)] +
                                     [e.matmul(out=bank(xy)[:, 256 + d * 128:256 + (d + 1) * 128], lhsT=kt[d][:, tsl[d]],
                                               rhs=qt[d][:, tsl[d]], start=True, stop=True) for d in range(2)],
                     reads=dqs + [constB], writes=[bankB[xy]])
                S.op("act", lambda e: e.activation(out=kA[r], in_=bank(xy), func=AF.Copy),
                     reads=[bankB[xy]], writes=[khtokB[r]])
                S.op("pool", lambda e: e.tensor_tensor(out=At2[r], in0=kA[r][:, 256:512],
                                                       in1=hgm.rearrange("p a n -> p (a n)"), op=ALU.mult),
                     reads=[khtokB[r], constB], writes=[AtB[r]])
                fr[step] = (r, z)

            def F2_emit(step):
                r, z = fr[step]
                tiles = [tile_of(d, step) for d in range(2)]
                S.op("pe", lambda e: [e.matmul(out=bank(z[sub])[:, d * 128:(d + 1) * 128],
                                               lhsT=kA[r][sub * 64:(sub + 1) * 64, d * 128:(d + 1) * 128],
                                               rhs=vh[sub * 64:(sub + 1) * 64, tiles[d], :], start=True, stop=True)
                                      for sub in range(2) for d in range(2)],
                     reads=[khtokB[r], voB], writes=[bankB[z[0]], bankB[z[1]]])

            def B_emit(step):
                r, z = fr[step]
                obs = {}
                for d in range(2):
                    i = tile_of(d, step)
                    ob = 6 + d
                    obs[d] = ob
                    S.op("pe", lambda e: e.matmul(out=bank(ob)[:, 0:128], lhsT=At2[r][:, d * 128:(d + 1) * 128],
                                                  rhs=vh[:, i, :], start=True, stop=False),
                         reads=[AtB[r], voB], writes=[bankB[ob]])
                for si_ in range(2):
                    if si_ == 1 and step + 1 < NT:
                        F2_emit(step + 1)
                    cur = spp[0]
                    nxt = 1 - cur
                    for d in range(2):
                        i = tile_of(d, step)
                        ob = obs[d]
                        sub = si_ if d == 0 else 1 - si_
                        c = 2 * i + sub
                        ssl = slice(i * 128 + sub * 64, i * 128 + sub * 64 + 64)
                        dq = dirQ[d][i // 4]
                        S.op("pe", lambda e: e.matmul(out=bank(ob)[sub * 64:(sub + 1) * 64, 0:128],
                                                      lhsT=qt[d][:, ssl], rhs=Sbf2[cur][:, d, :],
                                                      start=False, stop=True),
                             reads=[dq, SbB[cur]], writes=[bankB[ob]])
                        kv = bank(z[sub])[:, d * 128:(d + 1) * 128]
                        S.op("dve", lambda e: e.scalar_tensor_tensor(out=S32p[nxt][:, d, :], in0=S32p[cur][:, d, :],
                                                                     scalar=expT[d][:, c:c + 1], in1=kv,
                                                                     op0=ALU.mult, op1=ALU.add),
                             reads=[bankB[z[sub]], dq, S32B[cur]], writes=[S32B[nxt]])
                    S.op("act", lambda e: e.activation(out=Sbf2[nxt], in_=S32p[nxt], func=AF.Copy),
                         reads=[S32B[nxt]], writes=[SbB[nxt]])
                    spp[0] = nxt
                for d in range(2):
                    i = tile_of(d, step)
                    ob = obs[d]
                    if step < NT // 2:
                        S.op("act", lambda e: e.activation(out=oacc[:, i, :], in_=bank(ob)[:, 0:128], func=AF.Copy),
                             reads=[bankB[ob]], writes=[oaccB])
                    else:
                        S.op("dve", lambda e: e.tensor_tensor(out=oacc[:, i, :], in0=bank(ob)[:, 0:128],
                                                              in1=oacc[:, i, :], op=ALU.add),
                             reads=[bankB[ob], oaccB], writes=[oaccB])

            F1_emit(0)
            F2_emit(0)
            for step in range(NT):
                if step + 1 < NT:
                    F1_emit(step + 1)
                B_emit(step)
            pend_epi = epi_gen(h)
        for _ in pend_epi:
            pass
        if dbg == "y_hgT":
            dump([y_hgT[:, kc, 0:T] for kc in range(4)])

        S.barrier()
        ph.top = ph_mark
        memst = [ph.alloc([128, D], F32) for _ in range(2)]
        memT = ph.alloc([128, 8, 256], BF16)
        kmT = ph.alloc([128, 4, 256], BF16)
        vma = ph.alloc([128, 2, 4, 130], BF16)
        qmT = ph.alloc([128, 4, T], BF16)
        pm = [ph.alloc([128, 8, 512], BF16) for _ in range(2)]
        ystm = [ph.alloc([128, 512], F32) for _ in range(2)]
        ysraw = [ph.alloc([128, 4, 130], F32) for _ in range(2)]
        ysrawB = [Buf("ysraw0"), Buf("ysraw1")]
        memB = Buf("mem")
        assert ph.top <= TOP - 9 * KB
        wmg0 = A.at(TOP - 9 * KB, [128, 8, 384], BF16)
        wmb0 = A.at(TOP - 3 * KB, [128, 4, 384], BF16)
        wmB = [Buf("wm0"), Buf("wm1")]
        wmD = [S.dsem("wm0"), S.dsem("wm1")]
        load_w(wmg0, w_mg_d[0], wmB[0], wmD[0])
        load_w(wmb0, w_mb_d[0], wmB[0], wmD[0])
        mD = [S.dsem("mem0"), S.dsem("mem1")]
        pmB = [Buf("pm0"), Buf("pm1")]
        ystmB = [Buf("ystm0"), Buf("ystm1")]
        ws = wi["n"] % 2
        wi["n"] += 1
        Wk = wbuf[ws][:, :, 0:512]
        assert ws == 1
        ws2 = wi["n"] % 2
        wi["n"] += 1
        Wv = wbuf[ws2][:, :, 0:512]
        assert ws2 == 0
        for mc in range(2):
            S.dma("sp", out=memst[mc], in_=mem_d[mc * 128:(mc + 1) * 128, :], dsem=mD[mc], writes=[memB])
        S.op("pool", lambda e: e.memset(vma[:, :, :, 128:130], 1.0), writes=[memB])
        for mc in range(2):
            for half in range(2):
                srcs = [memst[mc][:, (half * 4 + q) * 128:(half * 4 + q + 1) * 128] for q in range(4)]
                transpose_tile_to(srcs, memB,
                                  lambda bap: (memT[:, half * 4:half * 4 + 4, mc * 128:(mc + 1) * 128],
                                               bap.rearrange("p (a b) -> p a b", a=4)), memB)
        for hd in range(4):
            bk = next_bank()
            S.op("pe", lambda e: [e.matmul(out=bank(bk)[:, 0:256], lhsT=Wk[:, kc, hd * 128:(hd + 1) * 128],
                                           rhs=memT[:, kc, :], start=(kc == 0), stop=(kc == 7)) for kc in range(8)],
                 reads=[wB[ws], memB], writes=[bankB[bk]])
            S.op("act", lambda e: e.activation(out=kmT[:, hd, :], in_=bank(bk)[:, 0:256], func=AF.Copy),
                 reads=[bankB[bk]], writes=[memB])
        for mc in range(2):
            bk = next_bank()
            S.op("pe", lambda e: [e.matmul(out=bank(bk), lhsT=memT[:, kc, mc * 128:(mc + 1) * 128],
                                           rhs=Wv[:, kc, :], start=(kc == 0), stop=(kc == 7)) for kc in range(8)],
                 reads=[wB[ws2], memB], writes=[bankB[bk]])
            S.op("act", lambda e: e.activation(out=vma[:, mc, :, 0:128],
                                               in_=bank(bk).rearrange("p (h d) -> p h d", h=4), func=AF.Copy),
                 reads=[bankB[bk]], writes=[memB])
        ws = wi["n"] % 2
        wi["n"] += 1
        Wq = wbuf[ws][:, :, 0:512]
        load_w(Wq, w_mq_d, wB[ws], wD[ws])
        for hd in range(4):
            proj_fm(Wq, wB[ws], hd * 128, lambda tb: qmT[:, hd, tb * 512:(tb + 1) * 512], memB)
        mscale = 128.0 ** -0.5
        for tb in range(4):
            ps_ = tb % 2
            for hd in range(4):
                for mc in range(2):
                    bk = next_bank(0, 4)
                    S.op("pe", lambda e: e.matmul(out=bank(bk), lhsT=kmT[:, hd, mc * 128:(mc + 1) * 128],
                                                  rhs=qmT[:, hd, tb * 512:(tb + 1) * 512], start=True, stop=True),
                         reads=[memB], writes=[bankB[bk]])
                    S.op("act", lambda e: e.activation(out=pm[ps_][:, hd * 2 + mc, :], in_=bank(bk), func=AF.Exp,
                                                       scale=mscale), reads=[bankB[bk]], writes=[pmB[ps_]])
            def mem_unit(i):
                q = i % 4
                ys = i % 2
                obs_ = [next_bank(4, 8), next_bank(4, 8)]
                for pr in range(2):
                    S.op("pe", lambda e: [e.matmul(out=bank(obs_[pr])[:, hh * 130:(hh + 1) * 130],
                                                   lhsT=pm[ps_][:, (pr * 2 + hh) * 2 + mc, q * 128:(q + 1) * 128],
                                                   rhs=vma[:, mc, pr * 2 + hh, :], start=(mc == 0), stop=(mc == 1))
                                          for hh in range(2) for mc in range(2)],
                         reads=[pmB[ps_], memB], writes=[bankB[obs_[pr]]])
                    yield
                    S.op("act", lambda e: e.activation(out=ysraw[ys][:, pr * 2:pr * 2 + 2, :],
                                                       in_=bank(obs_[pr])[:, 0:260].rearrange("p (h d) -> p h d", h=2),
                                                       func=AF.Copy), reads=[bankB[obs_[pr]]], writes=[ysrawB[ys]])
                    yield
                rc = rec_s[ys]
                S.op("dve", lambda e: e.reciprocal(out=rc[:, 0:4].rearrange("p (h o) -> p h o", o=1),
                                                   in_=ysraw[ys][:, :, 128:129]),
                     reads=[ysrawB[ys]], writes=[ystmB[ys]])
                yield
                S.op("dve", lambda e: e.tensor_tensor(out=ystm[ys].rearrange("p (h d) -> p h d", h=4),
                                                      in0=ysraw[ys][:, :, 0:128],
                                                      in1=rc[:, 0:4].rearrange("p (h o) -> p h o", o=1)
                                                      .to_broadcast([128, 4, 128]), op=ALU.mult),
                     reads=[ysrawB[ys], ystmB[ys]], writes=[ystmB[ys]])
                yield
                srcs = [ystm[ys][:, k * 128:(k + 1) * 128] for k in range(4)]
                yield from transpose_gen(srcs, ystmB[ys],
                                         lambda bap: (y_memT[:, 0:4, i * 128:(i + 1) * 128],
                                                      bap.rearrange("p (a b) -> p a b", a=4)), yTB)

            run_streams([mem_unit(tb * 4 + q) for q in range(4)], 2, skew=2)
        if dbg == "y_memT":
            dump([y_memT[:, kc, 0:T] for kc in range(4)])

        S.barrier()
        mergedT = A.at(RC, [128, 8, T], BF16)
        mgB = Buf("mergedT")
        p3 = Bump(A, RD, TOP)
        wmg = [wmg0, p3.alloc([128, 8, 384], BF16)]
        wmb = [wmb0, p3.alloc([128, 4, 384], BF16)]
        sg = [p3.alloc([128, 512], F32) for _ in range(2)]
        sgB = [Buf("sg0"), Buf("sg1")]
        macc = [p3.alloc([128, 512], F32) for _ in range(2)]
        maccB = [Buf("macc0"), Buf("macc1")]
        tmpm = [p3.alloc([128, 512], F32) for _ in range(2)]
        tmpmB = [Buf("tmpm0"), Buf("tmpm1")]
        yTs = [y_naT, y_hgT, y_memT]
        cnt = {"sg": 0, "ma": 0}
        def mg_load(j):
            ws_ = j % 2
            load_w(wmg[ws_], w_mg_d[j], wmB[ws_], wmD[ws_])
            load_w(wmb[ws_], w_mb_d[j], wmB[ws_], wmD[ws_])

        for j in range(8):
            ws = j % 2
            if j + 1 < 8:
                mg_load(j + 1)
            for tb in range(4):
                ma = cnt["ma"] % 2
                cnt["ma"] += 1
                tsl = slice(tb * 512, (tb + 1) * 512)
                for b in range(3):
                    gb_ = next_bank()
                    S.op("pe", lambda e: [e.matmul(out=bank(gb_), lhsT=wmg[ws][:, kc, b * 128:(b + 1) * 128],
                                                   rhs=x0T[:, kc, tsl], start=(kc == 0), stop=(kc == 7))
                                          for kc in range(8)], reads=[wmB[ws], x0TB], writes=[bankB[gb_]])
                    pb_ = next_bank()
                    S.op("pe", lambda e: [e.matmul(out=bank(pb_), lhsT=wmb[ws][:, kc, b * 128:(b + 1) * 128],
                                                   rhs=yTs[b][:, kc, tsl], start=(kc == 0), stop=(kc == 3))
                                          for kc in range(4)], reads=[wmB[ws], yTB], writes=[bankB[pb_]])
                    s_ = cnt["sg"] % 2
                    cnt["sg"] += 1
                    S.op("act", lambda e: e.activation(out=sg[s_], in_=bank(gb_), func=AF.Sigmoid),
                         reads=[bankB[gb_]], writes=[sgB[s_]])
                    if b == 0:
                        S.op("dve", lambda e: e.tensor_tensor(out=macc[ma], in0=bank(pb_), in1=sg[s_], op=ALU.mult),
                             reads=[bankB[pb_], sgB[s_]], writes=[maccB[ma]])
                    else:
                        S.op("dve", lambda e: e.tensor_tensor(out=tmpm[ma], in0=bank(pb_), in1=sg[s_], op=ALU.mult),
                             reads=[bankB[pb_], sgB[s_]], writes=[tmpmB[ma]])
                        if b == 1:
                            S.op("pool", lambda e: e.tensor_tensor(out=macc[ma], in0=macc[ma], in1=tmpm[ma], op=ALU.add),
                                 reads=[maccB[ma], tmpmB[ma]], writes=[maccB[ma]])
                        else:
                            S.op("pool", lambda e: e.tensor_tensor(out=mergedT[:, j, tsl], in0=macc[ma], in1=tmpm[ma],
                                                                   op=ALU.add),
                                 reads=[maccB[ma], tmpmB[ma]], writes=[mgB])
        if dbg == "mergedT":
            dump([mergedT[:, kc, 0:1024] for kc in range(8)])

        S.barrier()
        x1 = A.at(RD, [128, NT, D], F32)
        x1B = [Buf(f"x1_{i}") for i in range(NT)]
        x1T = A.at(RA, [128, 8, T], BF16)
        x1TB = Buf("x1T")
        pb3 = Bump(A, RB, RC)
        wout = pb3.alloc([128, 8, D], BF16)
        woB = Buf("wout")
        woD = S.dsem("wout")
        for hf in range(2):
            S.dma("pool", out=wout[:, :, hf * 512:(hf + 1) * 512],
                  in_=w_out_d[:, hf * 512:(hf + 1) * 512].rearrange("(c p) n -> p c n", p=128), dsem=woD, writes=[woB])
        g1 = pb3.alloc([128, D], F32)
        b1 = pb3.alloc([128, D], F32)
        g1B = Buf("g1")
        g1D = S.dsem("g1")
        S.dma("sp", out=g1, in_=lnp_d[2:3, :].to_broadcast([128, D]), dsem=g1D, writes=[g1B])
        S.dma("sp", out=b1, in_=lnp_d[3:4, :].to_broadcast([128, D]), dsem=g1D, writes=[g1B])
        pf = Bump(A, RB, RD)
        hT = [pf.alloc([128, 4, T], BF16) for _ in range(2)]
        hTB = [Buf("hT0"), Buf("hT1")]
        w1 = [None, None]
        w2 = [None, None]
        for k_ in range(2):
            w1[k_] = pf.alloc([128, 8, 512], BF16)
            w2[k_] = pf.alloc([128, 4, D], BF16)
        w1B = [Buf("w1_0"), Buf("w1_1")]
        w2B = [Buf("w2_0"), Buf("w2_1")]
        w1D = [S.dsem("w1_0"), S.dsem("w1_1")]
        w2D = [S.dsem("w2_0"), S.dsem("w2_1")]
        pf_mark = pf.top
        assert pb3.top <= RB + 28 * KB

        def ffn_load(g):
            s_ = g % 2
            load_w(w1[s_], w_ff1_d[:, g * 512:(g + 1) * 512], w1B[s_], w1D[s_])
            load_w(w2[s_], w_ff2_d[g * 512:(g + 1) * 512, :], w2B[s_], w2D[s_])

        hl0 = pb3.alloc([128, 2, D], BF16)
        hlB = Buf("hl")
        S.op("act", lambda e: e.activation(out=hl0[0:1, 0, :], in_=bbuf[0:1, :], func=AF.Copy, scale=ALPHA),
             reads=[gbB], writes=[hlB])
        S.op("dve", lambda e: e.scalar_tensor_tensor(out=hl0[0:1, 1, :], in0=bbuf[0:1, :], scalar=ALPHA,
                                                     in1=hl0[0:1, 0, :], op0=ALU.mult, op1=ALU.subtract),
             reads=[gbB, hlB], writes=[hlB])
        S.op("act", lambda e: e.activation(out=gbuf, in_=gbuf, func=AF.Copy, scale=ALPHA), reads=[gbB], writes=[gbB])
        S.op("act", lambda e: e.activation(out=g1, in_=g1, func=AF.Copy, scale=ALPHA), reads=[g1B], writes=[g1B])
        x1D = [S.dsem(f"x1d{i}") for i in range(NT)]
        ffn_load(0)

        def s3a_stream(i):
            sa = i % 4
            X = x1[:, i, :]
            S.dma("sp", out=X, in_=x_d[i * 128:(i + 1) * 128, :], dsem=x1D[i], writes=[x1B[i]],
                  reads=([x1B[i - 3]] if i >= 3 else [woB]))
            yield
            S.op("act", lambda e: e.activation(out=X, in_=X, func=AF.Identity, bias=lnst[:, i, 1:2],
                                               scale=lnst[:, i, 0:1]), reads=[x1B[i], lnstB], writes=[x1B[i]])
            yield
            S.op("pool", lambda e: e.tensor_tensor(out=X, in0=X, in1=gbuf, op=ALU.mult),
                 reads=[x1B[i], gbB], writes=[x1B[i]])
            yield
            for hf in range(2):
                bk = sa
                hsl = slice(hf * 512, (hf + 1) * 512)
                S.op("pe", lambda e: [e.matmul(out=bank(bk), lhsT=mergedT[:, kc, i * 128:(i + 1) * 128],
                                               rhs=wout[:, kc, hsl], start=(kc == 0), stop=False)
                                      for kc in range(8)] +
                                     [e.matmul(out=bank(bk), lhsT=ones_bf[0:1, :], rhs=hl0[0:1, r_, hsl],
                                               start=False, stop=(r_ == 1)) for r_ in range(2)],
                     reads=[mgB, woB, hlB, constB], writes=[bankB[bk]])
                yield
                S.op("dve", lambda e: e.tensor_tensor(out=X[:, hsl], in0=bank(bk), in1=X[:, hsl], op=ALU.add),
                     reads=[bankB[bk], x1B[i]], writes=[x1B[i]])
                yield

        def s3b_stream(i):
            sb = i % 4
            X = x1[:, i, :]
            yield from ln_gen(X, x1B[i], X, x1B[i], None, None, None, affine=False)
            for half in range(2):
                bk = 4 + sb
                S.op("pe", lambda e: [e.transpose(out=bank(bk)[:, q * 128:(q + 1) * 128],
                                                  in_=X[:, (half * 4 + q) * 128:(half * 4 + q + 1) * 128],
                                                  identity=ident) for q in range(4)],
                     reads=[x1B[i], constB], writes=[bankB[bk]])
                yield
                for q in range(4):
                    kc = half * 4 + q
                    if q % 2 == 0:
                        S.op("act", lambda e: e.activation(out=x1T[:, kc, i * 128:(i + 1) * 128],
                                                           in_=bank(bk)[:, q * 128:(q + 1) * 128], func=AF.Identity,
                                                           scale=lnpT[:, 16 + kc:17 + kc], bias=lnpT[:, 24 + kc:25 + kc]),
                             reads=[bankB[bk], constB], writes=[x1TB])
                    else:
                        S.op("dve", lambda e: e.tensor_scalar(out=x1T[:, kc, i * 128:(i + 1) * 128],
                                                              in0=bank(bk)[:, q * 128:(q + 1) * 128],
                                                              scalar1=lnpT[:, 16 + kc:17 + kc],
                                                              scalar2=lnpT[:, 24 + kc:25 + kc],
                                                              op0=ALU.mult, op1=ALU.add),
                             reads=[bankB[bk], constB], writes=[x1TB])
                    yield
            S.op("pool", lambda e: e.tensor_tensor(out=X, in0=X, in1=g1, op=ALU.mult),
                 reads=[x1B[i], g1B], writes=[x1B[i]])
            yield

        ga = [s3a_stream(i) for i in range(NT)]
        gb3 = [s3b_stream(i) for i in range(NT)]
        a_act, b_act = [], []
        na_, nb_, adone = 0, 0, 0
        tick = 0
        while na_ < NT or nb_ < NT or a_act or b_act:
            if na_ < NT and len(a_act) < 4 and tick % 2 == 0:
                a_act.append((na_, ga[na_]))
                na_ += 1
            if nb_ < NT and len(b_act) < 4 and nb_ < adone and tick % 2 == 1:
                b_act.append((nb_, gb3[nb_]))
                nb_ += 1
            for item in list(a_act):
                try:
                    next(item[1])
                except StopIteration:
                    a_act.remove(item)
                    adone += 1
            for item in list(b_act):
                try:
                    next(item[1])
                except StopIteration:
                    b_act.remove(item)
            tick += 1
        if dbg == "x1":
            dump([x1[:, i, :] for i in range(8)], stage_off=RB)

        S.barrier()
        pf.top = pf_mark
        rl = [pf.alloc([128, 512], F32) for _ in range(2)]
        rlB = [Buf("rl0"), Buf("rl1")]
        g2 = pf.alloc([128, D], F32)
        b2 = pf.alloc([128, D], F32)
        g2B = Buf("g2")
        g2D = S.dsem("g2")
        hl1 = pf.alloc([128, 2, D], BF16)
        hl1B = Buf("hl1")
        brow = A.at(RB + 64 * KB, [128, D], F32)
        browD = S.dsem("brow")
        S.dma("sp", out=brow[0:1, :], in_=lnp_d[3:4, :], dsem=browD, writes=[rlB[0], rlB[1]])
        S.op("act", lambda e: e.activation(out=hl1[0:1, 0, :], in_=brow[0:1, :], func=AF.Copy, scale=ALPHA),
             reads=[rlB[0], rlB[1]], writes=[hl1B])
        S.op("dve", lambda e: e.scalar_tensor_tensor(out=hl1[0:1, 1, :], in0=brow[0:1, :], scalar=ALPHA,
                                                     in1=hl1[0:1, 0, :], op0=ALU.mult, op1=ALU.subtract),
             reads=[rlB[0], rlB[1], hl1B], writes=[hl1B, rlB[0], rlB[1]])
        S.dma("sp", out=g2, in_=lnp_d[4:5, :].to_broadcast([128, D]), dsem=g2D, writes=[g2B])
        S.dma("sp", out=b2, in_=lnp_d[5:6, :].to_broadcast([128, D]), dsem=g2D, writes=[g2B])
        yo = [A.at(RD + 64 * KB, [128, D], F32), A.at(RD + 68 * KB, [128, D], F32)]
        yoB = [Buf("yo0"), Buf("yo1")]
        yoD = [S.dsem("yo0"), S.dsem("yo1")]
        NG = 8
        rcnt = {"n": 0}

        def ffn1(g):
            s = g % 2
            for fb in range(4):
                for tb in range(4):
                    bk = next_bank()
                    S.op("pe", lambda e: [e.matmul(out=bank(bk), lhsT=w1[s][:, kc, fb * 128:(fb + 1) * 128],
                                                   rhs=x1T[:, kc, tb * 512:(tb + 1) * 512],
                                                   start=(kc == 0), stop=(kc == 7)) for kc in range(8)],
                         reads=[w1B[s], x1TB], writes=[bankB[bk]])
                    r_ = rcnt["n"] % 2
                    rcnt["n"] += 1
                    S.op("act", lambda e: e.activation(out=rl[r_], in_=bank(bk), func=AF.Relu),
                         reads=[bankB[bk]], writes=[rlB[r_]])
                    S.op("dve", lambda e: e.tensor_tensor(out=hT[s][:, fb, tb * 512:(tb + 1) * 512], in0=bank(bk),
                                                          in1=rl[r_], op=ALU.mult),
                         reads=[bankB[bk], rlB[r_]], writes=[hTB[s]])

        def ffn2(g):
            s = g % 2
            for i in range(NT):
                for hf in range(2):
                    bk = next_bank()
                    extra = 2 if g == 0 else 0
                    S.op("pe", lambda e: [e.matmul(out=bank(bk), lhsT=hT[s][:, fb, i * 128:(i + 1) * 128],
                                                   rhs=w2[s][:, fb, hf * 512:(hf + 1) * 512],
                                                   start=(fb == 0), stop=(fb == 3 and extra == 0)) for fb in range(4)] +
                                         [e.matmul(out=bank(bk), lhsT=ones_bf[0:1, :],
                                                   rhs=hl1[0:1, r_, hf * 512:(hf + 1) * 512],
                                                   start=False, stop=(r_ == 1)) for r_ in range(extra)],
                         reads=[hTB[s], w2B[s], hl1B, constB], writes=[bankB[bk]])
                    xa = x1[:, i, hf * 512:(hf + 1) * 512]
                    S.op("dve", lambda e: e.tensor_tensor(out=xa, in0=bank(bk), in1=xa, op=ALU.add),
                         reads=[bankB[bk], x1B[i]], writes=[x1B[i]])
                if g == NG - 1:
                    bg.add(tail_stream(i))
                    bg.step(2)

        bg = BG()

        def tail_stream(i):
            yield from ln_gen(x1[:, i, :], x1B[i], x1[:, i, :], x1B[i], g2, b2, g2B)
            S.dma("sp", out=y_d[i * 128:(i + 1) * 128, :], in_=x1[:, i, :], dsem=yoD[i % 2], reads=[x1B[i]])
            yield

        def ffn2_last_pair():
            for i in range(NT):
                for hf in range(2):
                    bk = next_bank()
                    S.op("pe", lambda e: [e.matmul(out=bank(bk), lhsT=hT[gg % 2][:, fb, i * 128:(i + 1) * 128],
                                                   rhs=w2[gg % 2][:, fb, hf * 512:(hf + 1) * 512],
                                                   start=(gg == NG - 2 and fb == 0), stop=(gg == NG - 1 and fb == 3))
                                          for gg in (NG - 2, NG - 1) for fb in range(4)],
                         reads=[hTB[0], hTB[1], w2B[0], w2B[1]], writes=[bankB[bk]])
                    xa = x1[:, i, hf * 512:(hf + 1) * 512]
                    S.op("dve", lambda e: e.tensor_tensor(out=xa, in0=bank(bk), in1=xa, op=ALU.add),
                         reads=[bankB[bk], x1B[i]], writes=[x1B[i]])
                bg.add(tail_stream(i))
                bg.step(2)

        ffn1(0)
        for g in range(NG - 2):
            ffn_load(g + 1)
            ffn1(g + 1)
            ffn2(g)
        ffn_load(NG - 1)
        ffn1(NG - 1)
        ffn2_last_pair()
        bg.drain()
        S.barrier()
      except _StopBuild:
        pass
    return nc


_PROG = {}


def _prep_shared(inp):
    f = lambda a: np.ascontiguousarray(np.asarray(a, dtype=np.float32))
    w_in = f(inp["w_in"])[0]
    sh = {}
    sh["lnp"] = f(np.stack([inp["ln_emb_g"], inp["ln_emb_b"], inp["ln1_g"][0], inp["ln1_b"][0],
                            inp["ln2_g"][0], inp["ln2_b"][0]], axis=0))
    sh["w_na"] = f(w_in[:, 0:1536])
    hg = []
    for h in range(4):
        cs = lambda base: w_in[:, base + h * 128: base + (h + 1) * 128]
        hg.append(np.concatenate([cs(1536), cs(3072), cs(3584), cs(2048), cs(2560)], axis=1))
    sh["w_hg"] = f(np.stack(hg, axis=0))
    sh["w_mq"] = f(w_in[:, 4096:4608])
    sh["w_mkv"] = f(inp["w_mem_kv"][0])
    wbr = [inp["w_branch_na"][0], inp["w_branch_hg"][0], inp["w_branch_mem"][0]]
    mg, mb = [], []
    for j in range(8):
        mg.append(np.concatenate([w_in[:, 4608 + b * 1024 + j * 128: 4608 + b * 1024 + (j + 1) * 128]
                                  for b in range(3)], axis=1))
        mb.append(np.concatenate([np.asarray(wbr[b])[:, j * 128:(j + 1) * 128] for b in range(3)], axis=1))
    sh["w_mg"] = f(np.stack(mg, axis=0))
    sh["w_mb"] = f(np.stack(mb, axis=0))
    sh["w_out"] = f(inp["w_out"][0])
    sh["w_ff1"] = f(inp["w_ff1"][0])
    sh["w_ff2"] = f(inp["w_ff2"][0])
    sh["ident"] = np.eye(128, dtype=np.float32)
    mf, mb_ = _hg_masks()
    sh["hgm"] = f(np.stack([mf, mb_], axis=0))
    m = np.ones((1, T), np.float32)
    m[0, ::64] = 0.0
    sh["scanm"] = m
    bias, mask = _na_tables(f(inp["na_rpb"])[0])
    sh["nab"] = f(bias)
    sh["nam"] = f(mask)
    lbl = f(inp["hg_lb_logits"]).reshape(2, 2, 4, 128).transpose(3, 0, 1, 2).reshape(128, 16)
    sh["lbl"] = f(lbl)
    sh["hgn"] = f(inp["hg_norm_g"]).reshape(1, 512)
    sh["lnpT"] = f(sh["lnp"].reshape(6, 8, 128).transpose(2, 0, 1).reshape(128, 48))
    return sh


def kernel(**inputs):
    x = np.asarray(inputs["x"], dtype=np.float32)
    mem = np.asarray(inputs["mem"], dtype=np.float32)
    nb = x.shape[0]
    sh = _prep_shared(inputs)
    if "nc" not in _PROG:
        _PROG["nc"] = build_program()
    nc = _PROG["nc"]
    in_maps = []
    for b in range(nb):
        m = dict(sh)
        m["x"] = np.ascontiguousarray(x[b])
        m["mem"] = np.ascontiguousarray(mem[b])
        in_maps.append(m)
    res = run_bass_kernel_spmd(nc, in_maps, core_ids=list(range(nb)))
    out = np.stack([np.asarray(r["y"], dtype=np.float32) for r in res.results], axis=0)
    return out
```

```python
import numpy as np
from contextlib import ExitStack
import concourse.bass as bass
import concourse.mybir as mybir
from concourse.bass_utils import run_bass_kernel_spmd

F32 = mybir.dt.float32
BF16 = mybir.dt.bfloat16
AF = mybir.ActivationFunctionType
ALU = mybir.AluOpType

T = 2048
D = 1024
NT = 16
ALPHA = 2.0 ** 0.25
LN_EPS = 1e-5
RMS_EPS = 1e-6
NEG = -30000.0


class Buf:
    __slots__ = ("w", "r", "name")

    def __init__(self, name=""):
        self.w = {}
        self.r = {}
        self.name = name


class _Eng:
    def __init__(self, e, sem, name):
        self.e = e
        self.sem = sem
        self.cnt = 0
        self.waited = {}
        self.name = name


class _DSem:
    def __init__(self, sem):
        self.sem = sem
        self.cnt = 0


class Sched:
    def __init__(self, nc, stack):
        self.nc = nc
        self.stack = stack
        self.E = {}
        for name, e in (("pe", nc.tensor), ("act", nc.scalar), ("dve", nc.vector),
                        ("pool", nc.gpsimd), ("sp", nc.sync)):
            sem = stack.enter_context(nc.semaphore("s_" + name))
            self.E[name] = _Eng(e, sem, name)
        self.dsems = []
        self.sems = {}
        for E in self.E.values():
            self.sems[id(E.sem)] = E.sem

    def dsem(self, name):
        sem = self.stack.enter_context(self.nc.semaphore("d_" + name))
        d = _DSem(sem)
        self.dsems.append(d)
        self.sems[id(sem)] = sem
        return d

    def _need(self, E, deps):
        need = {}
        for sid, val in deps:
            if E.name == "pe" and sid == id(E.sem):
                continue
            if E.waited.get(sid, 0) >= val:
                continue
            if need.get(sid, 0) < val:
                need[sid] = val
        return list(need.items())

    @staticmethod
    def _deps(reads, writes):
        deps = []
        for b in reads:
            deps += list(b.w.items())
        for b in writes:
            deps += list(b.w.items())
            deps += list(b.r.items())
        return deps

    @staticmethod
    def _record(tok, reads, writes):
        sid, val = tok
        for b in reads:
            if b.r.get(sid, 0) < val:
                b.r[sid] = val
        for b in writes:
            if b.w.get(sid, 0) < val:
                b.w[sid] = val

    def op(self, eng, fn, reads=(), writes=()):
        E = self.E[eng]
        need = self._need(E, self._deps(reads, writes))
        for sid, val in need[:-1]:
            E.e.wait_ge(self.sems[sid], val)
        res = fn(E.e)
        insts = list(res) if isinstance(res, (list, tuple)) else [res]
        if need:
            sid, val = need[-1]
            insts[0]._wait_ge(self.sems[sid], val)
        for sid, val in need:
            E.waited[sid] = val
        E.cnt += 1
        insts[-1].then_inc(E.sem, 1)
        tok = (id(E.sem), E.cnt)
        self._record(tok, reads, writes)
        return tok

    def dma(self, q, out, in_, dsem, reads=(), writes=(), **kw):
        E = self.E[q]
        need = self._need(E, self._deps(reads, writes))
        for sid, val in need:
            E.e.wait_ge(self.sems[sid], val)
            E.waited[sid] = val
        ins = E.e.dma_start(out=out, in_=in_, **kw)
        dsem.cnt += 16
        ins.then_inc(dsem.sem, 16)
        tok = (id(dsem.sem), dsem.cnt)
        self._record(tok, reads, writes)
        return tok

    def barrier(self, engines=None):
        toks = [(id(E.sem), E.cnt) for E in self.E.values() if E.cnt > 0]
        toks += [(id(d.sem), d.cnt) for d in self.dsems if d.cnt > 0]
        for name, E in self.E.items():
            if engines is not None and name not in engines:
                continue
            saved = E.name
            E.name = "_"
            need = self._need(E, toks)
            E.name = saved
            for sid, val in need:
                E.e.wait_ge(self.sems[sid], val)
                E.waited[sid] = val


class _StopBuild(Exception):
    pass


class Arena:
    def __init__(self, big, cap_bytes):
        self.big = big
        self.cap = cap_bytes

    def at(self, off, shape, dt):
        n = 1
        for s in shape[1:]:
            n *= s
        esz = 4 if dt == F32 else 2
        assert off % 4 == 0
        assert off + n * esz <= self.cap, (off, shape, self.cap)
        ap = self.big[:, off // 2: off // 2 + n * esz // 2]
        if dt == F32:
            ap = ap.bitcast(F32)
        if len(shape) == 3:
            ap = ap.rearrange("p (a b) -> p a b", a=shape[1])
        elif len(shape) == 4:
            ap = ap.rearrange("p (a b c) -> p a b c", a=shape[1], b=shape[2])
        if shape[0] != 128:
            ap = ap[0:shape[0]]
        return ap


class Bump:
    def __init__(self, arena, lo, hi):
        self.a = arena
        self.lo = lo
        self.hi = hi
        self.top = lo

    def alloc(self, shape, dt):
        n = 1
        for s in shape[1:]:
            n *= s
        nbytes = n * (4 if dt == F32 else 2)
        nbytes = (nbytes + 63) // 64 * 64
        off = self.top
        assert off + nbytes <= self.hi, ("arena overflow", off, nbytes, self.hi)
        self.top += nbytes
        return self.a.at(off, shape, dt)


def _na_plan():
    rs = lambda r: int(np.clip(r - 4, 0, 24))
    plan = {}
    for j in range(16):
        lo = rs(2 * j) // 2
        hi = (rs(2 * j + 1) + 7) // 2
        plan[j] = list(range(lo, hi + 1))
    return plan


def _na_mask_tile(j, c):
    rs = lambda r: int(np.clip(r - 4, 0, 24))
    m = np.full((128, 128), NEG, np.float32)
    kc = np.arange(64)[:, None]
    qc = np.arange(64)[None, :]
    c0 = np.clip(qc - 8, 0, 48)
    colv = (kc >= c0) & (kc < c0 + 16)
    for krl in range(2):
        for qrl in range(2):
            kr = 2 * c + krl
            qr = 2 * j + qrl
            if rs(qr) <= kr <= rs(qr) + 7:
                blk = np.where(colv, 0.0, NEG).astype(np.float32)
                m[krl * 64:(krl + 1) * 64, qrl * 64:(qrl + 1) * 64] = blk
    return m


def _na_tables(rpb):
    plan = _na_plan()
    kc = np.arange(64)[:, None]
    qc = np.arange(64)[None, :]
    dc = np.clip(kc - qc + 15, 0, 30)
    bias = np.zeros((8, 128, 7 * 128), np.float32)
    for di in range(7):
        delta = di - 3
        for krl in range(2):
            for qrl in range(2):
                dr = int(np.clip(2 * delta + krl - qrl + 7, 0, 14))
                bias[:, krl * 64:(krl + 1) * 64, di * 128 + qrl * 64: di * 128 + (qrl + 1) * 64] = rpb[:, dr][:, dc]
    tiles = []
    for delta in range(-2, 3):
        t = _na_mask_tile(6, 6 + delta)
        tiles.append(t)
    for j in (0, 1, 14, 15):
        assert len(plan[j]) == 4
        for c in plan[j]:
            tiles.append(_na_mask_tile(j, c))
    mask = np.concatenate(tiles, axis=1)
    return bias, mask


def _na_offsets(j):
    plan = _na_plan()
    chunks = plan[j]
    delta0 = chunks[0] - j
    boff = (delta0 + 3) * 128
    if 2 <= j <= 13:
        assert len(chunks) == 5 and delta0 == -2
        moff = 0
    else:
        moff = 640 + {0: 0, 1: 512, 14: 1024, 15: 1536}[j]
    return chunks, boff, moff


def _hg_masks():
    s = np.arange(128)[:, None]
    t = np.arange(128)[None, :]
    same = (s // 64) == (t // 64)
    mf = (same & (s <= t)).astype(np.float32)
    mb = (same & (s >= t)).astype(np.float32)
    return mf, mb


CAP = 211968


def build_program(dbg=None):
    nc = bass.Bass("TRN2", target_bir_lowering=False)

    def din(name, shape):
        return nc.dram_tensor(name, list(shape), F32, kind="ExternalInput").ap()

    x_d = din("x", [T, D])
    mem_d = din("mem", [256, D])
    lnp_d = din("lnp", [6, D])
    w_na_d = din("w_na", [D, 1536])
    w_hg_d = din("w_hg", [4, D, 640])
    w_mq_d = din("w_mq", [D, 512])
    w_mkv_d = din("w_mkv", [D, 1024])
    w_mg_d = din("w_mg", [8, D, 384])
    w_mb_d = din("w_mb", [8, 512, 384])
    w_out_d = din("w_out", [D, D])
    w_ff1_d = din("w_ff1", [D, 4096])
    w_ff2_d = din("w_ff2", [4096, D])
    ident_d = din("ident", [128, 128])
    hgm_d = din("hgm", [2, 128, 128])
    scanm_d = din("scanm", [1, T])
    nab_d = din("nab", [8, 128, 896])
    nam_d = din("nam", [128, 2688])
    lbl_d = din("lbl", [128, 16])
    hgn_d = din("hgn", [1, 512])
    lnpT_d = din("lnpT", [128, 48])
    y_d = nc.dram_tensor("y", [T, D], F32, kind="ExternalOutput").ap()
    dbg_d = None
    if dbg is not None:
        dbg_d = nc.dram_tensor("dbg", [128, 8192], F32, kind="ExternalOutput").ap()

    with ExitStack() as stack:
      try:
        big = stack.enter_context(nc.sbuf_tensor("big", [128, CAP // 2], BF16))
        pp = [stack.enter_context(nc.psum_tensor(f"pp{i}", [128, 1024], F32)) for i in range(4)]
        S = Sched(nc, stack)
        A = Arena(big, CAP)
        KB = 1024

        def bank(k):
            return pp[k // 2][:, (k % 2) * 512:(k % 2) * 512 + 512]

        bankB = [Buf(f"bank{k}") for k in range(8)]
        rr = {"n": 0}

        def next_bank(lo=0, hi=8):
            k = lo + rr["n"] % (hi - lo)
            rr["n"] += 1
            return k

        cb = Bump(A, 0, 12 * KB)
        ident = cb.alloc([128, 128], F32)
        hgm = cb.alloc([128, 2, 128], F32)
        lbl = cb.alloc([128, 16], F32)
        lbv = cb.alloc([128, 8], F32)
        omv = cb.alloc([128, 8], F32)
        hgn = cb.alloc([128, 512], F32)
        epsln = cb.alloc([128, 1], F32)
        epsrms = cb.alloc([128, 1], F32)
        lnsm = [cb.alloc([128, 32], F32) for _ in range(6)]
        rec_s = [cb.alloc([128, 8], F32) for _ in range(4)]
        scanm = cb.alloc([128, T], BF16)
        lnpT = cb.alloc([128, 48], F32)
        identb = cb.alloc([128, 128], BF16)
        ones_bf = cb.alloc([128, 128], BF16)
        lnst = cb.alloc([128, NT, 2], F32)
        lnstB = Buf("lnst")
        constB = Buf("const")
        cD = S.dsem("const")
        gbuf = A.at(12 * KB, [128, D], F32)
        bbuf = A.at(16 * KB, [128, D], F32)
        gbB = Buf("gb")
        gbD = S.dsem("gb")
        RA = 20 * KB
        RB = 52 * KB
        RC = 100 * KB
        RD = 132 * KB
        TOP = CAP

        S.dma("sp", out=ident, in_=ident_d, dsem=cD, writes=[constB])
        S.dma("sp", out=hgm, in_=hgm_d.rearrange("a p n -> p a n"), dsem=cD, writes=[constB])
        S.dma("sp", out=lbl, in_=lbl_d, dsem=cD, writes=[constB])
        S.dma("sp", out=lnpT, in_=lnpT_d, dsem=cD, writes=[constB])
        S.dma("sp", out=hgn, in_=hgn_d.partition_broadcast(128) if False else hgn_d[0:1, :].to_broadcast([128, 512]),
              dsem=cD, writes=[constB])
        S.op("pool", lambda e: e.memset(epsln, LN_EPS), writes=[constB])
        S.op("act", lambda e: e.activation(out=identb, in_=ident, func=AF.Copy), reads=[constB], writes=[constB])
        S.op("pool", lambda e: e.memset(ones_bf, 1.0), writes=[constB])
        S.op("pool", lambda e: e.memset(epsrms, RMS_EPS), writes=[constB])
        lbl3 = lbl.rearrange("p (d s h) -> p d s h", d=2, s=2)
        lbv3 = lbv.rearrange("p (d h) -> p d h", d=2)
        omv3 = omv.rearrange("p (d h) -> p d h", d=2)
        S.op("dve", lambda e: e.tensor_tensor(out=lbv3, in0=lbl3[:, :, 0, :], in1=lbl3[:, :, 1, :], op=ALU.subtract),
             reads=[constB], writes=[constB])
        S.op("act", lambda e: e.activation(out=lbv, in_=lbv, func=AF.Sigmoid), reads=[constB], writes=[constB])
        S.op("dve", lambda e: e.tensor_scalar(out=omv, in0=lbv, scalar1=-1.0, scalar2=1.0, op0=ALU.mult, op1=ALU.add),
             reads=[constB], writes=[constB])

        def load_ln(k):
            S.dma("sp", out=gbuf, in_=lnp_d[2 * k:2 * k + 1, :].to_broadcast([128, D]), dsem=gbD, writes=[gbB])
            S.dma("sp", out=bbuf, in_=lnp_d[2 * k + 1:2 * k + 2, :].to_broadcast([128, D]), dsem=gbD, writes=[gbB])

        lnB = [Buf(f"ln{i}") for i in range(6)]
        lncnt = {"n": 0}

        def ln_gen(src, srcB, dst, dstB, gA, bA, gB_, rn=None, rnB=None, affine=True):
            slot = lncnt["n"] % 6
            lncnt["n"] += 1
            sm = lnsm[slot]
            B_ = lnB[slot]
            rstd_ap = sm[:, 15:16] if rn is None else rn[:, 0:1]
            nmr_ap = sm[:, 16:17] if rn is None else rn[:, 1:2]
            rw = [B_] if rnB is None else [B_, rnB]
            S.op("dve", lambda e: [e.bn_stats(out=sm[:, 0:6], in_=src[:, 0:512]),
                                   e.bn_stats(out=sm[:, 6:12], in_=src[:, 512:1024])], reads=[srcB], writes=[B_])
            yield
            S.op("dve", lambda e: e.bn_aggr(out=sm[:, 12:14], in_=sm[:, 0:12]), reads=[B_], writes=[B_])
            yield
            S.op("act", lambda e: e.activation(out=sm[:, 14:15], in_=sm[:, 13:14], func=AF.Sqrt, bias=epsln[:, 0:1],
                                               scale=1.0), reads=[B_, constB], writes=[B_])
            yield
            S.op("dve", lambda e: e.reciprocal(out=rstd_ap, in_=sm[:, 14:15]), reads=[B_], writes=rw)
            yield
            S.op("dve", lambda e: e.tensor_scalar(out=nmr_ap, in0=sm[:, 12:13], scalar1=rstd_ap,
                                                  scalar2=-1.0, op0=ALU.mult, op1=ALU.mult), reads=rw, writes=rw)
            yield
            S.op("act", lambda e: e.activation(out=src, in_=src, func=AF.Identity, bias=nmr_ap,
                                               scale=rstd_ap), reads=[srcB] + rw, writes=[srcB])
            yield
            if not affine:
                return
            S.op("pool", lambda e: e.tensor_tensor(out=src, in0=src, in1=gA, op=ALU.mult),
                 reads=[srcB, gB_], writes=[srcB])
            yield
            S.op("dve", lambda e: e.tensor_tensor(out=dst, in0=src, in1=bA, op=ALU.add),
                 reads=[srcB, gB_], writes=[dstB])
            yield

        def ln_apply_gen(src, srcB, dst, dstB, gA, bA, gB_, rn, rnB):
            S.op("act", lambda e: e.activation(out=src, in_=src, func=AF.Identity, bias=rn[:, 1:2],
                                               scale=rn[:, 0:1]), reads=[srcB, rnB], writes=[srcB])
            yield
            S.op("pool", lambda e: e.tensor_tensor(out=src, in0=src, in1=gA, op=ALU.mult),
                 reads=[srcB, gB_], writes=[srcB])
            yield
            S.op("dve", lambda e: e.tensor_tensor(out=dst, in0=src, in1=bA, op=ALU.add),
                 reads=[srcB, gB_], writes=[dstB])
            yield

        def ln_ip(*a_):
            for _ in ln_gen(*a_):
                pass

        def run_streams(gens, max_active, skew=1):
            active = []
            nxt = 0
            tick = 0
            while nxt < len(gens) or active:
                if nxt < len(gens) and len(active) < max_active and tick % skew == 0:
                    active.append(gens[nxt])
                    nxt += 1
                for g in list(active):
                    try:
                        next(g)
                    except StopIteration:
                        active.remove(g)
                tick += 1

        class BG:
            def __init__(self):
                self.active = []

            def add(self, g):
                self.active.append(g)

            def step(self, n=1):
                for _ in range(n):
                    for g in list(self.active):
                        try:
                            next(g)
                        except StopIteration:
                            self.active.remove(g)

            def drain(self):
                while self.active:
                    self.step(1)

        def transpose_tile_to(srcs, srcB, dst_fn, dstB, evac="act", banks=(4, 8)):
            bk = next_bank(*banks)
            n = len(srcs)
            S.op("pe", lambda e: [e.transpose(out=bank(bk)[:, q * 128:(q + 1) * 128], in_=srcs[q], identity=ident)
                                  for q in range(n)], reads=[srcB, constB], writes=[bankB[bk]])
            out_ap, in_ap = dst_fn(bank(bk)[:, 0:n * 128])
            if evac == "act":
                S.op("act", lambda e: e.activation(out=out_ap, in_=in_ap, func=AF.Copy),
                     reads=[bankB[bk]], writes=[dstB])
            else:
                S.op("dve", lambda e: e.tensor_copy(out=out_ap, in_=in_ap), reads=[bankB[bk]], writes=[dstB])

        def transpose_gen(srcs, srcB, dst_fn, dstB, bk=None):
            bk = next_bank(4, 8) if bk is None else bk
            n = len(srcs)
            S.op("pe", lambda e: [e.transpose(out=bank(bk)[:, q * 128:(q + 1) * 128], in_=srcs[q], identity=ident)
                                  for q in range(n)], reads=[srcB, constB], writes=[bankB[bk]])
            yield
            out_ap, in_ap = dst_fn(bank(bk)[:, 0:n * 128])
            S.op("act", lambda e: e.activation(out=out_ap, in_=in_ap, func=AF.Copy),
                 reads=[bankB[bk]], writes=[dstB])
            yield

        def load_w(dst, src, buf, dsem):
            S.dma("pool", out=dst, in_=src.rearrange("(c p) n -> p c n", p=128), dsem=dsem, writes=[buf])

        def dump(ap_list, stage_off=None):
            S.barrier()
            off = 0
            dD = S.dsem("dbg")
            stage = A.at(RD if stage_off is None else stage_off, [128, 8192], F32)
            sB = Buf("dbgstage")
            for ap in ap_list:
                n = ap.shape[1]
                S.op("dve", lambda e: e.tensor_copy(out=stage[:, off:off + n], in_=ap), writes=[sB])
                off += n
            S.dma("sp", out=dbg_d[:, 0:off], in_=stage[:, 0:off], dsem=dD, reads=[sB])
            S.barrier()
            raise _StopBuild()

        x0T = A.at(RA, [128, 8, T], BF16)
        x0TB = Buf("x0T")
        x0TBk = [Buf(f"x0Tk{i}") for i in range(4)]
        load_ln(0)
        p0 = Bump(A, RC, RC + 16 * KB)
        NS0 = 4
        xs = [p0.alloc([128, D], F32) for _ in range(NS0)]
        xsB = [Buf(f"xs{i}") for i in range(NS0)]
        xsD = [S.dsem(f"xs{i}") for i in range(NS0)]

        def p0_stream(i):
            sl = i % NS0
            S.dma("sp", out=xs[sl], in_=x_d[i * 128:(i + 1) * 128, :], dsem=xsD[sl], writes=[xsB[sl]])
            yield
            yield from ln_gen(xs[sl], xsB[sl], xs[sl], xsB[sl], gbuf, bbuf, gbB, rn=lnst[:, i, :], rnB=lnstB,
                              affine=False)
            for half in range(2):
                bk = 4 + sl
                S.op("pe", lambda e: [e.transpose(out=bank(bk)[:, q * 128:(q + 1) * 128],
                                                  in_=xs[sl][:, (half * 4 + q) * 128:(half * 4 + q + 1) * 128],
                                                  identity=ident) for q in range(4)],
                     reads=[xsB[sl], constB], writes=[bankB[bk]])
                yield
                for q in range(4):
                    kc = half * 4 + q
                    if q % 2 == 0:
                        S.op("act", lambda e: e.activation(out=x0T[:, kc, i * 128:(i + 1) * 128],
                                                           in_=bank(bk)[:, q * 128:(q + 1) * 128], func=AF.Identity,
                                                           scale=lnpT[:, kc:kc + 1], bias=lnpT[:, 8 + kc:9 + kc]),
                             reads=[bankB[bk], constB], writes=[x0TB, x0TBk[i // 4]])
                    else:
                        S.op("dve", lambda e: e.tensor_scalar(out=x0T[:, kc, i * 128:(i + 1) * 128],
                                                              in0=bank(bk)[:, q * 128:(q + 1) * 128],
                                                              scalar1=lnpT[:, kc:kc + 1], scalar2=lnpT[:, 8 + kc:9 + kc],
                                                              op0=ALU.mult, op1=ALU.add),
                             reads=[bankB[bk], constB], writes=[x0TB, x0TBk[i // 4]])
                    yield

        ph = Bump(A, RC + 16 * KB, TOP)
        ovl = Bump(A, RC, RC + 16 * KB)
        wbuf = [ph.alloc([128, 8, 640], BF16) for _ in range(2)]
        wB = [Buf("w0"), Buf("w1")]
        wD = [S.dsem("w0"), S.dsem("w1")]
        wi = {"n": 0}
        ph_mark = ph.top

        def proj_fm(W, wBuf, c0, dst_fn, dstB, func=AF.Copy, scale=1.0, kch=8, rhsT=None, rhsB=None, ntb=4,
                    banks=(0, 8), bgc=None, bgn=8):
            rT = x0T if rhsT is None else rhsT
            rB = x0TB if rhsB is None else rhsB
            for tb in range(ntb):
                bk = next_bank(*banks)
                S.op("pe", lambda e: [e.matmul(out=bank(bk), lhsT=W[:, kc, c0:c0 + 128],
                                               rhs=rT[:, kc, tb * 512:(tb + 1) * 512],
                                               start=(kc == 0), stop=(kc == kch - 1)) for kc in range(kch)],
                     reads=[wBuf, rB], writes=[bankB[bk]])
                o = dst_fn(tb)
                S.op("act", lambda e: e.activation(out=o, in_=bank(bk), func=func, scale=scale),
                     reads=[bankB[bk]], writes=[dstB[tb] if isinstance(dstB, list) else dstB])
                if bgc is not None:
                    bgc.step(bgn)

        y_naT = A.at(RB, [128, 4, T], BF16)
        y_hgT = A.at(RB + 16 * KB, [128, 4, T], BF16)
        y_memT = A.at(RB + 32 * KB, [128, 4, T], BF16)
        yTB = Buf("yT")

        qT = ph.alloc([128, 4, T], BF16)
        kT = ph.alloc([128, 4, T], BF16)
        vaug = ph.alloc([128, 16, 8, 66], BF16)
        qkvB = Buf("qkv")
        namask = ph.alloc([128, 2688], F32)
        nabias = [ph.alloc([128, 896], F32) for _ in range(2)]
        nabB = [Buf("nab0"), Buf("nab1")]
        nabD = [S.dsem("nab0"), S.dsem("nab1")]
        sc = [ovl.alloc([128, 640], F32) for _ in range(3)]
        scB = [Buf("sc0"), Buf("sc1"), Buf("sc2")]
        pT = [ph.alloc([128, 640], BF16) for _ in range(3)]
        pTB = [Buf("pT0"), Buf("pT1"), Buf("pT2")]
        yst = ovl.alloc([128, 16, 128], F32)
        ystB = Buf("yst")
        S.dma("sp", out=namask, in_=nam_d, dsem=cD, writes=[constB])
        S.op("pool", lambda e: e.memset(vaug[:, :, :, 64:66], 1.0), writes=[qkvB])

        nx = Bump(A, RB + 16 * KB, RB + 48 * KB)
        Bh = [nx.alloc([128, 2688], BF16) for _ in range(3)]
        BhB = [Buf(f"Bh{i}") for i in range(4)]
        ystraw = nx.alloc([128, 16, 2, 66], F32)
        ystrawB = Buf("ystraw")
        w3 = nx.alloc([128, 8, 512], BF16)
        w3B = Buf("w3")
        w3D = S.dsem("w3")
        Bh.append(w3.rearrange("p a b -> p (a b)")[:, 0:2688])
        Wna = [wbuf[0][:, :, 0:512], wbuf[1][:, :, 0:512], w3]
        WnaB = [wB[0], wB[1], w3B]
        load_w(Wna[0], w_na_d[:, 0:512], wB[0], wD[0])
        load_w(Wna[1], w_na_d[:, 512:1024], wB[1], wD[1])
        load_w(Wna[2], w_na_d[:, 1024:1536], w3B, w3D)
        wi["n"] = 3
        evn = {"n": 0, "b": 0}

        def pbank():
            evn["b"] += 1
            return evn["b"] % 4

        def na_proj_stream(tb):
            tsl = slice(tb * 512, (tb + 1) * 512)
            for blk in range(2):
                dstT = qT if blk == 0 else kT
                scl = 0.125 if blk == 0 else 1.0
                for cb_ in range(4):
                    bk = pbank()
                    S.op("pe", lambda e: [e.matmul(out=bank(bk), lhsT=Wna[blk][:, kc, cb_ * 128:(cb_ + 1) * 128],
                                                   rhs=x0T[:, kc, tsl], start=(kc == 0), stop=(kc == 7))
                                          for kc in range(8)], reads=[WnaB[blk], x0TBk[tb]], writes=[bankB[bk]])
                    yield
                    evn["n"] += 1
                    if evn["n"] % 2 == 0:
                        S.op("act", lambda e: e.activation(out=dstT[:, cb_, tsl], in_=bank(bk), func=AF.Copy,
                                                           scale=scl), reads=[bankB[bk]], writes=[qkvB])
                    else:
                        S.op("dve", lambda e: e.tensor_scalar(out=dstT[:, cb_, tsl], in0=bank(bk), scalar1=scl,
                                                              scalar2=None, op0=ALU.mult),
                             reads=[bankB[bk]], writes=[qkvB])
                    yield
            for q in range(4):
                i = tb * 4 + q
                bk = pbank()
                S.op("pe", lambda e: [e.matmul(out=bank(bk), lhsT=x0T[:, kc, i * 128:(i + 1) * 128],
                                               rhs=Wna[2][:, kc, :], start=(kc == 0), stop=(kc == 7))
                                      for kc in range(8)], reads=[WnaB[2], x0TBk[tb]], writes=[bankB[bk]])
                yield
                S.op("act", lambda e: e.activation(out=vaug[:, i, :, 0:64],
                                                   in_=bank(bk).rearrange("p (h d) -> p h d", h=8),
                                                   func=AF.Copy), reads=[bankB[bk]], writes=[qkvB])
                yield

        for tb in range(5):
            gens = []
            if tb < 4:
                gens += [p0_stream(i) for i in range(tb * 4, tb * 4 + 4)]
            if tb >= 1:
                gens.insert(min(1, len(gens)), na_proj_stream(tb - 1))
            run_streams(gens, 5, skew=2)
        if dbg == "x0T":
            dump([x0T[:, kc, 0:1024] for kc in range(8)])

        load_w(wbuf[1], w_hg_d[0], wB[1], wD[1])
        NSA = 3
        ucnt = {"n": 0}
        recn = lnsm[5][:, 0:32].rearrange("p (j h o) -> p j h o", j=16, h=2)

        def bh_build(h):
            hs = h % 2
            k = h % 4
            S.dma("sp", out=nabias[hs], in_=nab_d[h], dsem=nabD[hs], writes=[nabB[hs]])
            for (m0, m1, b0) in ((0, 640, 128), (640, 1152, 384), (1152, 1664, 256), (1664, 2176, 128),
                                 (2176, 2688, 0)):
                S.op("pool", lambda e: e.tensor_tensor(out=Bh[k][:, m0:m1], in0=namask[:, m0:m1],
                                                       in1=nabias[hs][:, b0:b0 + (m1 - m0)], op=ALU.add),
                     reads=[nabB[hs], constB] + ([w3B, qkvB] if k == 3 else []), writes=[BhB[k]])

        def na_unit(h, j):
            hb = h // 2
            po = (h % 2) * 64
            hs = h % 2
            k = h % 4
            u = ucnt["n"]
            ucnt["n"] += 1
            si = u % NSA
            ob = 6 + u % 2
            if j == 2 + (h % 2) and h + 2 < 8:
                bh_build(h + 2)
            chunks, boff, moff = _na_offsets(j)
            n = len(chunks)
            ps = pp[si]
            n0 = min(n, 4) * 128
            S.op("pe", lambda e: [e.matmul(out=ps[:, 0:n0], lhsT=identb, rhs=Bh[k][:, moff:moff + n0],
                                           start=True, stop=False)] +
                                 ([e.matmul(out=ps[:, 512:640], lhsT=identb, rhs=Bh[k][:, moff + 512:moff + 640],
                                            start=True, stop=False)] if n == 5 else []) +
                                 [e.matmul(out=ps[:, ci * 128:(ci + 1) * 128],
                                           lhsT=kT[po:po + 64, hb, c * 128:(c + 1) * 128],
                                           rhs=qT[po:po + 64, hb, j * 128:(j + 1) * 128],
                                           start=False, stop=True) for ci, c in enumerate(chunks)],
                 reads=[qkvB, BhB[k], constB], writes=[bankB[2 * si], bankB[2 * si + 1]])
            yield
            S.op("act", lambda e: e.activation(out=pT[si][:, 0:n * 128], in_=ps[:, 0:n * 128], func=AF.Exp),
                 reads=[bankB[2 * si], bankB[2 * si + 1]], writes=[pTB[si]])
            yield
            S.op("pe", lambda e: [e.matmul(out=bank(ob)[:, 0:66], lhsT=pT[si][:, ci * 128:(ci + 1) * 128],
                                           rhs=vaug[:, c, h, :], start=(ci == 0), stop=(ci == n - 1))
                                  for ci, c in enumerate(chunks)],
                 reads=[pTB[si], qkvB], writes=[bankB[ob]])
            yield
            S.op("dve", lambda e: e.tensor_copy(out=ystraw[:, j, hs, :], in_=bank(ob)[:, 0:66]),
                 reads=[bankB[ob]], writes=[ystrawB])
            yield

        bh_build(0)
        bh_build(1)
        for hb_ in range(4):
            run_streams([na_unit(h, j) for j in range(NT) for h in (2 * hb_, 2 * hb_ + 1)], NSA, skew=1)
            S.op("dve", lambda e: e.reciprocal(out=recn, in_=ystraw[:, :, :, 64:65]),
                 reads=[ystrawB], writes=[ystB])
            S.op("dve", lambda e: e.tensor_tensor(out=yst.rearrange("p j (h d) -> p j h d", h=2),
                                                  in0=ystraw[:, :, :, 0:64],
                                                  in1=recn.to_broadcast([128, 16, 2, 64]), op=ALU.mult),
                 reads=[ystrawB, ystB], writes=[ystB])
            for j4 in range(4):
                srcs = [yst[:, j4 * 4 + q, :] for q in range(4)]
                transpose_tile_to(srcs, ystB,
                                  lambda bap: (y_naT[:, hb_, j4 * 512:(j4 + 1) * 512], bap), yTB, banks=(6, 8))
        if dbg == "y_naT":
            dump([y_naT[:, kc, 0:T] for kc in range(4)])

        S.barrier()
        ph.top = ph_mark
        hx = Bump(A, RB + 32 * KB, RB + 48 * KB)
        ov2 = Bump(A, RC, RC + 16 * KB)
        q32 = ov2.alloc([128, T], F32)
        Fb = ov2.alloc([128, T], F32)
        Gext = ph.alloc([128, T + 16], F32)
        Gb = Gext[:, 1:T + 1]
        Tq = lnsm[4][:, 0:32].rearrange("p (a b) -> p a b", a=4)
        TqB = Buf("Tq")
        Ib = ph.alloc([128, T], F32)
        Ab = ph.alloc([128, T], F32)
        qt = [ph.alloc([128, T], BF16) for _ in range(2)]
        kt = [ph.alloc([128, T], BF16) for _ in range(2)]
        kh = [ph.alloc([128, T], F32) for _ in range(2)]
        expT = [ph.alloc([128, 32], F32) for _ in range(2)]
        vh = ph.alloc([128, 16, 128], BF16)
        oacc = ph.alloc([128, 16, 128], F32)
        sog = hx.alloc([128, 16, 128], F32)
        kA = [hx.alloc([128, 512], BF16) for _ in range(3)]
        At2 = [hx.alloc([128, 256], BF16) for _ in range(3)]
        S32p = [hx.alloc([128, 2, 128], F32) for _ in range(2)]
        S32B = [Buf("S32_0"), Buf("S32_1")]
        Sbf2 = [hx.alloc([128, 2, 128], BF16) for _ in range(2)]
        ssq = hx.alloc([128, 16], F32)
        rstd_h = hx.alloc([128, 16], F32)
        ysth = [ph.alloc([128, 128], F32) for _ in range(4)]
        junk = ph.alloc([128, 128], F32)
        hgQ = [Buf(f"hgQ{i}") for i in range(4)]
        q32Q = [Buf(f"q32Q{i}") for i in range(4)]
        dirQ = [[Buf(f"dirQ{d}{i}") for i in range(4)] for d in range(2)]
        hgB = Buf("hg")
        khtokB = [Buf(f"khtok{i}") for i in range(4)]
        AtB = [Buf(f"At{i}") for i in range(4)]
        SB_ = [Buf("S_f"), Buf("S_b")]
        SbB = [Buf("Sb0"), Buf("Sb1")]
        fbQ = [[Buf(f"fb{k}_{q}") for q in range(4)] for k in range(4)]
        oaccB = Buf("oacc")
        ysthB = [Buf(f"ysth{i}") for i in range(4)]
        voB = Buf("vo")
        S.op("pool", lambda e: e.memset(Gext[:, 0:1], 0.0), writes=[constB])
        S.op("pool", lambda e: e.memset(scanm, 1.0), writes=[constB])
        S.op("pool", lambda e: e.memset(scanm.rearrange("p (c s) -> p c s", c=32)[:, :, 0:1], 0.0), writes=[constB])
        rot = {"k": 0, "f": 0}
        HB = (4, 8)

        def Q(ap, qr):
            return ap[:, qr * 512:(qr + 1) * 512]

        def Q3(ap, qr):
            return ap[:, qr * 512:(qr + 1) * 512].rearrange("p (c s) -> p c s", c=8)

        def hg_load(h_):
            ws_ = (3 + h_) % 2
            load_w(wbuf[ws_], w_hg_d[h_], wB[ws_], wD[ws_])

        def epi_gen(h):
            S.op("act", lambda e: e.activation(out=sog.rearrange("p a b -> p (a b)"),
                                               in_=sog.rearrange("p a b -> p (a b)"), func=AF.Silu),
                 reads=[voB], writes=[voB])
            yield
            for i in range(NT):
                S.op("act", lambda e: e.activation(out=junk, in_=oacc[:, i, :], func=AF.Square,
                                                   accum_out=ssq[:, i:i + 1]), reads=[oaccB], writes=[hgB])
                yield
            S.op("act", lambda e: e.activation(out=rstd_h, in_=ssq, func=AF.Sqrt, bias=epsrms[:, 0:1],
                                               scale=1.0 / 128.0), reads=[hgB, constB], writes=[hgB])
            yield
            S.op("dve", lambda e: e.reciprocal(out=rstd_h, in_=rstd_h), reads=[hgB], writes=[hgB])
            yield
            for i4 in range(4):
                for q in range(4):
                    i = i4 * 4 + q
                    S.op("dve", lambda e: e.scalar_tensor_tensor(out=ysth[q], in0=oacc[:, i, :],
                                                                 scalar=rstd_h[:, i:i + 1],
                                                                 in1=hgn[:, h * 128:(h + 1) * 128],
                                                                 op0=ALU.mult, op1=ALU.mult),
                         reads=[oaccB, hgB, constB], writes=[ysthB[q]])
                    yield
                    S.op("pool", lambda e: e.tensor_tensor(out=ysth[q], in0=ysth[q], in1=sog[:, i, :], op=ALU.mult),
                         reads=[ysthB[q], voB], writes=[ysthB[q]])
                    yield
                bk = next_bank(*HB)
                S.op("pe", lambda e: [e.transpose(out=bank(bk)[:, q * 128:(q + 1) * 128], in_=ysth[q], identity=ident)
                                      for q in range(4)], reads=ysthB + [constB], writes=[bankB[bk]])
                S.op("act", lambda e: e.activation(out=y_hgT[:, h, i4 * 512:(i4 + 1) * 512], in_=bank(bk), func=AF.Copy),
                     reads=[bankB[bk]], writes=[yTB])
                yield

        pend_epi = None
        for h in range(4):
            ws = (3 + h) % 2
            wi["n"] += 1
            W = wbuf[ws]
            bge = BG()
            if pend_epi is not None:
                bge.add(pend_epi)
            proj_fm(W, wB[ws], 0, lambda tb: q32[:, tb * 512:(tb + 1) * 512], q32Q, func=AF.Silu, banks=HB,
                    bgc=bge, bgn=8)
            def vog_emit(bgc):
                for i in range(NT):
                    bk = next_bank(*HB)
                    S.op("pe", lambda e: [e.matmul(out=bank(bk)[:, 0:256], lhsT=x0T[:, kc, i * 128:(i + 1) * 128],
                                                   rhs=W[:, kc, 384:640], start=(kc == 0), stop=(kc == 7))
                                          for kc in range(8)], reads=[wB[ws], x0TB], writes=[bankB[bk]])
                    S.op("act", lambda e: e.activation(out=vh[:, i, :], in_=bank(bk)[:, 0:128], func=AF.Copy),
                         reads=[bankB[bk]], writes=[voB])
                    S.op("act", lambda e: e.activation(out=sog[:, i, :], in_=bank(bk)[:, 128:256], func=AF.Copy),
                         reads=[bankB[bk]], writes=[voB])
                    bgc.step(4)

            for d in range(2):
                col = d * 4 + h
                lb_ap = lbv[:, col:col + 1]
                om_ap = omv[:, col:col + 1]
                proj_fm(W, wB[ws], 128 + d * 128, lambda tb: Fb[:, tb * 512:(tb + 1) * 512], hgQ,
                        func=AF.Sigmoid, banks=HB, bgc=(bge if d == 0 else None), bgn=8)
                if d == 0:
                    bge.drain()
                if d == 1 and h + 1 < 4:
                    hg_load(h + 1)
                if d == 1 and h == 3:
                    load_w(wbuf[1][:, :, 0:512], w_mkv_d[:, 0:512], wB[1], wD[1])
                    load_w(wbuf[0][:, :, 0:512], w_mkv_d[:, 512:1024], wB[0], wD[0])
                chain = []
                chain.append(lambda qr: S.op("act", lambda e: e.activation(
                    out=Q(Fb, qr), in_=Q(Fb, qr), func=AF.Identity, bias=lb_ap, scale=om_ap),
                    reads=[hgQ[qr], constB], writes=[hgQ[qr]]))
                chain.append(lambda qr: S.op("act", lambda e: e.activation(out=Q(Gb, qr), in_=Q(Fb, qr), func=AF.Ln),
                                             reads=[hgQ[qr]], writes=[hgQ[qr]]))
                chain.append(lambda qr: S.op("act", lambda e: e.activation(
                    out=Q(Fb, qr), in_=Q(Fb, qr), func=AF.Identity, bias=1.0, scale=-1.0),
                    reads=[hgQ[qr]], writes=[hgQ[qr]]))
                if d == 0:
                    chain.append(lambda qr: S.op("dve", lambda e: e.tensor_tensor_scan(
                        out=Q(Ib, qr), data0=scanm[:, 0:512], data1=Q(Gb, qr), initial=0.0, op0=ALU.mult, op1=ALU.add),
                        reads=[hgQ[qr], constB], writes=[hgQ[qr]]))
                    chain.append(lambda qr: S.op("dve", lambda e: e.tensor_tensor(
                        out=Q3(Ab, qr), in0=Q3(Ib, qr)[:, :, 63:64].to_broadcast([128, 8, 64]), in1=Q3(Ib, qr),
                        op=ALU.subtract), reads=[hgQ[qr]], writes=[hgQ[qr]]))
                    chain.append(lambda qr: S.op("act", lambda e: e.activation(
                        out=expT[d][:, qr * 8:(qr + 1) * 8], in_=Q3(Ib, qr)[:, :, 63:64].rearrange("p c o -> p (c o)"),
                        func=AF.Exp), reads=[hgQ[qr]], writes=[dirQ[d][qr]]))
                    X_, U_ = Ab, Ib
                else:
                    chain.append(lambda qr: S.op("dve", lambda e: e.tensor_tensor_scan(
                        out=Q(Ib, qr), data0=Gext[:, qr * 512:qr * 512 + 512], data1=scanm[:, 0:512], initial=0.0,
                        op0=ALU.add, op1=ALU.mult), reads=[hgQ[qr], constB], writes=[hgQ[qr]]))
                    chain.append(lambda qr: S.op("dve", lambda e: e.tensor_tensor(
                        out=Tq[:, qr, :], in0=Q3(Ib, qr)[:, :, 63:64].rearrange("p c o -> p (c o)"),
                        in1=Q3(Gb, qr)[:, :, 63:64].rearrange("p c o -> p (c o)"), op=ALU.add),
                        reads=[hgQ[qr]], writes=[TqB]))
                    chain.append(lambda qr: S.op("act", lambda e: e.activation(
                        out=expT[d][:, qr * 8:(qr + 1) * 8], in_=Tq[:, qr, :], func=AF.Exp),
                        reads=[TqB], writes=[dirQ[d][qr]]))
                    chain.append(lambda qr: S.op("dve", lambda e: e.tensor_tensor(
                        out=Q3(Ab, qr), in0=Tq[:, qr, :].rearrange("p (c o) -> p c o", o=1).to_broadcast([128, 8, 64]),
                        in1=Q3(Ib, qr), op=ALU.subtract), reads=[hgQ[qr], TqB], writes=[hgQ[qr]]))
                    X_, U_ = Ib, Ab
                chain.append(lambda qr: S.op("act", lambda e: e.activation(out=Q(X_, qr), in_=Q(X_, qr), func=AF.Exp),
                                             reads=[hgQ[qr]], writes=[hgQ[qr]]))
                chain.append(lambda qr: S.op("pool", lambda e: e.tensor_tensor(
                    out=Q(kh[d], qr), in0=Q(Fb, qr), in1=Q(X_, qr), op=ALU.mult),
                    reads=[hgQ[qr]], writes=[dirQ[d][qr]]))
                chain.append(lambda qr: S.op("act", lambda e: e.activation(out=Q(X_, qr), in_=Q(U_, qr), func=AF.Exp),
                                             reads=[hgQ[qr], dirQ[d][qr]], writes=[hgQ[qr]]))
                chain.append(lambda qr: S.op("dve", lambda e: e.tensor_tensor(
                    out=Q(qt[d], qr), in0=Q(q32, qr), in1=Q(X_, qr), op=ALU.mult),
                    reads=[hgQ[qr], q32Q[qr]], writes=[dirQ[d][qr]]))
                chain.append(lambda qr: S.op("act", lambda e: e.activation(out=Q(X_, qr), in_=Q(U_, qr), func=AF.Exp,
                                                                           scale=-1.0),
                                             reads=[hgQ[qr], dirQ[d][qr]], writes=[hgQ[qr]]))
                chain.append(lambda qr: S.op("pool", lambda e: e.tensor_tensor(
                    out=Q(kt[d], qr), in0=Q(Fb, qr), in1=Q(X_, qr), op=ALU.mult),
                    reads=[hgQ[qr]], writes=[dirQ[d][qr]]))
                def chain_gen():
                    for opf in chain:
                        for qr in range(4):
                            opf(qr)
                            yield

                if d == 0:
                    bgc = BG()
                    bgc.add(chain_gen())
                    vog_emit(bgc)
                    bgc.drain()
                else:
                    for _ in chain_gen():
                        pass
            spp = [0]
            S.op("pool", lambda e: e.memset(S32p[0], 0.0), writes=[S32B[0]])
            S.op("pool", lambda e: e.memset(Sbf2[0], 0.0), writes=[SbB[0]])

            def tile_of(d, step):
                return step if d == 0 else NT - 1 - step

            fr = {}

            def F1_emit(step):
                r = rot["k"] % 3
                rot["k"] += 1
                xy = step % 2
                z = [2 + (step % 2) * 2, 3 + (step % 2) * 2]
                tiles = [tile_of(d, step) for d in range(2)]
                dqs = [dirQ[d][tiles[d] // 4] for d in range(2)]
                tsl = [slice(i * 128, (i + 1) * 128) for i in tiles]
                S.op("pe", lambda e: [e.transpose(out=bank(xy)[:, d * 128:(d + 1) * 128], in_=kh[d][:, tsl[d]],
                                                  identity=ident) for d in range(2)] +
                                     [e.matmul(out=bank(xy)[:, 256 + d * 128:256 + (d + 1) * 128], lhsT=kt[d][:, tsl[d]],
                                               rhs=qt[d][:, tsl[d]], start=True, stop=True) for d in range(2)],
                     reads=dqs + [constB], writes=[bankB[xy]])
                S.op("act", lambda e: e.activation(out=kA[r], in_=bank(xy), func=AF.Copy),
                     reads=[bankB[xy]], writes=[khtokB[r]])
                S.op("pool", lambda e: e.tensor_tensor(out=At2[r], in0=kA[r][:, 256:512],
                                                       in1=hgm.rearrange("p a n -> p (a n)"), op=ALU.mult),
                     reads=[khtokB[r], constB], writes=[AtB[r]])
                fr[step] = (r, z)

            def F2_emit(step):
                r, z = fr[step]
                tiles = [tile_of(d, step) for d in range(2)]
                S.op("pe", lambda e: [e.matmul(out=bank(z[sub])[:, d * 128:(d + 1) * 128],
                                               lhsT=kA[r][sub * 64:(sub + 1) * 64, d * 128:(d + 1) * 128],
                                               rhs=vh[sub * 64:(sub + 1) * 64, tiles[d], :], start=True, stop=True)
                                      for sub in range(2) for d in range(2)],
                     reads=[khtokB[r], voB], writes=[bankB[z[0]], bankB[z[1]]])

            def B_emit(step):
                r, z = fr[step]
                obs = {}
                for d in range(2):
                    i = tile_of(d, step)
                    ob = 6 + d
                    obs[d] = ob
                    S.op("pe", lambda e: e.matmul(out=bank(ob)[:, 0:128], lhsT=At2[r][:, d * 128:(d + 1) * 128],
                                                  rhs=vh[:, i, :], start=True, stop=False),
                         reads=[AtB[r], voB], writes=[bankB[ob]])
                for si_ in range(2):
                    if si_ == 1 and step + 1 < NT:
                        F2_emit(step + 1)
                    cur = spp[0]
                    nxt = 1 - cur
                    for d in range(2):
                        i = tile_of(d, step)
                        ob = obs[d]
                        sub = si_ if d == 0 else 1 - si_
                        c = 2 * i + sub
                        ssl = slice(i * 128 + sub * 64, i * 128 + sub * 64 + 64)
                        dq = dirQ[d][i // 4]
                        S.op("pe", lambda e: e.matmul(out=bank(ob)[sub * 64:(sub + 1) * 64, 0:128],
                                                      lhsT=qt[d][:, ssl], rhs=Sbf2[cur][:, d, :],
                                                      start=False, stop=True),
                             reads=[dq, SbB[cur]], writes=[bankB[ob]])
                        kv = bank(z[sub])[:, d * 128:(d + 1) * 128]
                        S.op("dve", lambda e: e.scalar_tensor_tensor(out=S32p[nxt][:, d, :], in0=S32p[cur][:, d, :],
                                                                     scalar=expT[d][:, c:c + 1], in1=kv,
                                                                     op0=ALU.mult, op1=ALU.add),
                             reads=[bankB[z[sub]], dq, S32B[cur]], writes=[S32B[nxt]])
                    S.op("act", lambda e: e.activation(out=Sbf2[nxt], in_=S32p[nxt], func=AF.Copy),
                         reads=[S32B[nxt]], writes=[SbB[nxt]])
                    spp[0] = nxt
                for d in range(2):
                    i = tile_of(d, step)
                    ob = obs[d]
                    if step < NT // 2:
                        S.op("act", lambda e: e.activation(out=oacc[:, i, :], in_=bank(ob)[:, 0:128], func=AF.Copy),
                             reads=[bankB[ob]], writes=[oaccB])
                    else:
                        S.op("dve", lambda e: e.tensor_tensor(out=oacc[:, i, :], in0=bank(ob)[:, 0:128],
                                                              in1=oacc[:, i, :], op=ALU.add),
                             reads=[bankB[ob], oaccB], writes=[oaccB])

            F1_emit(0)
            F2_emit(0)
            for step in range(NT):
                if step + 1 < NT:
                    F1_emit(step + 1)
                B_emit(step)
            pend_epi = epi_gen(h)
        for _ in pend_epi:
            pass
        if dbg == "y_hgT":
            dump([y_hgT[:, kc, 0:T] for kc in range(4)])

        S.barrier()
        ph.top = ph_mark
        memst = [ph.alloc([128, D], F32) for _ in range(2)]
        memT = ph.alloc([128, 8, 256], BF16)
        kmT = ph.alloc([128, 4, 256], BF16)
        vma = ph.alloc([128, 2, 4, 130], BF16)
        qmT = ph.alloc([128, 4, T], BF16)
        pm = [ph.alloc([128, 8, 512], BF16) for _ in range(2)]
        ystm = [ph.alloc([128, 512], F32) for _ in range(2)]
        ysraw = [ph.alloc([128, 4, 130], F32) for _ in range(2)]
        ysrawB = [Buf("ysraw0"), Buf("ysraw1")]
        memB = Buf("mem")
        assert ph.top <= TOP - 9 * KB
        wmg0 = A.at(TOP - 9 * KB, [128, 8, 384], BF16)
        wmb0 = A.at(TOP - 3 * KB, [128, 4, 384], BF16)
        wmB = [Buf("wm0"), Buf("wm1")]
        wmD = [S.dsem("wm0"), S.dsem("wm1")]
        load_w(wmg0, w_mg_d[0], wmB[0], wmD[0])
        load_w(wmb0, w_mb_d[0], wmB[0], wmD[0])
        mD = [S.dsem("mem0"), S.dsem("mem1")]
        pmB = [Buf("pm0"), Buf("pm1")]
        ystmB = [Buf("ystm0"), Buf("ystm1")]
        ws = wi["n"] % 2
        wi["n"] += 1
        Wk = wbuf[ws][:, :, 0:512]
        assert ws == 1
        ws2 = wi["n"] % 2
        wi["n"] += 1
        Wv = wbuf[ws2][:, :, 0:512]
        assert ws2 == 0
        for mc in range(2):
            S.dma("sp", out=memst[mc], in_=mem_d[mc * 128:(mc + 1) * 128, :], dsem=mD[mc], writes=[memB])
        S.op("pool", lambda e: e.memset(vma[:, :, :, 128:130], 1.0), writes=[memB])
        for mc in range(2):
            for half in range(2):
                srcs = [memst[mc][:, (half * 4 + q) * 128:(half * 4 + q + 1) * 128] for q in range(4)]
                transpose_tile_to(srcs, memB,
                                  lambda bap: (memT[:, half * 4:half * 4 + 4, mc * 128:(mc + 1) * 128],
                                               bap.rearrange("p (a b) -> p a b", a=4)), memB)
        for hd in range(4):
            bk = next_bank()
            S.op("pe", lambda e: [e.matmul(out=bank(bk)[:, 0:256], lhsT=Wk[:, kc, hd * 128:(hd + 1) * 128],
                                           rhs=memT[:, kc, :], start=(kc == 0), stop=(kc == 7)) for kc in range(8)],
                 reads=[wB[ws], memB], writes=[bankB[bk]])
            S.op("act", lambda e: e.activation(out=kmT[:, hd, :], in_=bank(bk)[:, 0:256], func=AF.Copy),
                 reads=[bankB[bk]], writes=[memB])
        for mc in range(2):
            bk = next_bank()
            S.op("pe", lambda e: [e.matmul(out=bank(bk), lhsT=memT[:, kc, mc * 128:(mc + 1) * 128],
                                           rhs=Wv[:, kc, :], start=(kc == 0), stop=(kc == 7)) for kc in range(8)],
                 reads=[wB[ws2], memB], writes=[bankB[bk]])
            S.op("act", lambda e: e.activation(out=vma[:, mc, :, 0:128],
                                               in_=bank(bk).rearrange("p (h d) -> p h d", h=4), func=AF.Copy),
                 reads=[bankB[bk]], writes=[memB])
        ws = wi["n"] % 2
        wi["n"] += 1
        Wq = wbuf[ws][:, :, 0:512]
        load_w(Wq, w_mq_d, wB[ws], wD[ws])
        for hd in range(4):
            proj_fm(Wq, wB[ws], hd * 128, lambda tb: qmT[:, hd, tb * 512:(tb + 1) * 512], memB)
        mscale = 128.0 ** -0.5
        for tb in range(4):
            ps_ = tb % 2
            for hd in range(4):
                for mc in range(2):
                    bk = next_bank(0, 4)
                    S.op("pe", lambda e: e.matmul(out=bank(bk), lhsT=kmT[:, hd, mc * 128:(mc + 1) * 128],
                                                  rhs=qmT[:, hd, tb * 512:(tb + 1) * 512], start=True, stop=True),
                         reads=[memB], writes=[bankB[bk]])
                    S.op("act", lambda e: e.activation(out=pm[ps_][:, hd * 2 + mc, :], in_=bank(bk), func=AF.Exp,
                                                       scale=mscale), reads=[bankB[bk]], writes=[pmB[ps_]])
            def mem_unit(i):
                q = i % 4
                ys = i % 2
                obs_ = [next_bank(4, 8), next_bank(4, 8)]
                for pr in range(2):
                    S.op("pe", lambda e: [e.matmul(out=bank(obs_[pr])[:, hh * 130:(hh + 1) * 130],
                                                   lhsT=pm[ps_][:, (pr * 2 + hh) * 2 + mc, q * 128:(q + 1) * 128],
                                                   rhs=vma[:, mc, pr * 2 + hh, :], start=(mc == 0), stop=(mc == 1))
                                          for hh in range(2) for mc in range(2)],
                         reads=[pmB[ps_], memB], writes=[bankB[obs_[pr]]])
                    yield
                    S.op("act", lambda e: e.activation(out=ysraw[ys][:, pr * 2:pr * 2 + 2, :],
                                                       in_=bank(obs_[pr])[:, 0:260].rearrange("p (h d) -> p h d", h=2),
                                                       func=AF.Copy), reads=[bankB[obs_[pr]]], writes=[ysrawB[ys]])
                    yield
                rc = rec_s[ys]
                S.op("dve", lambda e: e.reciprocal(out=rc[:, 0:4].rearrange("p (h o) -> p h o", o=1),
                                                   in_=ysraw[ys][:, :, 128:129]),
                     reads=[ysrawB[ys]], writes=[ystmB[ys]])
                yield
                S.op("dve", lambda e: e.tensor_tensor(out=ystm[ys].rearrange("p (h d) -> p h d", h=4),
                                                      in0=ysraw[ys][:, :, 0:128],
                                                      in1=rc[:, 0:4].rearrange("p (h o) -> p h o", o=1)
                                                      .to_broadcast([128, 4, 128]), op=ALU.mult),
                     reads=[ysrawB[ys], ystmB[ys]], writes=[ystmB[ys]])
                yield
                srcs = [ystm[ys][:, k * 128:(k + 1) * 128] for k in range(4)]
                yield from transpose_gen(srcs, ystmB[ys],
                                         lambda bap: (y_memT[:, 0:4, i * 128:(i + 1) * 128],
                                                      bap.rearrange("p (a b) -> p a b", a=4)), yTB)

            run_streams([mem_unit(tb * 4 + q) for q in range(4)], 2, skew=2)
        if dbg == "y_memT":
            dump([y_memT[:, kc, 0:T] for kc in range(4)])

        S.barrier()
        mergedT = A.at(RC, [128, 8, T], BF16)
        mgB = Buf("mergedT")
        p3 = Bump(A, RD, TOP)
        wmg = [wmg0, p3.alloc([128, 8, 384], BF16)]
        wmb = [wmb0, p3.alloc([128, 4, 384], BF16)]
        sg = [p3.alloc([128, 512], F32) for _ in range(2)]
        sgB = [Buf("sg0"), Buf("sg1")]
        macc = [p3.alloc([128, 512], F32) for _ in range(2)]
        maccB = [Buf("macc0"), Buf("macc1")]
        tmpm = [p3.alloc([128, 512], F32) for _ in range(2)]
        tmpmB = [Buf("tmpm0"), Buf("tmpm1")]
        yTs = [y_naT, y_hgT, y_memT]
        cnt = {"sg": 0, "ma": 0}
        def mg_load(j):
            ws_ = j % 2
            load_w(wmg[ws_], w_mg_d[j], wmB[ws_], wmD[ws_])
            load_w(wmb[ws_], w_mb_d[j], wmB[ws_], wmD[ws_])

        for j in range(8):
            ws = j % 2
            if j + 1 < 8:
                mg_load(j + 1)
            for tb in range(4):
                ma = cnt["ma"] % 2
                cnt["ma"] += 1
                tsl = slice(tb * 512, (tb + 1) * 512)
                for b in range(3):
                    gb_ = next_bank()
                    S.op("pe", lambda e: [e.matmul(out=bank(gb_), lhsT=wmg[ws][:, kc, b * 128:(b + 1) * 128],
                                                   rhs=x0T[:, kc, tsl], start=(kc == 0), stop=(kc == 7))
                                          for kc in range(8)], reads=[wmB[ws], x0TB], writes=[bankB[gb_]])
                    pb_ = next_bank()
                    S.op("pe", lambda e: [e.matmul(out=bank(pb_), lhsT=wmb[ws][:, kc, b * 128:(b + 1) * 128],
                                                   rhs=yTs[b][:, kc, tsl], start=(kc == 0), stop=(kc == 3))
                                          for kc in range(4)], reads=[wmB[ws], yTB], writes=[bankB[pb_]])
                    s_ = cnt["sg"] % 2
                    cnt["sg"] += 1
                    S.op("act", lambda e: e.activation(out=sg[s_], in_=bank(gb_), func=AF.Sigmoid),
                         reads=[bankB[gb_]], writes=[sgB[s_]])
                    if b == 0:
                        S.op("dve", lambda e: e.tensor_tensor(out=macc[ma], in0=bank(pb_), in1=sg[s_], op=ALU.mult),
                             reads=[bankB[pb_], sgB[s_]], writes=[maccB[ma]])
                    else:
                        S.op("dve", lambda e: e.tensor_tensor(out=tmpm[ma], in0=bank(pb_), in1=sg[s_], op=ALU.mult),
                             reads=[bankB[pb_], sgB[s_]], writes=[tmpmB[ma]])
                        if b == 1:
                            S.op("pool", lambda e: e.tensor_tensor(out=macc[ma], in0=macc[ma], in1=tmpm[ma], op=ALU.add),
                                 reads=[maccB[ma], tmpmB[ma]], writes=[maccB[ma]])
                        else:
                            S.op("pool", lambda e: e.tensor_tensor(out=mergedT[:, j, tsl], in0=macc[ma], in1=tmpm[ma],
                                                                   op=ALU.add),
                                 reads=[maccB[ma], tmpmB[ma]], writes=[mgB])
        if dbg == "mergedT":
            dump([mergedT[:, kc, 0:1024] for kc in range(8)])

        S.barrier()
        x1 = A.at(RD, [128, NT, D], F32)
        x1B = [Buf(f"x1_{i}") for i in range(NT)]
        x1T = A.at(RA, [128, 8, T], BF16)
        x1TB = Buf("x1T")
        pb3 = Bump(A, RB, RC)
        wout = pb3.alloc([128, 8, D], BF16)
        woB = Buf("wout")
        woD = S.dsem("wout")
        for hf in range(2):
            S.dma("pool", out=wout[:, :, hf * 512:(hf + 1) * 512],
                  in_=w_out_d[:, hf * 512:(hf + 1) * 512].rearrange("(c p) n -> p c n", p=128), dsem=woD, writes=[woB])
        g1 = pb3.alloc([128, D], F32)
        b1 = pb3.alloc([128, D], F32)
        g1B = Buf("g1")
        g1D = S.dsem("g1")
        S.dma("sp", out=g1, in_=lnp_d[2:3, :].to_broadcast([128, D]), dsem=g1D, writes=[g1B])
        S.dma("sp", out=b1, in_=lnp_d[3:4, :].to_broadcast([128, D]), dsem=g1D, writes=[g1B])
        pf = Bump(A, RB, RD)
        hT = [pf.alloc([128, 4, T], BF16) for _ in range(2)]
        hTB = [Buf("hT0"), Buf("hT1")]
        w1 = [None, None]
        w2 = [None, None]
        for k_ in range(2):
            w1[k_] = pf.alloc([128, 8, 512], BF16)
            w2[k_] = pf.alloc([128, 4, D], BF16)
        w1B = [Buf("w1_0"), Buf("w1_1")]
        w2B = [Buf("w2_0"), Buf("w2_1")]
        w1D = [S.dsem("w1_0"), S.dsem("w1_1")]
        w2D = [S.dsem("w2_0"), S.dsem("w2_1")]
        pf_mark = pf.top
        assert pb3.top <= RB + 28 * KB

        def ffn_load(g):
            s_ = g % 2
            load_w(w1[s_], w_ff1_d[:, g * 512:(g + 1) * 512], w1B[s_], w1D[s_])
            load_w(w2[s_], w_ff2_d[g * 512:(g + 1) * 512, :], w2B[s_], w2D[s_])

        hl0 = pb3.alloc([128, 2, D], BF16)
        hlB = Buf("hl")
        S.op("act", lambda e: e.activation(out=hl0[0:1, 0, :], in_=bbuf[0:1, :], func=AF.Copy, scale=ALPHA),
             reads=[gbB], writes=[hlB])
        S.op("dve", lambda e: e.scalar_tensor_tensor(out=hl0[0:1, 1, :], in0=bbuf[0:1, :], scalar=ALPHA,
                                                     in1=hl0[0:1, 0, :], op0=ALU.mult, op1=ALU.subtract),
             reads=[gbB, hlB], writes=[hlB])
        S.op("act", lambda e: e.activation(out=gbuf, in_=gbuf, func=AF.Copy, scale=ALPHA), reads=[gbB], writes=[gbB])
        S.op("act", lambda e: e.activation(out=g1, in_=g1, func=AF.Copy, scale=ALPHA), reads=[g1B], writes=[g1B])
        x1D = [S.dsem(f"x1d{i}") for i in range(NT)]
        ffn_load(0)

        def s3a_stream(i):
            sa = i % 4
            X = x1[:, i, :]
            S.dma("sp", out=X, in_=x_d[i * 128:(i + 1) * 128, :], dsem=x1D[i], writes=[x1B[i]],
                  reads=([x1B[i - 3]] if i >= 3 else [woB]))
            yield
            S.op("act", lambda e: e.activation(out=X, in_=X, func=AF.Identity, bias=lnst[:, i, 1:2],
                                               scale=lnst[:, i, 0:1]), reads=[x1B[i], lnstB], writes=[x1B[i]])
            yield
            S.op("pool", lambda e: e.tensor_tensor(out=X, in0=X, in1=gbuf, op=ALU.mult),
                 reads=[x1B[i], gbB], writes=[x1B[i]])
            yield
            for hf in range(2):
                bk = sa
                hsl = slice(hf * 512, (hf + 1) * 512)
                S.op("pe", lambda e: [e.matmul(out=bank(bk), lhsT=mergedT[:, kc, i * 128:(i + 1) * 128],
                                               rhs=wout[:, kc, hsl], start=(kc == 0), stop=False)
                                      for kc in range(8)] +
                                     [e.matmul(out=bank(bk), lhsT=ones_bf[0:1, :], rhs=hl0[0:1, r_, hsl],
                                               start=False, stop=(r_ == 1)) for r_ in range(2)],
                     reads=[mgB, woB, hlB, constB], writes=[bankB[bk]])
                yield
                S.op("dve", lambda e: e.tensor_tensor(out=X[:, hsl], in0=bank(bk), in1=X[:, hsl], op=ALU.add),
                     reads=[bankB[bk], x1B[i]], writes=[x1B[i]])
                yield

        def s3b_stream(i):
            sb = i % 4
            X = x1[:, i, :]
            yield from ln_gen(X, x1B[i], X, x1B[i], None, None, None, affine=False)
            for half in range(2):
                bk = 4 + sb
                S.op("pe", lambda e: [e.transpose(out=bank(bk)[:, q * 128:(q + 1) * 128],
                                                  in_=X[:, (half * 4 + q) * 128:(half * 4 + q + 1) * 128],
                                                  identity=ident) for q in range(4)],
                     reads=[x1B[i], constB], writes=[bankB[bk]])
                yield
                for q in range(4):
                    kc = half * 4 + q
                    if True:
                        S.op("act", lambda e: e.activation(out=x1T[:, kc, i * 128:(i + 1) * 128],
                                                           in_=bank(bk)[:, q * 128:(q + 1) * 128], func=AF.Identity,
                                                           scale=lnpT[:, 16 + kc:17 + kc], bias=lnpT[:, 24 + kc:25 + kc]),
                             reads=[bankB[bk], constB], writes=[x1TB])
                    else:
                        S.op("dve", lambda e: e.tensor_scalar(out=x1T[:, kc, i * 128:(i + 1) * 128],
                                                              in0=bank(bk)[:, q * 128:(q + 1) * 128],
                                                              scalar1=lnpT[:, 16 + kc:17 + kc],
                                                              scalar2=lnpT[:, 24 + kc:25 + kc],
                                                              op0=ALU.mult, op1=ALU.add),
                             reads=[bankB[bk], constB], writes=[x1TB])
                    yield
            S.op("pool", lambda e: e.tensor_tensor(out=X, in0=X, in1=g1, op=ALU.mult),
                 reads=[x1B[i], g1B], writes=[x1B[i]])
            yield

        ga = [s3a_stream(i) for i in range(NT)]
        gb3 = [s3b_stream(i) for i in range(NT)]
        a_act, b_act = [], []
        na_, nb_, adone = 0, 0, 0
        tick = 0
        while na_ < NT or nb_ < NT or a_act or b_act:
            if na_ < NT and len(a_act) < 4 and tick % 2 == 0:
                a_act.append((na_, ga[na_]))
                na_ += 1
            if nb_ < NT and len(b_act) < 4 and nb_ < adone and tick % 2 == 1:
                b_act.append((nb_, gb3[nb_]))
                nb_ += 1
            for item in list(a_act):
                try:
                    next(item[1])
                except StopIteration:
                    a_act.remove(item)
                    adone += 1
            for item in list(b_act):
                try:
                    next(item[1])
                except StopIteration:
                    b_act.remove(item)
            tick += 1
        if dbg == "x1":
            dump([x1[:, i, :] for i in range(8)], stage_off=RB)

        S.barrier()
        pf.top = pf_mark
        rl = [pf.alloc([128, 512], F32) for _ in range(2)]
        rlB = [Buf("rl0"), Buf("rl1")]
        g2 = pf.alloc([128, D], F32)
        b2 = pf.alloc([128, D], F32)
        g2B = Buf("g2")
        g2D = S.dsem("g2")
        hl1 = pf.alloc([128, 2, D], BF16)
        hl1B = Buf("hl1")
        brow = A.at(RB + 64 * KB, [128, D], F32)
        browD = S.dsem("brow")
        S.dma("sp", out=brow[0:1, :], in_=lnp_d[3:4, :], dsem=browD, writes=[rlB[0], rlB[1]])
        S.op("act", lambda e: e.activation(out=hl1[0:1, 0, :], in_=brow[0:1, :], func=AF.Copy, scale=ALPHA),
             reads=[rlB[0], rlB[1]], writes=[hl1B])
        S.op("dve", lambda e: e.scalar_tensor_tensor(out=hl1[0:1, 1, :], in0=brow[0:1, :], scalar=ALPHA,
                                                     in1=hl1[0:1, 0, :], op0=ALU.mult, op1=ALU.subtract),
             reads=[rlB[0], rlB[1], hl1B], writes=[hl1B, rlB[0], rlB[1]])
        S.dma("sp", out=g2, in_=lnp_d[4:5, :].to_broadcast([128, D]), dsem=g2D, writes=[g2B])
        S.dma("sp", out=b2, in_=lnp_d[5:6, :].to_broadcast([128, D]), dsem=g2D, writes=[g2B])
        yo = [A.at(RD + 64 * KB, [128, D], F32), A.at(RD + 68 * KB, [128, D], F32)]
        yoB = [Buf("yo0"), Buf("yo1")]
        yoD = [S.dsem("yo0"), S.dsem("yo1")]
        NG = 8
        rcnt = {"n": 0}

        def ffn1(g):
            s = g % 2
            for fb in range(4):
                for tb in range(4):
                    bk = next_bank()
                    S.op("pe", lambda e: [e.matmul(out=bank(bk), lhsT=w1[s][:, kc, fb * 128:(fb + 1) * 128],
                                                   rhs=x1T[:, kc, tb * 512:(tb + 1) * 512],
                                                   start=(kc == 0), stop=(kc == 7)) for kc in range(8)],
                         reads=[w1B[s], x1TB], writes=[bankB[bk]])
                    r_ = rcnt["n"] % 2
                    rcnt["n"] += 1
                    S.op("act", lambda e: e.activation(out=rl[r_], in_=bank(bk), func=AF.Relu),
                         reads=[bankB[bk]], writes=[rlB[r_]])
                    S.op("dve", lambda e: e.tensor_tensor(out=hT[s][:, fb, tb * 512:(tb + 1) * 512], in0=bank(bk),
                                                          in1=rl[r_], op=ALU.mult),
                         reads=[bankB[bk], rlB[r_]], writes=[hTB[s]])

        def ffn2(g):
            s = g % 2
            for i in range(NT):
                for hf in range(2):
                    bk = next_bank()
                    extra = 2 if g == 0 else 0
                    S.op("pe", lambda e: [e.matmul(out=bank(bk), lhsT=hT[s][:, fb, i * 128:(i + 1) * 128],
                                                   rhs=w2[s][:, fb, hf * 512:(hf + 1) * 512],
                                                   start=(fb == 0), stop=(fb == 3 and extra == 0)) for fb in range(4)] +
                                         [e.matmul(out=bank(bk), lhsT=ones_bf[0:1, :],
                                                   rhs=hl1[0:1, r_, hf * 512:(hf + 1) * 512],
                                                   start=False, stop=(r_ == 1)) for r_ in range(extra)],
                         reads=[hTB[s], w2B[s], hl1B, constB], writes=[bankB[bk]])
                    xa = x1[:, i, hf * 512:(hf + 1) * 512]
                    S.op("dve", lambda e: e.tensor_tensor(out=xa, in0=bank(bk), in1=xa, op=ALU.add),
                         reads=[bankB[bk], x1B[i]], writes=[x1B[i]])
                if g == NG - 1:
                    bg.add(tail_stream(i))
                    bg.step(2)

        bg = BG()

        def tail_stream(i):
            yield from ln_gen(x1[:, i, :], x1B[i], x1[:, i, :], x1B[i], g2, b2, g2B)
            S.dma("sp", out=y_d[i * 128:(i + 1) * 128, :], in_=x1[:, i, :], dsem=yoD[i % 2], reads=[x1B[i]])
            yield

        def ffn2_last_pair():
            for i in range(NT):
                for hf in range(2):
                    bk = next_bank()
                    S.op("pe", lambda e: [e.matmul(out=bank(bk), lhsT=hT[gg % 2][:, fb, i * 128:(i + 1) * 128],
                                                   rhs=w2[gg % 2][:, fb, hf * 512:(hf + 1) * 512],
                                                   start=(gg == NG - 2 and fb == 0), stop=(gg == NG - 1 and fb == 3))
                                          for gg in (NG - 2, NG - 1) for fb in range(4)],
                         reads=[hTB[0], hTB[1], w2B[0], w2B[1]], writes=[bankB[bk]])
                    xa = x1[:, i, hf * 512:(hf + 1) * 512]
                    S.op("dve", lambda e: e.tensor_tensor(out=xa, in0=bank(bk), in1=xa, op=ALU.add),
                         reads=[bankB[bk], x1B[i]], writes=[x1B[i]])
                bg.add(tail_stream(i))
                bg.step(2)

        ffn1(0)
        for g in range(NG - 2):
            ffn_load(g + 1)
            ffn1(g + 1)
            ffn2(g)
        ffn_load(NG - 1)
        ffn1(NG - 1)
        ffn2_last_pair()
        bg.drain()
        S.barrier()
      except _StopBuild:
        pass
    return nc


_PROG = {}


def _prep_shared(inp):
    f = lambda a: np.ascontiguousarray(np.asarray(a, dtype=np.float32))
    w_in = f(inp["w_in"])[0]
    sh = {}
    sh["lnp"] = f(np.stack([inp["ln_emb_g"], inp["ln_emb_b"], inp["ln1_g"][0], inp["ln1_b"][0],
                            inp["ln2_g"][0], inp["ln2_b"][0]], axis=0))
    sh["w_na"] = f(w_in[:, 0:1536])
    hg = []
    for h in range(4):
        cs = lambda base: w_in[:, base + h * 128: base + (h + 1) * 128]
        hg.append(np.concatenate([cs(1536), cs(3072), cs(3584), cs(2048), cs(2560)], axis=1))
    sh["w_hg"] = f(np.stack(hg, axis=0))
    sh["w_mq"] = f(w_in[:, 4096:4608])
    sh["w_mkv"] = f(inp["w_mem_kv"][0])
    wbr = [inp["w_branch_na"][0], inp["w_branch_hg"][0], inp["w_branch_mem"][0]]
    mg, mb = [], []
    for j in range(8):
        mg.append(np.concatenate([w_in[:, 4608 + b * 1024 + j * 128: 4608 + b * 1024 + (j + 1) * 128]
                                  for b in range(3)], axis=1))
        mb.append(np.concatenate([np.asarray(wbr[b])[:, j * 128:(j + 1) * 128] for b in range(3)], axis=1))
    sh["w_mg"] = f(np.stack(mg, axis=0))
    sh["w_mb"] = f(np.stack(mb, axis=0))
    sh["w_out"] = f(inp["w_out"][0])
    sh["w_ff1"] = f(inp["w_ff1"][0])
    sh["w_ff2"] = f(inp["w_ff2"][0])
    sh["ident"] = np.eye(128, dtype=np.float32)
    mf, mb_ = _hg_masks()
    sh["hgm"] = f(np.stack([mf, mb_], axis=0))
    m = np.ones((1, T), np.float32)
    m[0, ::64] = 0.0
    sh["scanm"] = m
    bias, mask = _na_tables(f(inp["na_rpb"])[0])
    sh["nab"] = f(bias)
    sh["nam"] = f(mask)
    lbl = f(inp["hg_lb_logits"]).reshape(2, 2, 4, 128).transpose(3, 0, 1, 2).reshape(128, 16)
    sh["lbl"] = f(lbl)
    sh["hgn"] = f(inp["hg_norm_g"]).reshape(1, 512)
    sh["lnpT"] = f(sh["lnp"].reshape(6, 8, 128).transpose(2, 0, 1).reshape(128, 48))
    return sh


def kernel(**inputs):
    x = np.asarray(inputs["x"], dtype=np.float32)
    mem = np.asarray(inputs["mem"], dtype=np.float32)
    nb = x.shape[0]
    sh = _prep_shared(inputs)
    if "nc" not in _PROG:
        _PROG["nc"] = build_program()
    nc = _PROG["nc"]
    in_maps = []
    for b in range(nb):
        m = dict(sh)
        m["x"] = np.ascontiguousarray(x[b])
        m["mem"] = np.ascontiguousarray(mem[b])
        in_maps.append(m)
    res = run_bass_kernel_spmd(nc, in_maps, core_ids=list(range(nb)))
    out = np.stack([np.asarray(r["y"], dtype=np.float32) for r in res.results], axis=0)
    return out
```
